# Optimizing a Trainium2 kernel written in Bass

```python
import math
import jax, jax.numpy as jnp
from jax import lax
import numpy as np

D_MODEL = 1024
BATCH = 4
SEQ = 8192
DEPTH = 2

MEM_TOKENS = 256
DN_HEADS = 4
DN_HEAD_DIM = 128
DN_WIDTH = DN_HEADS * DN_HEAD_DIM
DN_CONV = 4
DN_CHUNK = 64
SWA_HEADS = 4
SWA_HEAD_DIM = 64
SWA_WIDTH = SWA_HEADS * SWA_HEAD_DIM
SWA_PATTERNS = ((128, 1), (512, 4), (2048, 16))
SWA_BLOCK = 128
SG_GROUPS = 4
SG_GROUP_DIM = 64
SG_WIDTH = SG_GROUPS * SG_GROUP_DIM
SG_CHUNK = 128
IN_SIZES = (3 * DN_WIDTH, DN_WIDTH, DN_HEADS, DN_HEADS, 3 * SWA_WIDTH, 2 * SG_WIDTH)
IN_COLS = 3 * DN_WIDTH + DN_WIDTH + 2 * DN_HEADS + 3 * SWA_WIDTH + 2 * SG_WIDTH
MIX_WIDTH = DN_WIDTH + SWA_WIDTH + SG_WIDTH
XA_HEADS = 4
XA_HEAD_DIM = D_MODEL // XA_HEADS
D_FF = 2816
NORM_EPS = 1e-6

kernel_name = 'hymba_style_delta_dilated_gmlp_hybrid'


def rms_norm(x, gain):
    xf = x.astype(jnp.float32)
    y = xf * lax.rsqrt(jnp.mean(xf * xf, axis=-1, keepdims=True) + NORM_EPS)
    return (y * gain.astype(jnp.float32)).astype(x.dtype)


def layer_norm(x, gain, bias):
    xf = x.astype(jnp.float32)
    mu = jnp.mean(xf, axis=-1, keepdims=True)
    var = jnp.mean(jnp.square(xf - mu), axis=-1, keepdims=True)
    y = (xf - mu) * lax.rsqrt(var + NORM_EPS) * gain.astype(jnp.float32) + bias.astype(jnp.float32)
    return y.astype(x.dtype)


def swiglu(x, w_gate_up, w_down):
    gate, up = jnp.split(x @ w_gate_up, 2, axis=-1)
    return (jax.nn.silu(gate) * up) @ w_down


def alibi_slopes(n):
    return jnp.asarray([2.0 ** (-8.0 * (i + 1) / n) for i in range(n)], dtype=jnp.float32)


def causal_depthwise_conv(x, w):
    c = x.shape[-1]
    return lax.conv_general_dilated(
        x, w[:, None, :], window_strides=(1,), padding=[(w.shape[0] - 1, 0)],
        dimension_numbers=('NWC', 'WIO', 'NWC'), feature_group_count=c)


def l2_normalize(x):
    return x * lax.rsqrt(jnp.sum(x * x, axis=-1, keepdims=True) + NORM_EPS)


def chunk_gated_delta_rule(q, k, v, g, beta):
    B, S, H, Dk = q.shape
    Dv = v.shape[-1]
    C = DN_CHUNK
    N = S // C

    def to_chunks(t):
        return t.reshape(B, N, C, H, -1).transpose(0, 3, 1, 2, 4)

    q, k, v = to_chunks(q), to_chunks(k), to_chunks(v)
    g = g.reshape(B, N, C, H).transpose(0, 3, 1, 2)
    beta = beta.reshape(B, N, C, H).transpose(0, 3, 1, 2)
    gc = jnp.cumsum(g, axis=-1)
    tri_incl = jnp.tril(jnp.ones((C, C), dtype=bool))
    tri_strict = jnp.tril(jnp.ones((C, C), dtype=bool), -1)
    diff = gc[..., :, None] - gc[..., None, :]
    decay = jnp.where(tri_incl, jnp.exp(jnp.where(tri_incl, diff, 0.0)), 0.0)
    kb = k * beta[..., None]
    lower = jnp.where(tri_strict, jnp.einsum('bhncd,bhnsd->bhncs', kb, k) * decay, 0.0)
    eye = jnp.eye(C, dtype=q.dtype)
    t_inv = lax.linalg.triangular_solve(eye + lower, jnp.broadcast_to(eye, lower.shape),
                                        left_side=True, lower=True, unit_diagonal=True)
    u = jnp.einsum('bhncs,bhnse->bhnce', t_inv, v * beta[..., None])
    w = jnp.einsum('bhncs,bhnsd->bhncd', t_inv, kb * jnp.exp(gc)[..., None])
    a_qk = jnp.where(tri_incl, jnp.einsum('bhncd,bhnsd->bhncs', q, k) * decay, 0.0)
    g_last = gc[..., -1]
    k_tail = k * jnp.exp(g_last[..., None] - gc)[..., None]
    q_dec = q * jnp.exp(gc)[..., None]

    xs = tuple(jnp.moveaxis(t, 2, 0) for t in (q_dec, k_tail, u, w, a_qk, g_last))

    def step(state, inp):
        qd, kt, uc, wc, aqk, gl = inp
        v_new = uc - jnp.einsum('bhcd,bhde->bhce', wc, state)
        o = jnp.einsum('bhcd,bhde->bhce', qd, state) + jnp.einsum('bhcs,bhse->bhce', aqk, v_new)
        state = state * jnp.exp(gl)[..., None, None] + jnp.einsum('bhcd,bhce->bhde', kt, v_new)
        return state, o

    state0 = jnp.zeros((B, H, Dk, Dv), dtype=q.dtype)
    _, o = lax.scan(step, state0, xs)
    return o.transpose(1, 0, 3, 2, 4).reshape(B, S, H, Dv)


def gated_deltanet(qkv, z, b_raw, a_raw, conv_w, a_log, dt_bias, out_gain):
    B, S, _ = qkv.shape
    f32 = jnp.float32
    qkv = jax.nn.silu(causal_depthwise_conv(qkv.astype(f32), conv_w.astype(f32)))
    q, k, v = jnp.split(qkv, 3, axis=-1)
    q = l2_normalize(q.reshape(B, S, DN_HEADS, DN_HEAD_DIM)) * (DN_HEAD_DIM ** -0.5)
    k = l2_normalize(k.reshape(B, S, DN_HEADS, DN_HEAD_DIM))
    v = v.reshape(B, S, DN_HEADS, DN_HEAD_DIM)
    beta = jax.nn.sigmoid(b_raw.astype(f32))
    g = -jnp.exp(a_log.astype(f32)) * jax.nn.softplus(a_raw.astype(f32) + dt_bias.astype(f32))
    o = chunk_gated_delta_rule(q, k, v, g, beta)
    o = o * lax.rsqrt(jnp.mean(o * o, axis=-1, keepdims=True) + NORM_EPS) * out_gain.astype(f32)
    o = o * jax.nn.silu(z.astype(f32).reshape(B, S, DN_HEADS, DN_HEAD_DIM))
    return o.reshape(B, S, DN_WIDTH)


def dilated_window_attention(q, k, v, slopes, window, dilation):
    B, S, H, Dh = q.shape
    f32 = jnp.float32
    span = window // dilation
    L = S // dilation
    nb = -(-L // SWA_BLOCK)
    Lp = nb * SWA_BLOCK

    def to_sub(t):
        t = t.reshape(B, L, dilation, H, Dh).transpose(0, 2, 1, 3, 4)
        t = jnp.pad(t, ((0, 0), (0, 0), (0, Lp - L), (0, 0), (0, 0)))
        return t.reshape(B, dilation, nb, SWA_BLOCK, H, Dh)

    def with_prev(t):
        prev = jnp.pad(t[:, :, :-1], ((0, 0), (0, 0), (1, 0), (0, 0), (0, 0), (0, 0)))
        return jnp.concatenate([prev, t], axis=3)

    qs = to_sub(q)
    kw = with_prev(to_sub(k))
    vw = with_prev(to_sub(v))
    qi = jnp.arange(SWA_BLOCK)[:, None]
    kj = jnp.arange(2 * SWA_BLOCK)[None, :]
    rel = SWA_BLOCK + qi - kj
    first = (jnp.arange(nb) == 0)[:, None, None] & (kj < SWA_BLOCK)[None]
    valid = ((rel >= 0) & (rel <= span))[None] & ~first
    bias = -slopes[:, None, None] * (rel * dilation).astype(f32)[None]
    s = jnp.einsum('bdnqhe,bdnkhe->bdnhqk', qs, kw).astype(f32) * (Dh ** -0.5) + bias
    s = jnp.where(valid[:, None], s, -jnp.inf)
    m = jnp.max(s, axis=-1, keepdims=True)
    p = jnp.exp(s - m)
    den = jnp.sum(p, axis=-1)
    o = jnp.einsum('bdnhqk,bdnkhe->bdnqhe', p, vw.astype(f32)) / jnp.swapaxes(den, -1, -2)[..., None]
    lse = jnp.swapaxes(m[..., 0] + jnp.log(den), -1, -2)

    def from_sub(t):
        t = t.reshape((B, dilation, Lp) + t.shape[4:])[:, :, :L]
        t = jnp.swapaxes(t, 1, 2)
        return t.reshape((B, S) + t.shape[3:])

    return from_sub(o), from_sub(lse)


def dilated_mixture_attention(qkv):
    B, S, _ = qkv.shape
    q, k, v = (t.reshape(B, S, SWA_HEADS, SWA_HEAD_DIM) for t in jnp.split(qkv, 3, axis=-1))
    slopes = alibi_slopes(SWA_HEADS)
    outs, lses = [], []
    for window, dilation in SWA_PATTERNS:
        o, lse = dilated_window_attention(q, k, v, slopes, window, dilation)
        outs.append(o)
        lses.append(lse)
    alpha = jax.nn.softmax(jnp.stack(lses), axis=0)
    o = jnp.einsum('pbsh,pbshe->bshe', alpha, jnp.stack(outs))
    return o.reshape(B, S, SWA_WIDTH)


def chunked_spatial_gating(uv, ln_gain, ln_bias, w_spatial, b_spatial):
    B, S, _ = uv.shape
    n_chunks = S // SG_CHUNK
    u, v = jnp.split(jax.nn.gelu(uv), 2, axis=-1)
    v = layer_norm(v, ln_gain, ln_bias).reshape(B, n_chunks, SG_CHUNK, SG_GROUPS, SG_GROUP_DIM)
    w_causal = jnp.tril(w_spatial)
    mixed = jnp.einsum('gts,bnsgd->bntgd', w_causal, v) + b_spatial.T[:, :, None]
    gated = u.reshape(B, n_chunks, SG_CHUNK, SG_GROUPS, SG_GROUP_DIM) * mixed
    return gated.reshape(B, S, SG_WIDTH)


def parallel_mixing(n, w_in, conv_w, a_log, dt_bias, dn_gain, sg_gain, sg_bias, w_sp, b_sp, w_out):
    proj = n @ w_in
    cuts = []
    acc = 0
    for size in IN_SIZES[:-1]:
        acc += size
        cuts.append(acc)
    dn_qkv, dn_z, dn_b, dn_a, swa_qkv, sg_uv = jnp.split(proj, cuts, axis=-1)
    out_a = gated_deltanet(dn_qkv, dn_z, dn_b, dn_a, conv_w, a_log, dt_bias, dn_gain)
    out_b = dilated_mixture_attention(swa_qkv)
    out_c = chunked_spatial_gating(sg_uv, sg_gain, sg_bias, w_sp, b_sp)
    merged = jnp.concatenate([out_a.astype(n.dtype), out_b.astype(n.dtype), out_c.astype(n.dtype)], axis=-1)
    return merged @ w_out


def memory_cross_attention(h, mem, w_q, w_kv, w_o):
    B, S, _ = h.shape
    M = mem.shape[1]
    q = (h @ w_q).reshape(B, S, XA_HEADS, XA_HEAD_DIM)
    k, v = jnp.split(mem @ w_kv, 2, axis=-1)
    k = k.reshape(B, M, XA_HEADS, XA_HEAD_DIM)
    v = v.reshape(B, M, XA_HEADS, XA_HEAD_DIM)
    s = jnp.einsum('bshe,bmhe->bhsm', q, k).astype(jnp.float32) * (XA_HEAD_DIM ** -0.5)
    p = jax.nn.softmax(s, axis=-1)
    o = jnp.einsum('bhsm,bmhe->bshe', p.astype(v.dtype), v).reshape(B, S, D_MODEL)
    return o @ w_o


def setup_inputs(seed: int = 0) -> dict:
    key = jax.random.key(seed)
    ks = jax.random.split(key, 32)
    f32 = jnp.float32

    def nrm(k, shape, scale):
        return jax.random.normal(k, shape, f32) * scale

    def gain(k, shape):
        return 1.0 + 0.02 * jax.random.normal(k, shape, f32)

    dt = jnp.exp(jax.random.uniform(ks[9], (DEPTH, DN_HEADS), f32, math.log(1e-3), math.log(1e-1)))
    return {
        'x': nrm(ks[0], (BATCH, SEQ, D_MODEL), 1.0),
        'mem': nrm(ks[1], (BATCH, MEM_TOKENS, D_MODEL), 1.0),
        'ffn1_norm': gain(ks[2], (DEPTH, D_MODEL)),
        'ffn1_w_gate_up': nrm(ks[3], (DEPTH, D_MODEL, 2 * D_FF), D_MODEL ** -0.5),
        'ffn1_w_down': nrm(ks[4], (DEPTH, D_FF, D_MODEL), D_FF ** -0.5),
        'mix_norm': gain(ks[5], (DEPTH, D_MODEL)),
        'mix_w_in': nrm(ks[6], (DEPTH, D_MODEL, IN_COLS), D_MODEL ** -0.5),
        'dn_conv_w': nrm(ks[7], (DEPTH, DN_CONV, 3 * DN_WIDTH), DN_CONV ** -0.5),
        'dn_a_log': jnp.log(jax.random.uniform(ks[8], (DEPTH, DN_HEADS), f32, 1.0, 16.0)),
        'dn_dt_bias': dt + jnp.log(-jnp.expm1(-dt)),
        'dn_out_norm': gain(ks[10], (DEPTH, DN_HEAD_DIM)),
        'sg_norm_gain': gain(ks[11], (DEPTH, SG_WIDTH)),
        'sg_norm_bias': nrm(ks[12], (DEPTH, SG_WIDTH), 0.02),
        'sg_w_spatial': nrm(ks[13], (DEPTH, SG_GROUPS, SG_CHUNK, SG_CHUNK), SG_CHUNK ** -0.5),
        'sg_b_spatial': gain(ks[14], (DEPTH, SG_GROUPS, SG_CHUNK)),
        'mix_w_out': nrm(ks[15], (DEPTH, MIX_WIDTH, D_MODEL), MIX_WIDTH ** -0.5),
        'xa_norm': gain(ks[16], (DEPTH, D_MODEL)),
        'xa_mem_norm': gain(ks[17], (DEPTH, D_MODEL)),
        'xa_w_q': nrm(ks[18], (DEPTH, D_MODEL, D_MODEL), D_MODEL ** -0.5),
        'xa_w_kv': nrm(ks[19], (DEPTH, D_MODEL, 2 * D_MODEL), D_MODEL ** -0.5),
        'xa_w_o': nrm(ks[20], (DEPTH, D_MODEL, D_MODEL), D_MODEL ** -0.5),
        'ffn2_norm': gain(ks[21], (DEPTH, D_MODEL)),
        'ffn2_w_gate_up': nrm(ks[22], (DEPTH, D_MODEL, 2 * D_FF), D_MODEL ** -0.5),
        'ffn2_w_down': nrm(ks[23], (DEPTH, D_FF, D_MODEL), D_FF ** -0.5),
        'final_norm': gain(ks[24], (D_MODEL,)),
    }


def reference(x, mem, ffn1_norm, ffn1_w_gate_up, ffn1_w_down, mix_norm, mix_w_in, dn_conv_w,
              dn_a_log, dn_dt_bias, dn_out_norm, sg_norm_gain, sg_norm_bias, sg_w_spatial,
              sg_b_spatial, mix_w_out, xa_norm, xa_mem_norm, xa_w_q, xa_w_kv, xa_w_o,
              ffn2_norm, ffn2_w_gate_up, ffn2_w_down, final_norm):
    h = x
    for i in range(DEPTH):
        h = h + 0.5 * swiglu(rms_norm(h, ffn1_norm[i]), ffn1_w_gate_up[i], ffn1_w_down[i])
        h = h + parallel_mixing(rms_norm(h, mix_norm[i]), mix_w_in[i], dn_conv_w[i], dn_a_log[i],
                                dn_dt_bias[i], dn_out_norm[i], sg_norm_gain[i], sg_norm_bias[i],
                                sg_w_spatial[i], sg_b_spatial[i], mix_w_out[i])
        h = h + memory_cross_attention(rms_norm(h, xa_norm[i]), rms_norm(mem, xa_mem_norm[i]),
                                       xa_w_q[i], xa_w_kv[i], xa_w_o[i])
        h = h + 0.5 * swiglu(rms_norm(h, ffn2_norm[i]), ffn2_w_gate_up[i], ffn2_w_down[i])
    return rms_norm(h, final_norm)
```

```python
import numpy as np
import ml_dtypes
from contextlib import ExitStack, contextmanager
import concourse.bass as bass
import concourse.mybir as mybir
from concourse.bass_utils import run_bass_kernel_spmd

F32 = mybir.dt.float32
BF16 = mybir.dt.bfloat16
AF = mybir.ActivationFunctionType
ALU = mybir.AluOpType
AX = mybir.AxisListType

D = 1024
DFF = 2816
KC = 8
FC = 22
IN_COLS = 3336
EPS = 1e-6
NEG = -1.0e30
N_CORES = 4
SEQ = 8192
DEPTH = 2
MEMT = 256
VW = 72

O_G1, O_GM, O_GX, O_G2, O_GMEM = 0, 8, 16, 24, 32
O_CONV = 40
O_ALOG = 88
O_DTB = 92
O_BSP = 96
NSP = 100
O_DNG = 100
O_SGG = 612
O_SGB = 868
NS = 1124
C_ID, C_ONE, C_U, C_MB, C_ST, C_SWA = 0, 128, 256, 384, 512, 640
NCST = 640 + 24 * 128


class Buf:
    __slots__ = ("name", "w", "r", "psum")

    def __init__(self, name="", psum=False):
        self.name = name
        self.w = None
        self.r = {}
        self.psum = psum


class Q:
    def __init__(self, name):
        self.name = name
        self.prog = []
        self.sem = None
        self.count = 0
        self.waited = {}
        self.dsems = []
        self.dnext = 0


class _Rec:
    def __init__(self):
        self.call = None

    def __getattr__(self, name):
        def f(*a, **kw):
            self.call = (name, a, kw)
            return self
        return f


class K:
    def __init__(self, nc, n_dma_sems=(24, 12, 4)):
        self.nc = nc
        self.stacks = [ExitStack()]
        self.q = {n: Q(n) for n in ("pe", "act", "dve", "pool", "sp")}
        for n in ("pe", "act", "dve", "pool"):
            self.q[n].sem = self.stacks[0].enter_context(nc.semaphore("s_" + n))
        for qn, cnt in zip(("sp", "pool", "act"), n_dma_sems):
            for i in range(cnt):
                s = self.stacks[0].enter_context(nc.semaphore(f"d_{qn}_{i}"))
                self.q[qn].dsems.append([s, 0])
        self.nbuf = 0
        self.uid = 0

    def sb(self, name, shape, dt):
        self.uid += 1
        return self.stacks[-1].enter_context(self.nc.sbuf_tensor(f"{name}_{self.uid}", list(shape), dt))

    def ps(self, name, shape, dt=F32):
        self.uid += 1
        return self.stacks[-1].enter_context(self.nc.psum_tensor(f"{name}_{self.uid}", list(shape), dt))

    def buf(self, name=""):
        self.nbuf += 1
        return Buf(name or f"b{self.nbuf}")

    def bufs(self, n):
        return [self.buf() for _ in range(n)]

    def pbuf(self):
        self.nbuf += 1
        return Buf(f"p{self.nbuf}", psum=True)

    def pbufs(self, n):
        return [self.pbuf() for _ in range(n)]

    @contextmanager
    def scope(self):
        st = ExitStack()
        self.stacks.append(st)
        try:
            yield
        finally:
            self.barrier()
            self.stacks.pop()
            st.close()

    def barrier(self):
        deps = {}
        for n in ("pe", "act", "dve", "pool"):
            q = self.q[n]
            if q.count > 0:
                deps[id(q.sem)] = (q.sem, q.count)
        for n in ("sp", "pool", "act"):
            for s, v in self.q[n].dsems:
                if v > 0:
                    deps[id(s)] = (s, v)
        for n in ("pe", "act", "dve", "pool", "sp"):
            self._emit_waits(self.q[n], deps)

    def _deps(self, reads, writes, own=None):
        deps = {}

        def add(k, s, v):
            if k not in deps or deps[k][1] < v:
                deps[k] = (s, v)

        for b in reads:
            if b.w is not None:
                add(*b.w)
            if b.psum:
                for k, (s, v) in b.r.items():
                    if k != own:
                        add(k, s, v)
        for b in writes:
            if b.w is not None:
                add(*b.w)
            for k, (s, v) in b.r.items():
                add(k, s, v)
        return deps

    def _emit_waits(self, q, deps, skip_self=False):
        for k, (s, v) in deps.items():
            if skip_self and q.sem is not None and k == id(q.sem):
                continue
            if q.waited.get(k, 0) >= v:
                continue
            q.waited[k] = v
            q.prog.append(lambda e, s=s, v=v: e.wait_ge(s, v))

    def _mark(self, ev, reads, writes):
        k, s, v = ev
        for b in reads:
            b.r[k] = (s, v)
        for b in writes:
            b.w = ev
            b.r = {}

    def op(self, qn, fn, reads=(), writes=()):
        q = self.q[qn]
        deps = self._deps(reads, writes, own=id(q.sem))
        self._emit_waits(q, deps, skip_self=(qn == "pe"))
        q.count += 1
        c = q.count
        rec = _Rec()
        fn(rec)
        name, a, kw = rec.call
        q.prog.append(lambda e, name=name, a=a, kw=kw, s=q.sem: getattr(e, name)(*a, **kw).then_inc(s, 1))
        self._mark((id(q.sem), q.sem, c), reads, writes)

    def dma(self, qn, out, in_, reads=(), writes=(), **kw):
        q = self.q[qn]
        deps = self._deps(reads, writes)
        self._emit_waits(q, deps)
        slot = q.dsems[q.dnext]
        q.dnext = (q.dnext + 1) % len(q.dsems)
        s, v = slot
        if v > 0 and q.waited.get(id(s), 0) < v:
            q.waited[id(s)] = v
            q.prog.append(lambda e, s=s, v=v: e.wait_ge(s, v))
        slot[1] = v + 16
        q.prog.append(lambda e, s=s, out=out, in_=in_, kw=kw:
                      e.dma_start(out=out, in_=in_, **kw).then_inc(s, 16))
        self._mark((id(s), s, v + 16), reads, writes)

    def finish(self):
        self.barrier()
        nc = self.nc
        with nc.Block() as block:
            @block.tensor
            def _(e):
                for f in self.q["pe"].prog:
                    f(e)

            @block.scalar
            def _(e):
                for f in self.q["act"].prog:
                    f(e)

            @block.vector
            def _(e):
                for f in self.q["dve"].prog:
                    f(e)

            @block.gpsimd
            def _(e):
                for f in self.q["pool"].prog:
                    f(e)

            @block.sync
            def _(e):
                for f in self.q["sp"].prog:
                    f(e)
        while self.stacks:
            self.stacks.pop().close()


def build(S=SEQ, n_layers=DEPTH, stop_after=None, dbg=False):
    nc = bass.Bass("TRN2", target_bir_lowering=False)
    NT = S // 128
    NG = S // 512

    def din(name, shape, dt=F32):
        return nc.dram_tensor(name, list(shape), dt, kind="ExternalInput").ap()

    def dscr(name, shape, dt=F32):
        return nc.dram_tensor(name, list(shape), dt, kind=("ExternalOutput" if dbg else "Internal")).ap()

    x_in = din("x", [S, D])
    mem_in = din("mem", [MEMT, D])
    w_gu = [din("ffn1_w_gate_up", [DEPTH, D, 2 * DFF]), din("ffn2_w_gate_up", [DEPTH, D, 2 * DFF])]
    w_dn = [din("ffn1_w_down", [DEPTH, DFF, D]), din("ffn2_w_down", [DEPTH, DFF, D])]
    w_in = din("mix_w_in", [DEPTH, D, IN_COLS])
    w_out = din("mix_w_out", [DEPTH, D, D])
    w_q = din("xa_w_q", [DEPTH, D, D])
    w_kv = din("xa_w_kv", [DEPTH, D, 2 * D])
    w_o = din("xa_w_o", [DEPTH, D, D])
    w_sp = din("sg_w_spatial", [DEPTH, 4, 128, 128])
    small_in = din("small", [DEPTH, 128, NS])
    fin_in = din("final_b", [128, D])
    cst_in = din("cst", [128, NCST])
    out = nc.dram_tensor("out", [S, D], F32, kind="ExternalOutput").ap()

    hbuf = dscr("hbuf", [S, D])
    dnT = dscr("dnT", [1536, S])
    swaQT = dscr("swaQT", [256, S], BF16)
    swaKT = dscr("swaKT", [256, S], BF16)
    swaV = dscr("swaV", [S, 4 * VW], BF16)
    zbuf = dscr("zbuf", [S, 512])
    babuf = dscr("babuf", [S, 8])
    sgbuf = dscr("sgbuf", [S, 512])
    ndbuf = [dscr(f"nd{p}", [S, 4 * VW]) for p in range(3)]
    merged = dscr("merged", [S, D], BF16)

    k = K(nc)

    cst = k.sb("cst", [128, C_SWA], F32)
    small = [k.sb(f"small{l}", [128, NSP], F32) for l in range(DEPTH)]
    identb = k.sb("identb", [128, 128], BF16)
    onesb = k.sb("onesb", [128, 128], BF16)
    b_cst = k.buf("cst")
    b_small = k.bufs(DEPTH)
    b_identb = k.buf("identb")
    k.dma("sp", cst[:], cst_in[:, 0:C_SWA], writes=[b_cst])
    for l in range(DEPTH):
        k.dma("sp", small[l][:], small_in[l, :, 0:NSP], writes=[b_small[l]])
    k.op("dve", lambda e: e.tensor_copy(out=identb[:], in_=cst[:, C_ID:C_ID + 128]), reads=[b_cst], writes=[b_identb])
    k.op("dve", lambda e: e.tensor_copy(out=onesb[:], in_=cst[:, C_ONE:C_ONE + 128]), reads=[b_cst], writes=[b_identb])
    ident = cst[:, C_ID:C_ID + 128]
    ones = cst[:, C_ONE:C_ONE + 128]
    Umat = cst[:, C_U:C_U + 128]
    maskbT = cst[:, C_MB:C_MB + 128]
    strictT = cst[:, C_ST:C_ST + 128]

    rr = {"i": 0}

    def load_w(dst, dbufs, src2d, kcn, ncols, gain, gbuf, stage, sbufs):
        srcv = src2d.rearrange("(kc p) n -> p kc n", p=128)
        CH = stage[0].shape[1]
        for kc in range(kcn):
            for c0 in range(0, ncols, CH):
                w = min(CH, ncols - c0)
                i = rr["i"]
                rr["i"] += 1
                st, sbf = stage[i % len(stage)], sbufs[i % len(stage)]
                k.dma("sp", st[:, :w], srcv[:, kc, c0:c0 + w], writes=[sbf])
                o_ap = dst[:, kc, c0:c0 + w]
                i_ap = st[:, :w]
                rds = [sbf] + ([gbuf] if gain is not None else [])
                eng = ("act", "dve")[i % 2]
                if gain is not None:
                    g_ap = gain[:, kc:kc + 1]
                    if eng == "act":
                        k.op("act", lambda e, o_ap=o_ap, i_ap=i_ap, g_ap=g_ap:
                             e.activation(out=o_ap, in_=i_ap, func=AF.Copy, scale=g_ap),
                             reads=rds, writes=[dbufs[kc]])
                    else:
                        k.op(eng, lambda e, o_ap=o_ap, i_ap=i_ap, g_ap=g_ap:
                             e.tensor_scalar(out=o_ap, in0=i_ap, scalar1=g_ap, scalar2=None, op0=ALU.mult),
                             reads=rds, writes=[dbufs[kc]])
                else:
                    if eng == "act":
                        k.op("act", lambda e, o_ap=o_ap, i_ap=i_ap: e.activation(out=o_ap, in_=i_ap, func=AF.Copy),
                             reads=rds, writes=[dbufs[kc]])
                    else:
                        k.op(eng, lambda e, o_ap=o_ap, i_ap=i_ap: e.tensor_copy(out=o_ap, in_=i_ap),
                             reads=rds, writes=[dbufs[kc]])

    class NormT:
        def __init__(self):
            self.xs = [k.sb("xs", [128, D], BF16) for _ in range(2)]
            self.st = [k.sb("nst", [128, 4], F32) for _ in range(2)]
            self.tp = [k.ps("tp", [128, D], BF16) for _ in range(2)]
            self.b_xs = k.bufs(2)
            self.b_st = k.bufs(2)
            self.b_tp = k.pbufs(2)
            self.i = 0

        def run(self, hb, b_hb, dst3, b_dst, evac_eng=None):
            i = self.i % 2
            self.i += 1
            st, xs, tp = self.st[i], self.xs[i], self.tp[i]
            bst, bxs, btp = self.b_st[i], self.b_xs[i], self.b_tp[i]
            k.op("act", lambda e: e.activation(out=xs[:], in_=hb, func=AF.Square, accum_out=st[:, 0:1]),
                 reads=[b_hb], writes=[bxs, bst])
            k.op("act", lambda e: e.activation(out=st[:, 1:2], in_=st[:, 0:1], func=AF.Sqrt, bias=epsb[:, 0:1], scale=1.0 / D),
                 reads=[bst, b_eps], writes=[bst])
            k.op("dve", lambda e: e.reciprocal(out=st[:, 2:3], in_=st[:, 1:2]), reads=[bst], writes=[bst])
            k.op("dve", lambda e: e.tensor_scalar(out=xs[:], in0=hb, scalar1=st[:, 2:3], scalar2=None, op0=ALU.mult),
                 reads=[b_hb, bst], writes=[bxs])
            for kc in range(KC):
                k.op("pe", lambda e, kc=kc: e.transpose(out=tp[:, kc * 128:(kc + 1) * 128], in_=xs[:, kc * 128:(kc + 1) * 128],
                                                        identity=identb[:]),
                     reads=[bxs, b_identb], writes=[btp])
            eng = evac_eng or ("act" if self.i % 2 else "dve")
            src = tp[:].rearrange("p (a b) -> p a b", a=KC)
            if eng == "act":
                k.op("act", lambda e: e.activation(out=dst3, in_=src, func=AF.Copy), reads=[btp], writes=[b_dst])
            else:
                k.op("dve", lambda e: e.tensor_copy(out=dst3, in_=src), reads=[btp], writes=[b_dst])

    epsb = k.sb("epsb", [128, 1], F32)
    b_eps = k.buf()
    k.op("dve", lambda e: e.memset(epsb[:], EPS), writes=[b_eps])

    def phase_copy_x():
        with k.scope():
            t = [k.sb("cx", [128, D], F32) for _ in range(4)]
            bt = k.bufs(4)
            for i in range(NT):
                k.dma("sp", t[i % 4][:], x_in[i * 128:(i + 1) * 128, :], writes=[bt[i % 4]])
                k.dma("sp", hbuf[i * 128:(i + 1) * 128, :], t[i % 4][:], reads=[bt[i % 4]])

    def phase_ffn(l, which):
        gain_off = O_G1 if which == 0 else O_G2
        with k.scope():
            wgu = k.sb("wgu", [128, KC, 2 * DFF], BF16)
            wd = k.sb("wd", [128, FC, D], BF16)
            b_wgu = k.bufs(KC)
            b_wd = k.bufs(FC)
            stage = [k.sb("stg", [128, 704], F32) for _ in range(2)]
            b_stage = k.bufs(2)
            load_w(wgu, b_wgu, w_gu[which][l], KC, 2 * DFF, small[l][:, gain_off:gain_off + 8], b_small[l], stage, b_stage)
            load_w(wd, b_wd, w_dn[which][l], FC, D, None, None, stage, b_stage)
            nrm = NormT()
            hb = [k.sb("hb", [128, D], F32) for _ in range(4)]
            b_hb = k.bufs(4)
            nT = [k.sb("nT", [128, KC, 512], BF16) for _ in range(2)]
            b_nT = [k.bufs(4) for _ in range(2)]
            actT = k.sb("actT", [128, FC, 512], BF16)
            b_act = k.bufs(FC)
            sg = [k.sb("sg", [128, 512], F32) for _ in range(2)]
            b_sg = k.bufs(2)
            pg = [k.ps("pg", [128, 512]) for _ in range(2)]
            pu = [k.ps("pu", [128, 512]) for _ in range(2)]
            po = [k.ps("po", [128, 512]) for _ in range(2)]
            b_pg, b_pu, b_po = k.pbufs(2), k.pbufs(2), k.pbufs(2)
            for g in range(NG):
                gi = g % 2
                for t in range(4):
                    tok0 = g * 512 + t * 128
                    hbt, bhb = hb[t], b_hb[t]
                    k.dma("sp", hbt[:], hbuf[tok0:tok0 + 128, :], writes=[bhb])
                    nrm.run(hbt[:], bhb, nT[gi][:, :, t * 128:(t + 1) * 128], b_nT[gi][t])
                for fc in range(FC):
                    j = fc % 2
                    for kc in range(KC):
                        k.op("pe", lambda e, kc=kc, fc=fc, j=j: e.matmul(
                            pg[j][:], lhsT=wgu[:, kc, fc * 128:(fc + 1) * 128], rhs=nT[gi][:, kc, :],
                            start=(kc == 0), stop=(kc == KC - 1)),
                            reads=[b_wgu[kc]] + b_nT[gi], writes=[b_pg[j]])
                    for kc in range(KC):
                        k.op("pe", lambda e, kc=kc, fc=fc, j=j: e.matmul(
                            pu[j][:], lhsT=wgu[:, kc, DFF + fc * 128:DFF + (fc + 1) * 128], rhs=nT[gi][:, kc, :],
                            start=(kc == 0), stop=(kc == KC - 1)),
                            reads=[b_wgu[kc]] + b_nT[gi], writes=[b_pu[j]])
                    k.op("act", lambda e, j=j: e.activation(out=sg[j][:], in_=pg[j][:], func=AF.Silu),
                         reads=[b_pg[j]], writes=[b_sg[j]])
                    k.op("dve", lambda e, j=j, fc=fc: e.tensor_tensor(out=actT[:, fc, :], in0=pu[j][:], in1=sg[j][:], op=ALU.mult),
                         reads=[b_pu[j], b_sg[j]], writes=[b_act[fc]])
                for t in range(4):
                    tok0 = g * 512 + t * 128
                    hbt, bhb = hb[t], b_hb[t]
                    for nh in range(2):
                        j = (t * 2 + nh) % 2
                        for fc in range(FC):
                            k.op("pe", lambda e, fc=fc, t=t, nh=nh, j=j: e.matmul(
                                po[j][:], lhsT=actT[:, fc, t * 128:(t + 1) * 128], rhs=wd[:, fc, nh * 512:(nh + 1) * 512],
                                start=(fc == 0), stop=(fc == FC - 1)),
                                reads=[b_act[fc], b_wd[fc]], writes=[b_po[j]])
                        k.op("dve", lambda e, nh=nh, j=j, hbt=hbt: e.scalar_tensor_tensor(
                            out=hbt[:, nh * 512:(nh + 1) * 512], in0=po[j][:], scalar=0.5,
                            in1=hbt[:, nh * 512:(nh + 1) * 512], op0=ALU.mult, op1=ALU.add),
                            reads=[b_po[j], bhb], writes=[bhb])
                    k.dma("sp", hbuf[tok0:tok0 + 128, :], hbt[:], reads=[bhb])

    def phase_mixproj(l):
        with k.scope():
            win = k.sb("win", [128, KC, IN_COLS], BF16)
            b_win = k.bufs(KC)
            stage = [k.sb("stg", [128, 1112], F32) for _ in range(2)]
            b_stage = k.bufs(2)
            load_w(win, b_win, w_in[l], KC, IN_COLS, small[l][:, O_GM:O_GM + 8], b_small[l], stage, b_stage)
            nrm = NormT()
            hb = [k.sb("hb", [128, D], F32) for _ in range(4)]
            b_hb = k.bufs(4)
            nT = [k.sb("nT", [128, KC, 512], BF16) for _ in range(2)]
            b_nT = [k.bufs(4) for _ in range(2)]
            pf = [k.ps("pf", [128, 512]) for _ in range(2)]
            b_pf = k.pbufs(2)
            pt = [k.ps("pt", [128, 512]) for _ in range(4)]
            b_pt = k.pbufs(4)
            fo = [k.sb("fo", [128, 512], F32) for _ in range(2)]
            b_fo = k.bufs(2)
            fob = [k.sb("fob", [128, 512], BF16) for _ in range(2)]
            b_fob = k.bufs(2)
            zt = [k.sb("zt", [128, 512], F32) for _ in range(2)]
            b_zt = k.bufs(2)
            bat = [k.sb("bat", [128, 8], F32) for _ in range(2)]
            b_bat = k.bufs(2)
            vt = [k.sb("vt", [128, 4, VW], BF16) for _ in range(2)]
            b_vt = k.bufs(2)
            sgt = [k.sb("sgt", [128, 512], F32) for _ in range(2)]
            b_sgt = k.bufs(2)
            for j in range(2):
                k.op("dve", lambda e, j=j: e.memset(vt[j][:], 1.0), writes=[b_vt[j]])
            fch = [(c * 128, "dn", c) for c in range(12)] + [(2056 + j * 128, "q", j) for j in range(2)] + \
                  [(2312 + j * 128, "k", j) for j in range(2)]
            cnt = 0
            for g in range(NG):
                gi = g % 2
                for t in range(4):
                    tok0 = g * 512 + t * 128
                    k.dma("sp", hb[t][:], hbuf[tok0:tok0 + 128, :], writes=[b_hb[t]])
                    nrm.run(hb[t][:], b_hb[t], nT[gi][:, :, t * 128:(t + 1) * 128], b_nT[gi][t])
                for ci, (col0, kind, idx) in enumerate(fch):
                    if DBG.get("nofeat"):
                        break
                    j = ci % 2
                    for kc in range(KC):
                        k.op("pe", lambda e, kc=kc, col0=col0, j=j: e.matmul(
                            pf[j][:], lhsT=win[:, kc, col0:col0 + 128], rhs=nT[gi][:, kc, :],
                            start=(kc == 0), stop=(kc == KC - 1)), reads=[b_win[kc]] + b_nT[gi], writes=[b_pf[j]])
                    if kind == "dn":
                        if ci % 4 < 2:
                            k.op("act", lambda e, j=j: e.activation(out=fo[j][:], in_=pf[j][:], func=AF.Copy),
                                 reads=[b_pf[j]], writes=[b_fo[j]])
                        else:
                            k.op("dve", lambda e, j=j: e.tensor_copy(out=fo[j][:], in_=pf[j][:]),
                                 reads=[b_pf[j]], writes=[b_fo[j]])
                        k.dma("sp", dnT[idx * 128:(idx + 1) * 128, g * 512:(g + 1) * 512], fo[j][:], reads=[b_fo[j]])
                    else:
                        k.op("dve", lambda e, j=j: e.tensor_copy(out=fob[j][:], in_=pf[j][:]),
                             reads=[b_pf[j]], writes=[b_fob[j]])
                        dst = swaQT if kind == "q" else swaKT
                        k.dma("sp", dst[idx * 128:(idx + 1) * 128, g * 512:(g + 1) * 512], fob[j][:], reads=[b_fob[j]])
                for t in range(4):
                    if DBG.get("notok"):
                        break
                    tok0 = g * 512 + t * 128
                    j = cnt % 2
                    cnt += 1
                    groups = [(1536, 512), (2048, 8), (2568, 512), (3080, 256)]
                    for gi2, (c0, w) in enumerate(groups):
                        if DBG.get("noba") and gi2 == 1:
                            continue
                        for kc in range(KC):
                            k.op("pe", lambda e, kc=kc, c0=c0, w=w, gi2=gi2, t=t: e.matmul(
                                pt[gi2][:, 0:w], lhsT=nT[gi][:, kc, t * 128:(t + 1) * 128], rhs=win[:, kc, c0:c0 + w],
                                start=(kc == 0), stop=(kc == KC - 1)), reads=[b_win[kc], b_nT[gi][t]], writes=[b_pt[gi2]])
                    k.op("act", lambda e, j=j: e.activation(out=zt[j][:], in_=pt[0][:], func=AF.Copy),
                         reads=[b_pt[0]], writes=[b_zt[j]])
                    k.dma("sp", zbuf[tok0:tok0 + 128, :], zt[j][:], reads=[b_zt[j]])
                    if not DBG.get("noba"):
                        k.op("dve", lambda e, j=j: e.tensor_copy(out=bat[j][:], in_=pt[1][:, 0:8]),
                             reads=[b_pt[1]], writes=[b_bat[j]])
                        k.dma("sp", babuf[tok0:tok0 + 128, :], bat[j][:], reads=[b_bat[j]])
                    if not DBG.get("novt"):
                        k.op("dve", lambda e, j=j: e.tensor_copy(out=vt[j][:, :, 0:64],
                                                                 in_=pt[2][:, 0:256].rearrange("p (h d) -> p h d", h=4)),
                             reads=[b_pt[2]], writes=[b_vt[j]])
                        k.dma("sp", swaV[tok0:tok0 + 128, :], vt[j][:].rearrange("p h d -> p (h d)"), reads=[b_vt[j]])
                    k.op("act", lambda e, j=j: e.activation(out=sgt[j][:, 0:256], in_=pt[2][:, 256:512], func=AF.Copy),
                         reads=[b_pt[2]], writes=[b_sgt[j]])
                    k.op("act", lambda e, j=j: e.activation(out=sgt[j][:, 256:512], in_=pt[3][:, 0:256], func=AF.Copy),
                         reads=[b_pt[3]], writes=[b_sgt[j]])
                    k.dma("sp", sgbuf[tok0:tok0 + 128, :], sgt[j][:], reads=[b_sgt[j]])

    def phase_dn(l):
        sm = small[l]
        bsm = b_small[l]
        with k.scope():
            ba = k.sb("ba", [128, NT, 8], F32)
            b_ba = k.buf()
            bav = babuf.rearrange("(t p) c -> p t c", p=128)
            for t0 in range(0, NT, 8):
                t1 = min(NT, t0 + 8)
                k.dma("sp", ba[:, t0:t1, :], bav[:, t0:t1, :], writes=[b_ba])
            dng = k.sb("dng", [128, 512], F32)
            b_dng = k.buf()
            k.dma("sp", dng[:], small_in[l, :, O_DNG:O_DNG + 512], writes=[b_dng])
            G = k.sb("G", [128, NT, 4], F32)
            BETA = k.sb("BETA", [128, NT, 4], F32)
            NBETA = k.sb("NBETA", [128, NT, 4], F32)
            GC = k.sb("GC", [128, NT, 4], F32)
            EGC = k.sb("EGC", [128, NT, 4], F32)
            ETAIL = k.sb("ETAIL", [128, NT, 4], F32)
            EGL = k.sb("EGL", [128, NT, 4], F32)
            tA = k.sb("tA", [128, NT], F32)
            tB = k.sb("tB", [128, NT], F32)
            tC = k.sb("tC", [128, NT], F32)
            na = k.sb("na", [128, 4], F32)
            b_g = k.buf()
            b_tmp = k.buf()
            b_na = k.buf()
            k.op("act", lambda e: e.activation(out=na[:], in_=sm[:, O_ALOG:O_ALOG + 4], func=AF.Exp), reads=[bsm], writes=[b_na])
            k.op("dve", lambda e: e.tensor_scalar(out=na[:], in0=na[:], scalar1=-1.0, scalar2=None, op0=ALU.mult),
                 reads=[b_na], writes=[b_na])
            for h in range(4):
                k.op("dve", lambda e, h=h: e.tensor_scalar(out=tA[:], in0=ba[:, :, 4 + h], scalar1=sm[:, O_DTB + h:O_DTB + h + 1],
                                                           scalar2=None, op0=ALU.add), reads=[b_ba, bsm], writes=[b_tmp])
                k.op("dve", lambda e: e.scalar_tensor_tensor(out=tB[:], in0=tA[:], scalar=-1.0, in1=tA[:], op0=ALU.mult, op1=ALU.min),
                     reads=[b_tmp], writes=[b_tmp])
                k.op("act", lambda e: e.activation(out=tB[:], in_=tB[:], func=AF.Exp), reads=[b_tmp], writes=[b_tmp])
                k.op("act", lambda e: e.activation(out=tB[:], in_=tB[:], func=AF.Ln, bias=ones[:, 0:1], scale=1.0),
                     reads=[b_tmp, b_cst], writes=[b_tmp])
                k.op("dve", lambda e: e.scalar_tensor_tensor(out=tC[:], in0=tA[:], scalar=0.0, in1=tB[:], op0=ALU.max, op1=ALU.add),
                     reads=[b_tmp], writes=[b_tmp])
                k.op("dve", lambda e, h=h: e.tensor_scalar(out=G[:, :, h], in0=tC[:], scalar1=na[:, h:h + 1], scalar2=None, op0=ALU.mult),
                     reads=[b_tmp, b_na], writes=[b_g])
                k.op("act", lambda e, h=h: e.activation(out=tA[:], in_=ba[:, :, h], func=AF.Exp, scale=-1.0),
                     reads=[b_ba], writes=[b_tmp])
                k.op("dve", lambda e: e.tensor_scalar(out=tA[:], in0=tA[:], scalar1=1.0, scalar2=None, op0=ALU.add),
                     reads=[b_tmp], writes=[b_tmp])
                k.op("dve", lambda e, h=h: e.reciprocal(out=BETA[:, :, h], in_=tA[:]), reads=[b_tmp], writes=[b_g])
                k.op("dve", lambda e, h=h: e.tensor_scalar(out=NBETA[:, :, h], in0=BETA[:, :, h], scalar1=-1.0, scalar2=None, op0=ALU.mult),
                     reads=[b_g], writes=[b_g])
            PS = [k.ps("pss", [128, 512]) for _ in range(2)]
            b_PS = k.pbufs(2)
            pgc, pgl = PS
            b_pgc, b_pgl = b_PS
            Gf = G[:].rearrange("p t h -> p (t h)")
            k.op("pe", lambda e: e.matmul(pgc[:, 0:NT * 4], lhsT=Umat, rhs=Gf, start=True, stop=True), reads=[b_cst, b_g], writes=[b_pgc])
            k.op("pe", lambda e: e.matmul(pgl[:, 0:NT * 4], lhsT=ones, rhs=Gf, start=True, stop=True), reads=[b_cst, b_g], writes=[b_pgl])
            fl = lambda T: T[:].rearrange("p t h -> p (t h)")
            k.op("dve", lambda e: e.tensor_copy(out=fl(GC), in_=pgc[:, 0:NT * 4]), reads=[b_pgc], writes=[b_g])
            k.op("act", lambda e: e.activation(out=fl(EGC), in_=pgc[:, 0:NT * 4], func=AF.Exp), reads=[b_pgc], writes=[b_g])
            k.op("act", lambda e: e.activation(out=fl(EGL), in_=pgl[:, 0:NT * 4], func=AF.Exp), reads=[b_pgl], writes=[b_g])
            k.op("dve", lambda e: e.tensor_tensor(out=fl(ETAIL), in0=pgl[:, 0:NT * 4], in1=fl(GC), op=ALU.subtract),
                 reads=[b_pgl, b_g], writes=[b_g])
            k.op("act", lambda e: e.activation(out=fl(ETAIL), in_=fl(ETAIL), func=AF.Exp), reads=[b_g], writes=[b_g])

            x12 = k.sb("x12", [128, 12, 515], F32)
            b_x12 = k.buf()
            y12 = [k.sb("y12", [128, 12, 512], F32) for _ in range(2)]
            b_y12 = [k.bufs(12) for _ in range(2)]
            cacc = [k.sb("cacc", [128, 512], F32) for _ in range(2)]
            b_cacc = k.bufs(2)
            sqt = [k.sb("sqt", [128, 512], F32) for _ in range(2)]
            b_sqt = k.bufs(2)
            ident4 = k.sb("ident4", [128, 4, 128], F32)
            strict4 = k.sb("strict4", [128, 4, 128], F32)
            b_c4 = k.buf()
            for h in range(4):
                k.op("dve", lambda e: e.tensor_copy(out=ident4[:, h, :], in_=ident), reads=[b_cst], writes=[b_c4])
                k.op("dve", lambda e: e.tensor_copy(out=strict4[:, h, :], in_=strictT), reads=[b_cst], writes=[b_c4])

            def mk(name, n):
                return [k.sb(name, [128, 4, 128], F32) for _ in range(n)], k.bufs(n)

            vtok, b_vtok = mk("vtok", 4)
            ktail, b_ktail = mk("ktail", 4)
            KTc, b_KTc = mk("KTc", 4)
            Pm, b_Pm = mk("Pm", 4)
            aqk, b_aqk = mk("aqk", 4)
            QdT, b_QdT = mk("QdT", 4)
            GU, b_GU = mk("GU", 2)
            tmpd, b_tmpd = mk("tmpd", 2)
            decT, b_decT = mk("decT", 2)
            EGB, b_EGB = mk("EGB", 2)
            N1, b_N1 = mk("N1", 2)
            Ya, b_Ya = mk("Ya", 2)
            YTa, b_YTa = mk("YTa", 2)
            Yb, b_Yb = mk("Yb", 2)
            YTb, b_YTb = mk("YTb", 2)
            rneg, b_rneg = mk("rneg", 2)
            vnew, b_vnew = mk("vnew", 2)
            Sst = k.sb("Sst", [128, 4, 128], F32)
            b_S = k.buf()
            k.op("pool", lambda e: e.memset(Sst[:], 0.0), writes=[b_S])
            PBr = [[k.ps("pbr", [128, 512]) for _ in range(3)] for _ in range(2)]
            b_PBr = [k.pbufs(3) for _ in range(2)]
            zt = [k.sb("zt", [128, 512], F32) for _ in range(2)]
            b_zt = k.bufs(2)
            ost = [k.sb("ost", [128, 8], F32) for _ in range(2)]
            b_ost = k.bufs(2)
            ojunk = k.sb("ojunk", [128, 128], F32)
            b_ojunk = k.buf()
            mo = [k.sb("mo", [128, 512], BF16) for _ in range(2)]
            b_mo = k.bufs(2)
            H4 = range(4)

            def hs(h):
                return slice(h * 128, (h + 1) * 128)

            def v4(ps_tile):
                return ps_tile[:].rearrange("p (h c) -> p h c", h=4)

            def group_prep(g):
                gi = g % 2
                xg, bxg = x12, b_x12
                src = dnT.rearrange("(c p) s -> p c s", p=128)
                if g == 0:
                    k.op("pool", lambda e: e.memset(xg[:, :, 0:3], 0.0), writes=[bxg])
                    k.dma("sp", xg[:, :, 3:515], src[:, :, 0:512], writes=[bxg])
                else:
                    k.dma("sp", xg[:, :, :], src[:, :, g * 512 - 3:g * 512 + 512], writes=[bxg])
                for c in range(12):
                    j = c % 2
                    wv = lambda tap: sm[:, O_CONV + c * 4 + tap:O_CONV + c * 4 + tap + 1]
                    k.op("act", lambda e: e.activation(out=cacc[j][:], in_=xg[:, c, 0:512], func=AF.Copy, scale=wv(0)),
                         reads=[bxg, bsm], writes=[b_cacc[j]])
                    for tap in range(1, 4):
                        k.op("dve", lambda e: e.scalar_tensor_tensor(
                            out=cacc[j][:], in0=xg[:, c, tap:tap + 512], scalar=wv(tap), in1=cacc[j][:], op0=ALU.mult, op1=ALU.add),
                            reads=[bxg, bsm, b_cacc[j]], writes=[b_cacc[j]])
                    k.op("act", lambda e: e.activation(out=y12[gi][:, c, :], in_=cacc[j][:], func=AF.Silu),
                         reads=[b_cacc[j]], writes=[b_y12[gi][c]])
                    yield
                    if c < 8:
                        k.op("act", lambda e: e.activation(out=sqt[j][:], in_=y12[gi][:, c, :], func=AF.Square),
                             reads=[b_y12[gi][c]], writes=[b_sqt[j]])
                        k.op("pe", lambda e: e.matmul(PS[1][:], lhsT=ones, rhs=sqt[j][:], start=True, stop=True),
                             reads=[b_sqt[j], b_cst], writes=[b_PS[1]])
                        k.op("act", lambda e: e.activation(out=sqt[j][:], in_=PS[1][:], func=AF.Sqrt, bias=epsb[:, 0:1], scale=1.0),
                             reads=[b_PS[1], b_eps], writes=[b_sqt[j]])
                        k.op("dve", lambda e: e.reciprocal(out=sqt[j][:], in_=sqt[j][:]), reads=[b_sqt[j]], writes=[b_sqt[j]])
                        sc = (128.0 ** -0.5) if c < 4 else 1.0
                        k.op("dve", lambda e: e.scalar_tensor_tensor(
                            out=y12[gi][:, c, :], in0=y12[gi][:, c, :], scalar=sc, in1=sqt[j][:], op0=ALU.mult, op1=ALU.mult),
                            reads=[b_y12[gi][c], b_sqt[j]], writes=[b_y12[gi][c]])
                        yield

            def tile_prep(T):
                g, t = divmod(T, 4)
                gi = g % 2
                s_ = T % 4
                r = T % 2
                X, Y, Z = PBr[r]
                bX, bY, bZ = b_PBr[r]
                ts = slice(t * 128, (t + 1) * 128)
                QT = lambda h: y12[gi][:, h, ts]
                KT = lambda h: y12[gi][:, 4 + h, ts]
                VT = lambda h: y12[gi][:, 8 + h, ts]
                bQ = [b_y12[gi][h] for h in H4]
                bK = [b_y12[gi][4 + h] for h in H4]
                bV = [b_y12[gi][8 + h] for h in H4]
                col = lambda A, h: A[:, T, h:h + 1]
                for h in H4:
                    k.op("pe", lambda e: e.transpose(out=X[:, hs(h)], in_=VT(h), identity=ident), reads=[bV[h], b_cst], writes=[bX])
                for h in H4:
                    k.op("pe", lambda e: e.transpose(out=Y[:, hs(h)], in_=KT(h), identity=ident), reads=[bK[h], b_cst], writes=[bY])
                k.op("act", lambda e: e.activation(out=vtok[s_][:], in_=v4(X), func=AF.Copy), reads=[bX], writes=[b_vtok[s_]])
                for h in H4:
                    k.op("dve", lambda e: e.tensor_scalar(out=ktail[s_][:, h, :], in0=Y[:, hs(h)], scalar1=col(ETAIL, h), scalar2=None, op0=ALU.mult),
                         reads=[bY, b_g], writes=[b_ktail[s_]])
                k.op("pool", lambda e: e.tensor_copy(out=KTc[s_][:], in_=y12[gi][:, 4:8, ts]), reads=bK, writes=[b_KTc[s_]])
                for h in H4:
                    k.op("dve", lambda e: e.tensor_scalar(out=GU[r][:, h, :], in0=Umat, scalar1=col(G, h), scalar2=None, op0=ALU.mult),
                         reads=[b_cst, b_g], writes=[b_GU[r]])
                yield
                for h in H4:
                    k.op("pe", lambda e: e.matmul(X[:, hs(h)], lhsT=ones, rhs=GU[r][:, h, :], start=True, stop=True),
                         reads=[b_cst, b_GU[r]], writes=[bX])
                for h in H4:
                    k.op("pe", lambda e: e.matmul(Y[:, hs(h)], lhsT=KT(h), rhs=KT(h), start=True, stop=True), reads=[bK[h]], writes=[bY])
                for h in H4:
                    k.op("dve", lambda e: e.scalar_tensor_tensor(out=tmpd[r][:, h, :], in0=X[:, hs(h)], scalar=col(GC, h), in1=maskbT,
                                                                 op0=ALU.subtract, op1=ALU.add),
                         reads=[bX, b_g, b_cst], writes=[b_tmpd[r]])
                k.op("act", lambda e: e.activation(out=EGB[r][:], in_=v4(X), func=AF.Exp), reads=[bX], writes=[b_EGB[r]])
                k.op("act", lambda e: e.activation(out=decT[r][:], in_=tmpd[r][:], func=AF.Exp), reads=[b_tmpd[r]], writes=[b_decT[r]])
                yield
                for h in H4:
                    k.op("dve", lambda e: e.scalar_tensor_tensor(out=N1[r][:, h, :], in0=Y[:, hs(h)], scalar=col(BETA, h), in1=decT[r][:, h, :],
                                                                 op0=ALU.mult, op1=ALU.mult),
                         reads=[bY, b_g, b_decT[r]], writes=[b_N1[r]])
                k.op("pool", lambda e: e.tensor_tensor(out=Ya[r][:], in0=N1[r][:], in1=strict4[:], op=ALU.mult),
                     reads=[b_N1[r], b_c4], writes=[b_Ya[r]])
                k.op("pool", lambda e: e.tensor_tensor(out=Pm[s_][:], in0=ident4[:], in1=Ya[r][:], op=ALU.subtract),
                     reads=[b_Ya[r], b_c4], writes=[b_Pm[s_]])
                k.op("pool", lambda e: e.tensor_tensor(out=QdT[s_][:], in0=y12[gi][:, 0:4, ts], in1=EGB[r][:], op=ALU.mult),
                     reads=bQ + [b_EGB[r]], writes=[b_QdT[s_]])
                yield
                for h in H4:
                    k.op("pe", lambda e: e.transpose(out=X[:, hs(h)], in_=Ya[r][:, h, :], identity=ident), reads=[b_Ya[r], b_cst], writes=[bX])
                for h in H4:
                    k.op("pe", lambda e: e.matmul(Y[:, hs(h)], lhsT=KT(h), rhs=QT(h), start=True, stop=True), reads=[bK[h], bQ[h]], writes=[bY])
                k.op("act", lambda e: e.activation(out=YTa[r][:], in_=v4(X), func=AF.Copy), reads=[bX], writes=[b_YTa[r]])
                k.op("dve", lambda e: e.tensor_tensor(out=aqk[s_][:], in0=v4(Y), in1=decT[r][:], op=ALU.mult),
                     reads=[bY, b_decT[r]], writes=[b_aqk[s_]])
                yield
                Yc, YTc, bYc, bYTc = Ya[r], YTa[r], b_Ya[r], b_YTa[r]
                Yn, YTn, bYn, bYTn = Yb[r], YTb[r], b_Yb[r], b_YTb[r]
                for lev in range(1, 7):
                    if lev < 6:
                        for h in H4:
                            k.op("pe", lambda e: e.matmul(X[:, hs(h)], lhsT=YTc[:, h, :], rhs=Yc[:, h, :], start=True, stop=True),
                                 reads=[bYc, bYTc], writes=[bX])
                    for h in H4:
                        k.op("pe", lambda e: e.matmul(Y[:, hs(h)], lhsT=Yc[:, h, :], rhs=YTc[:, h, :], start=True, stop=True),
                             reads=[bYc, bYTc], writes=[bY])
                    if lev > 1:
                        for h in H4:
                            k.op("pe", lambda e: e.matmul(Z[:, hs(h)], lhsT=YTc[:, h, :], rhs=Pm[s_][:, h, :], start=True, stop=True),
                                 reads=[bYTc, b_Pm[s_]], writes=[bZ])
                    if lev < 6:
                        k.op("act", lambda e: e.activation(out=Yn[:], in_=v4(X), func=AF.Copy), reads=[bX], writes=[bYn])
                    k.op("dve", lambda e: e.tensor_copy(out=YTn[:], in_=v4(Y)), reads=[bY], writes=[bYTn])
                    if lev > 1:
                        k.op("dve", lambda e: e.tensor_tensor(out=Pm[s_][:], in0=v4(Z), in1=Pm[s_][:], op=ALU.add),
                             reads=[bZ, b_Pm[s_]], writes=[b_Pm[s_]])
                    yield
                    Yc, YTc, bYc, bYTc, Yn, YTn, bYn, bYTn = Yn, YTn, bYn, bYTn, Yc, YTc, bYc, bYTc
                for h in H4:
                    k.op("pe", lambda e: e.matmul(Z[:, hs(h)], lhsT=YTc[:, h, :], rhs=Pm[s_][:, h, :], start=True, stop=True),
                         reads=[bYTc, b_Pm[s_]], writes=[bZ])
                k.op("dve", lambda e: e.tensor_tensor(out=Pm[s_][:], in0=v4(Z), in1=Pm[s_][:], op=ALU.add),
                     reads=[bZ, b_Pm[s_]], writes=[b_Pm[s_]])
                yield

            def tile_scan(T):
                s_ = T % 4
                p = T % 2
                SA, SB = PS
                bSA, bSB = b_PS
                col = lambda A, h: A[:, T, h:h + 1]
                tok0 = T * 128
                k.dma("sp", zt[p][:], zbuf[tok0:tok0 + 128, :], writes=[b_zt[p]])
                for h in H4:
                    k.op("pe", lambda e: e.matmul(SA[:, hs(h)], lhsT=KTc[s_][:, h, :], rhs=Sst[:, h, :], start=True, stop=True),
                         reads=[b_KTc[s_], b_S], writes=[bSA])
                for h in H4:
                    k.op("dve", lambda e: e.scalar_tensor_tensor(out=rneg[p][:, h, :], in0=SA[:, hs(h)], scalar=col(EGC, h), in1=vtok[s_][:, h, :],
                                                                 op0=ALU.mult, op1=ALU.subtract),
                         reads=[bSA, b_g, b_vtok[s_]], writes=[b_rneg[p]])
                yield
                for h in H4:
                    k.op("pe", lambda e: e.matmul(SA[:, hs(h)], lhsT=Pm[s_][:, h, :], rhs=rneg[p][:, h, :], start=True, stop=True),
                         reads=[b_Pm[s_], b_rneg[p]], writes=[bSA])
                for h in H4:
                    k.op("act", lambda e: e.activation(out=vnew[p][:, h, :], in_=SA[:, hs(h)], func=AF.Copy, scale=col(NBETA, h)),
                         reads=[bSA, b_g], writes=[b_vnew[p]])
                yield
                for h in H4:
                    k.op("pe", lambda e: e.matmul(SB[:, hs(h)], lhsT=QdT[s_][:, h, :], rhs=Sst[:, h, :], start=True, stop=False),
                         reads=[b_QdT[s_], b_S], writes=[bSB])
                    k.op("pe", lambda e: e.matmul(SB[:, hs(h)], lhsT=aqk[s_][:, h, :], rhs=vnew[p][:, h, :], start=False, stop=True),
                         reads=[b_aqk[s_], b_vnew[p]], writes=[bSB])
                for h in H4:
                    k.op("pe", lambda e: e.matmul(SA[:, hs(h)], lhsT=ktail[s_][:, h, :], rhs=vnew[p][:, h, :], start=True, stop=True),
                         reads=[b_ktail[s_], b_vnew[p]], writes=[bSA])
                for h in H4:
                    k.op("dve", lambda e: e.scalar_tensor_tensor(out=Sst[:, h, :], in0=Sst[:, h, :], scalar=col(EGL, h), in1=SA[:, hs(h)],
                                                                 op0=ALU.mult, op1=ALU.add),
                         reads=[bSA, b_g, b_S], writes=[b_S])
                yield
                k.op("act", lambda e: e.activation(out=zt[p][:], in_=zt[p][:], func=AF.Silu), reads=[b_zt[p]], writes=[b_zt[p]])
                k.op("pool", lambda e: e.tensor_tensor(out=zt[p][:], in0=zt[p][:], in1=dng[:], op=ALU.mult),
                     reads=[b_zt[p], b_dng], writes=[b_zt[p]])
                for h in H4:
                    k.op("act", lambda e: e.activation(out=ojunk[:], in_=SB[:, hs(h)], func=AF.Square, accum_out=ost[p][:, h:h + 1]),
                         reads=[bSB], writes=[b_ojunk, b_ost[p]])
                k.op("act", lambda e: e.activation(out=ost[p][:, 4:8], in_=ost[p][:, 0:4], func=AF.Sqrt, bias=epsb[:, 0:1], scale=1.0 / 128),
                     reads=[b_ost[p], b_eps], writes=[b_ost[p]])
                k.op("dve", lambda e: e.reciprocal(out=ost[p][:, 4:8], in_=ost[p][:, 4:8]), reads=[b_ost[p]], writes=[b_ost[p]])
                for h in H4:
                    k.op("dve", lambda e: e.scalar_tensor_tensor(out=mo[p][:, hs(h)], in0=SB[:, hs(h)], scalar=ost[p][:, 4 + h:5 + h],
                                                                 in1=zt[p][:, hs(h)], op0=ALU.mult, op1=ALU.mult),
                         reads=[bSB, b_ost[p], b_zt[p]], writes=[b_mo[p]])
                k.dma("sp", merged[tok0:tok0 + 128, 0:512], mo[p][:], reads=[b_mo[p]])
                yield

            def chain(*gens):
                for gnr in gens:
                    yield from gnr

            def drive(gens):
                alive = list(gens)
                while alive:
                    nxt = []
                    for gnr in alive:
                        try:
                            next(gnr)
                            nxt.append(gnr)
                        except StopIteration:
                            pass
                    alive = nxt

            drive([group_prep(0)])
            st0 = [tile_prep(0), tile_prep(1)]
            if NG > 1:
                st0.append(group_prep(1))
            drive(st0)
            NP = NT // 2
            for pstep in range(NP):
                gens = [chain(tile_scan(2 * pstep), tile_scan(2 * pstep + 1))]
                if 2 * pstep + 2 < NT:
                    gens.append(tile_prep(2 * pstep + 2))
                    gens.append(tile_prep(2 * pstep + 3))
                if pstep % 2 == 0 and pstep > 0:
                    gq = pstep // 2 + 1
                    if gq < NG:
                        gens.append(group_prep(gq))
                drive(gens)

    def phase_swa(l, hook=None, do_merge=True):
        with k.scope():
            QTs = k.sb("QTs", [128, 2, S], BF16)
            KTz = k.sb("KTz", [128, 4, S], BF16)
            b_qk = k.buf()
            qv = swaQT.rearrange("(j p) s -> p j s", p=128)
            CH = min(S, 2048)
            for h in range(4):
                k.op("dve" if h % 2 else "pool", lambda e: e.memset(KTz[:, h, :], 0.0), writes=[b_qk])
            for c0 in range(0, S, CH):
                k.dma("sp", QTs[:, :, c0:c0 + CH], qv[:, :, c0:c0 + CH], writes=[b_qk])
                for h in range(4):
                    r0 = 64 * (h % 2)
                    k.dma("sp", KTz[r0:r0 + 64, h, c0:c0 + CH], swaKT[h * 64:(h + 1) * 64, c0:c0 + CH], writes=[b_qk])
            swab = k.sb("swab", [128, 24 * 128], F32)
            b_swab = k.buf()
            k.dma("sp", swab[:], cst_in[:, C_SWA:C_SWA + 24 * 128], writes=[b_swab])
            Vb = [k.sb("Vb", [128, 4 * VW], BF16) for _ in range(4)]
            b_Vb = k.bufs(4)
            pS = [[k.ps("pS", [128, 512]) for _ in range(2)] for _ in range(2)]
            b_pS = [k.pbufs(2) for _ in range(2)]
            pO = [k.ps("pO", [128, 512]) for _ in range(2)]
            b_pO = k.pbufs(2)
            tmp = [[k.sb("stmp", [128, 512], F32) for _ in range(2)] for _ in range(2)]
            b_tmp = [k.bufs(2) for _ in range(2)]
            PT = [[k.sb("PT", [128, 512], BF16) for _ in range(2)] for _ in range(2)]
            b_PT = [k.bufs(2) for _ in range(2)]
            ot = [k.sb("ot", [128, 4 * VW], F32) for _ in range(2)]
            b_ot = k.bufs(2)
            it = 0
            vcnt = 0
            for p, (win, dil) in enumerate(SWA_PATTERNS):
                L = S // dil
                nblk = L // 128
                for r in range(dil):
                    prev_v = None
                    for n in range(nblk):
                        t0 = r + dil * 128 * n
                        sl = lambda n_: slice(r + dil * 128 * n_, r + dil * 128 * n_ + dil * 127 + 1, dil)
                        vi = vcnt % 4
                        vcnt += 1
                        k.dma("sp", Vb[vi][:], swaV[sl(n), :], writes=[b_Vb[vi]])
                        ii = it % 2
                        it += 1
                        blks = ([0] if n > 0 else []) + [1]
                        for blk in blks:
                            ks = sl(n - 1) if blk == 0 else sl(n)
                            for h in range(4):
                                k.op("pe", lambda e: e.matmul(pS[ii][blk][:, h * 128:(h + 1) * 128],
                                                              lhsT=KTz[:, h, ks], rhs=QTs[:, h // 2, sl(n)],
                                                              start=True, stop=True),
                                     reads=[b_qk], writes=[b_pS[ii][blk]])
                            bo = ((p * 2 + blk) * 4) * 128
                            k.op("dve", lambda e: e.scalar_tensor_tensor(out=tmp[ii][blk][:], in0=pS[ii][blk][:], scalar=0.125,
                                                                         in1=swab[:, bo:bo + 512], op0=ALU.mult, op1=ALU.add),
                                 reads=[b_pS[ii][blk], b_swab], writes=[b_tmp[ii][blk]])
                            k.op("act", lambda e: e.activation(out=PT[ii][blk][:], in_=tmp[ii][blk][:], func=AF.Exp),
                                 reads=[b_tmp[ii][blk]], writes=[b_PT[ii][blk]])
                        for h in range(4):
                            if n > 0:
                                k.op("pe", lambda e: e.matmul(pO[ii][:, h * VW:(h + 1) * VW], lhsT=PT[ii][0][:, h * 128:(h + 1) * 128],
                                                              rhs=Vb[prev_v][:, h * VW:(h + 1) * VW], start=True, stop=False),
                                     reads=[b_PT[ii][0], b_Vb[prev_v]], writes=[b_pO[ii]])
                            k.op("pe", lambda e: e.matmul(pO[ii][:, h * VW:(h + 1) * VW], lhsT=PT[ii][1][:, h * 128:(h + 1) * 128],
                                                          rhs=Vb[vi][:, h * VW:(h + 1) * VW], start=(n == 0), stop=True),
                                 reads=[b_PT[ii][1], b_Vb[vi]], writes=[b_pO[ii]])
                        k.op("act", lambda e: e.activation(out=ot[ii][:], in_=pO[ii][:, 0:4 * VW], func=AF.Copy),
                             reads=[b_pO[ii]], writes=[b_ot[ii]])
                        k.dma("sp", ndbuf[p][sl(n), :], ot[ii][:], reads=[b_ot[ii]])
                        prev_v = vi
                        if hook is not None:
                            hook(it)
        if not do_merge:
            return
        with k.scope():
            nd = [[k.sb("nd", [128, 4 * VW], F32) for _ in range(3)] for _ in range(2)]
            b_nd = [k.bufs(3) for _ in range(2)]
            rd = [k.sb("rd", [128, 4], F32) for _ in range(2)]
            b_rd = k.bufs(2)
            mo = [k.sb("mo", [128, 256], BF16) for _ in range(2)]
            b_mo = k.bufs(2)
            for T in range(NT):
                i = T % 2
                tok0 = T * 128
                for p in range(3):
                    k.dma("sp", nd[i][p][:], ndbuf[p][tok0:tok0 + 128, :], writes=[b_nd[i][p]])
                k.op("dve", lambda e: e.tensor_tensor(out=nd[i][0][:], in0=nd[i][0][:], in1=nd[i][1][:], op=ALU.add),
                     reads=[b_nd[i][0], b_nd[i][1]], writes=[b_nd[i][0]])
                k.op("dve", lambda e: e.tensor_tensor(out=nd[i][0][:], in0=nd[i][0][:], in1=nd[i][2][:], op=ALU.add),
                     reads=[b_nd[i][0], b_nd[i][2]], writes=[b_nd[i][0]])
                k.op("dve", lambda e: e.reciprocal(out=rd[i][:], in_=nd[i][0][:, 64:4 * VW:VW]), reads=[b_nd[i][0]], writes=[b_rd[i]])
                for h in range(4):
                    k.op("dve", lambda e: e.tensor_scalar(out=mo[i][:, h * 64:(h + 1) * 64], in0=nd[i][0][:, h * VW:h * VW + 64],
                                                          scalar1=rd[i][:, h:h + 1], scalar2=None, op0=ALU.mult),
                         reads=[b_nd[i][0], b_rd[i]], writes=[b_mo[i]])
                k.dma("sp", merged[tok0:tok0 + 128, 512:768], mo[i][:], reads=[b_mo[i]])

    def sg_setup(l):
        sm = small[l]
        bsm = b_small[l]
        if True:
            sgg = k.sb("sgg", [128, 256], F32)
            sgb = k.sb("sgb", [128, 256], F32)
            b_sgp = k.buf()
            k.dma("sp", sgg[:], small_in[l, :, O_SGG:O_SGG + 256], writes=[b_sgp])
            k.dma("sp", sgb[:], small_in[l, :, O_SGB:O_SGB + 256], writes=[b_sgp])
            wraw = k.sb("wraw", [128, 4, 128], F32)
            b_wraw = k.buf()
            k.dma("sp", wraw[:], w_sp[l].rearrange("g t s -> t g s"), writes=[b_wraw])
            WT = k.sb("WT", [128, 4, 128], BF16)
            b_WT = k.buf()
            pm = [k.ps("pm", [128, 512]) for _ in range(2)]
            b_pm = k.pbufs(2)
            pw, b_pw = pm[0], b_pm[0]
            for g in range(4):
                k.op("pe", lambda e: e.transpose(out=pw[:, g * 128:(g + 1) * 128], in_=wraw[:, g, :], identity=ident),
                     reads=[b_wraw, b_cst], writes=[b_pw])
            for g in range(4):
                k.op("dve", lambda e: e.tensor_tensor(out=WT[:, g, :], in0=pw[:, g * 128:(g + 1) * 128], in1=Umat, op=ALU.mult),
                     reads=[b_pw, b_cst], writes=[b_WT])
            xt = [k.sb("xt", [128, 512], F32) for _ in range(2)]
            b_xt = k.bufs(2)
            t1 = [k.sb("t1", [128, 512], F32) for _ in range(2)]
            b_t1 = k.bufs(2)
            ge = [k.sb("ge", [128, 512], F32) for _ in range(2)]
            b_ge = k.bufs(2)
            stt = [k.sb("sst", [128, 16], F32) for _ in range(2)]
            b_stt = k.bufs(2)
            vn = [k.sb("vn", [128, 256], F32) for _ in range(2)]
            b_vn = k.bufs(2)
            vnb = [k.sb("vnb", [128, 256], BF16) for _ in range(2)]
            b_vnb = k.bufs(2)
            mo = [k.sb("mo", [128, 256], BF16) for _ in range(2)]
            b_mo = k.bufs(2)
            def sg_tile(T):
                i = T % 2
                tok0 = T * 128
                k.dma("sp", xt[i][:], sgbuf[tok0:tok0 + 128, :], writes=[b_xt[i]])
                k.op("pool", lambda e: e.tensor_tensor(out=t1[i][:], in0=xt[i][:], in1=xt[i][:], op=ALU.mult),
                     reads=[b_xt[i]], writes=[b_t1[i]])
                k.op("dve", lambda e: e.tensor_scalar(out=t1[i][:], in0=t1[i][:], scalar1=0.044715, scalar2=1.0, op0=ALU.mult, op1=ALU.add),
                     reads=[b_t1[i]], writes=[b_t1[i]])
                k.op("pool", lambda e: e.tensor_tensor(out=t1[i][:], in0=t1[i][:], in1=xt[i][:], op=ALU.mult),
                     reads=[b_t1[i], b_xt[i]], writes=[b_t1[i]])
                k.op("act", lambda e: e.activation(out=t1[i][:], in_=t1[i][:], func=AF.Sigmoid, scale=1.5957691216057308),
                     reads=[b_t1[i]], writes=[b_t1[i]])
                k.op("dve", lambda e: e.tensor_tensor(out=ge[i][:], in0=t1[i][:], in1=xt[i][:], op=ALU.mult),
                     reads=[b_t1[i], b_xt[i]], writes=[b_ge[i]])
                k.op("dve", lambda e: e.bn_stats(out=stt[i][:, 0:6], in_=ge[i][:, 256:512]), reads=[b_ge[i]], writes=[b_stt[i]])
                k.op("dve", lambda e: e.bn_aggr(out=stt[i][:, 8:10], in_=stt[i][:, 0:6]), reads=[b_stt[i]], writes=[b_stt[i]])
                k.op("act", lambda e: e.activation(out=stt[i][:, 10:11], in_=stt[i][:, 9:10], func=AF.Sqrt, bias=epsb[:, 0:1], scale=1.0),
                     reads=[b_stt[i], b_eps], writes=[b_stt[i]])
                k.op("dve", lambda e: e.reciprocal(out=stt[i][:, 11:12], in_=stt[i][:, 10:11]), reads=[b_stt[i]], writes=[b_stt[i]])
                k.op("dve", lambda e: e.tensor_scalar(out=vn[i][:], in0=ge[i][:, 256:512], scalar1=stt[i][:, 8:9], scalar2=stt[i][:, 11:12],
                                                      op0=ALU.subtract, op1=ALU.mult),
                     reads=[b_ge[i], b_stt[i]], writes=[b_vn[i]])
                k.op("pool", lambda e: e.tensor_tensor(out=vn[i][:], in0=vn[i][:], in1=sgg[:], op=ALU.mult),
                     reads=[b_vn[i], b_sgp], writes=[b_vn[i]])
                k.op("pool", lambda e: e.tensor_tensor(out=vnb[i][:], in0=vn[i][:], in1=sgb[:], op=ALU.add),
                     reads=[b_vn[i], b_sgp], writes=[b_vnb[i]])
                for g in range(4):
                    k.op("pe", lambda e: e.matmul(pm[i][:, g * 64:(g + 1) * 64], lhsT=WT[:, g, :], rhs=vnb[i][:, g * 64:(g + 1) * 64],
                                                  start=True, stop=True),
                         reads=[b_WT, b_vnb[i]], writes=[b_pm[i]])
                for g in range(4):
                    k.op("dve", lambda e: e.scalar_tensor_tensor(out=mo[i][:, g * 64:(g + 1) * 64], in0=pm[i][:, g * 64:(g + 1) * 64],
                                                                 scalar=sm[:, O_BSP + g:O_BSP + g + 1], in1=ge[i][:, g * 64:(g + 1) * 64],
                                                                 op0=ALU.add, op1=ALU.mult),
                         reads=[b_pm[i], bsm, b_ge[i]], writes=[b_mo[i]])
                k.dma("sp", merged[tok0:tok0 + 128, 768:1024], mo[i][:], reads=[b_mo[i]])
            return sg_tile

    def phase_sg(l):
        with k.scope():
            f = sg_setup(l)
            for T in range(NT):
                f(T)

    def wout_load(l):
        wo = k.sb("wo", [128, KC, D], BF16)
        b_wo = k.bufs(KC)
        stage = [k.sb("stg", [128, 1024], F32) for _ in range(2)]
        b_stage = k.bufs(2)
        load_w(wo, b_wo, w_out[l], KC, D, None, None, stage, b_stage)
        return wo, b_wo

    def phase_wout(l, pre=None, fused_merge=False):
        with k.scope():
            if pre is None:
                pre = wout_load(l)
            wo, b_wo = pre
            if fused_merge:
                nd = [[k.sb("nd", [128, 4 * VW], F32) for _ in range(3)] for _ in range(2)]
                b_nd = [k.bufs(3) for _ in range(2)]
                rd = [k.sb("rd", [128, 4], F32) for _ in range(2)]
                b_rd = k.bufs(2)
            mt = [k.sb("mt", [128, D], BF16) for _ in range(2)]
            b_mt = k.bufs(2)
            tp = [k.ps("tp", [128, D], BF16) for _ in range(2)]
            b_tp = k.pbufs(2)
            mT = [k.sb("mT", [128, KC, 128], BF16) for _ in range(2)]
            b_mT = k.bufs(2)
            hb = [k.sb("hb", [128, D], F32) for _ in range(2)]
            b_hb = k.bufs(2)
            po = [k.ps("po", [128, 512]) for _ in range(2)]
            b_po = k.pbufs(2)
            for T in range(NT):
                i = T % 2
                tok0 = T * 128
                if fused_merge:
                    k.dma("sp", mt[i][:, 0:512], merged[tok0:tok0 + 128, 0:512], writes=[b_mt[i]])
                    k.dma("sp", mt[i][:, 768:1024], merged[tok0:tok0 + 128, 768:1024], writes=[b_mt[i]])
                    for p in range(3):
                        k.dma("sp", nd[i][p][:], ndbuf[p][tok0:tok0 + 128, :], writes=[b_nd[i][p]])
                    k.op("pool", lambda e: e.tensor_tensor(out=nd[i][0][:], in0=nd[i][0][:], in1=nd[i][1][:], op=ALU.add),
                         reads=[b_nd[i][0], b_nd[i][1]], writes=[b_nd[i][0]])
                    k.op("pool", lambda e: e.tensor_tensor(out=nd[i][0][:], in0=nd[i][0][:], in1=nd[i][2][:], op=ALU.add),
                         reads=[b_nd[i][0], b_nd[i][2]], writes=[b_nd[i][0]])
                    k.op("dve", lambda e: e.reciprocal(out=rd[i][:], in_=nd[i][0][:, 64:4 * VW:VW]), reads=[b_nd[i][0]], writes=[b_rd[i]])
                    for h in range(4):
                        k.op("dve", lambda e: e.tensor_scalar(out=mt[i][:, 512 + h * 64:512 + (h + 1) * 64], in0=nd[i][0][:, h * VW:h * VW + 64],
                                                              scalar1=rd[i][:, h:h + 1], scalar2=None, op0=ALU.mult),
                             reads=[b_nd[i][0], b_rd[i]], writes=[b_mt[i]])
                else:
                    k.dma("sp", mt[i][:], merged[tok0:tok0 + 128, :], writes=[b_mt[i]])
                k.dma("sp", hb[i][:], hbuf[tok0:tok0 + 128, :], writes=[b_hb[i]])
                for kc in range(KC):
                    k.op("pe", lambda e: e.transpose(out=tp[i][:, kc * 128:(kc + 1) * 128], in_=mt[i][:, kc * 128:(kc + 1) * 128], identity=identb[:]),
                         reads=[b_mt[i], b_identb], writes=[b_tp[i]])
                k.op("act", lambda e: e.activation(out=mT[i][:], in_=tp[i][:].rearrange("p (a b) -> p a b", a=KC), func=AF.Copy),
                     reads=[b_tp[i]], writes=[b_mT[i]])
                for nh in range(2):
                    for kc in range(KC):
                        k.op("pe", lambda e: e.matmul(po[nh][:], lhsT=mT[i][:, kc, :], rhs=wo[:, kc, nh * 512:(nh + 1) * 512],
                                                      start=(kc == 0), stop=(kc == KC - 1)),
                             reads=[b_mT[i], b_wo[kc]], writes=[b_po[nh]])
                    k.op("dve", lambda e: e.tensor_tensor(out=hb[i][:, nh * 512:(nh + 1) * 512], in0=po[nh][:],
                                                          in1=hb[i][:, nh * 512:(nh + 1) * 512], op=ALU.add),
                         reads=[b_po[nh], b_hb[i]], writes=[b_hb[i]])
                k.dma("sp", hbuf[tok0:tok0 + 128, :], hb[i][:], reads=[b_hb[i]])

    def phase_mix2(l):
        with k.scope():
            pre = wout_load(l)
            sg_tile = sg_setup(l)
            state = {"T": 0}

            def hook(it):
                if it % 3 == 0 and state["T"] < NT:
                    sg_tile(state["T"])
                    state["T"] += 1

            phase_swa(l, hook=hook, do_merge=False)
            while state["T"] < NT:
                sg_tile(state["T"])
                state["T"] += 1
            k.barrier()
            phase_wout(l, pre=pre, fused_merge=True)

    def phase_xa(l):
        sm = small[l]
        bsm = b_small[l]
        with k.scope():
            KmT = k.sb("KmT", [128, 8, 256], BF16)
            Vm = k.sb("Vm", [128, 2, D], BF16)
            b_KmT = k.buf()
            b_Vm = k.buf()
            wq = k.sb("wq", [128, KC, D], BF16)
            wo = k.sb("wo", [128, KC, D], BF16)
            b_wq = k.bufs(KC)
            b_wo = k.bufs(KC)
            stage = [k.sb("stg", [128, 1024], F32) for _ in range(2)]
            b_stage = k.bufs(2)
            nrm = NormT()
            with k.scope():
                wkv = k.sb("wkv", [128, KC, 2 * D], BF16)
                b_wkv = k.bufs(KC)
                load_w(wkv, b_wkv, w_kv[l], KC, 2 * D, sm[:, O_GMEM:O_GMEM + 8], bsm, stage, b_stage)
                memT = k.sb("memT", [128, KC, 256], BF16)
                b_memT = k.bufs(2)
                mb = [k.sb("mb", [128, D], F32) for _ in range(2)]
                b_mb = k.bufs(2)
                pk = [k.ps("pk", [128, 512]) for _ in range(2)]
                b_pk = k.pbufs(2)
                for mt_ in range(2):
                    k.dma("sp", mb[mt_][:], mem_in[mt_ * 128:(mt_ + 1) * 128, :], writes=[b_mb[mt_]])
                    nrm.run(mb[mt_][:], b_mb[mt_], memT[:, :, mt_ * 128:(mt_ + 1) * 128], b_memT[mt_])
                for c in range(8):
                    j = c % 2
                    for kc in range(KC):
                        k.op("pe", lambda e: e.matmul(pk[j][:, 0:256], lhsT=wkv[:, kc, c * 128:(c + 1) * 128], rhs=memT[:, kc, :],
                                                      start=(kc == 0), stop=(kc == KC - 1)),
                             reads=[b_wkv[kc]] + b_memT, writes=[b_pk[j]])
                    k.op("act", lambda e: e.activation(out=KmT[:, c, :], in_=pk[j][:, 0:256], func=AF.Copy), reads=[b_pk[j]], writes=[b_KmT])
                for mt_ in range(2):
                    for nh in range(2):
                        j = (mt_ * 2 + nh) % 2
                        for kc in range(KC):
                            k.op("pe", lambda e: e.matmul(pk[j][:], lhsT=memT[:, kc, mt_ * 128:(mt_ + 1) * 128],
                                                          rhs=wkv[:, kc, D + nh * 512:D + (nh + 1) * 512], start=(kc == 0), stop=(kc == KC - 1)),
                                 reads=[b_wkv[kc], b_memT[mt_]], writes=[b_pk[j]])
                        k.op("dve", lambda e: e.tensor_copy(out=Vm[:, mt_, nh * 512:(nh + 1) * 512], in_=pk[j][:]), reads=[b_pk[j]], writes=[b_Vm])
            load_w(wq, b_wq, w_q[l], KC, D, sm[:, O_GX:O_GX + 8], bsm, stage, b_stage)
            load_w(wo, b_wo, w_o[l], KC, D, None, None, stage, b_stage)
            hb = [k.sb("hb", [128, D], F32) for _ in range(4)]
            b_hb = k.bufs(4)
            nT = [k.sb("nT", [128, KC, 512], BF16) for _ in range(2)]
            b_nT = [k.bufs(4) for _ in range(2)]
            QT = k.sb("QT", [128, 8, 512], BF16)
            b_QT = k.bufs(8)
            PT = [[k.sb("PT", [128, 512], BF16) for _ in range(2)] for _ in range(2)]
            b_PT = [k.bufs(2) for _ in range(2)]
            rden = [k.sb("rden", [128, 512], F32) for _ in range(2)]
            b_rden = k.bufs(2)
            OT = k.sb("OT", [128, 8, 512], BF16)
            b_OT = k.bufs(8)
            pq = [k.ps("pq", [128, 512]) for _ in range(2)]
            b_pq = k.pbufs(2)
            pd = k.ps("pd", [128, 512])
            b_pd = k.pbuf()
            pov = [k.ps("pov", [128, 512]) for _ in range(2)]
            b_pov = k.pbufs(2)
            po = k.ps("po", [128, 512])
            b_po = k.pbuf()
            cq = 0
            for g in range(NG):
                gi = g % 2
                for t in range(4):
                    tok0 = g * 512 + t * 128
                    k.dma("sp", hb[t][:], hbuf[tok0:tok0 + 128, :], writes=[b_hb[t]])
                    nrm.run(hb[t][:], b_hb[t], nT[gi][:, :, t * 128:(t + 1) * 128], b_nT[gi][t])
                for c in range(8):
                    j = c % 2
                    for kc in range(KC):
                        k.op("pe", lambda e: e.matmul(pq[j][:], lhsT=wq[:, kc, c * 128:(c + 1) * 128], rhs=nT[gi][:, kc, :],
                                                      start=(kc == 0), stop=(kc == KC - 1)),
                             reads=[b_wq[kc]] + b_nT[gi], writes=[b_pq[j]])
                    k.op("act", lambda e: e.activation(out=QT[:, c, :], in_=pq[j][:], func=AF.Copy, scale=0.0625),
                         reads=[b_pq[j]], writes=[b_QT[c]])
                for h in range(4):
                    hi = h % 2
                    for mt_ in range(2):
                        j = cq % 2
                        cq += 1
                        for half in range(2):
                            k.op("pe", lambda e: e.matmul(pq[j][:], lhsT=KmT[:, 2 * h + half, mt_ * 128:(mt_ + 1) * 128], rhs=QT[:, 2 * h + half, :],
                                                          start=(half == 0), stop=(half == 1)),
                                 reads=[b_KmT, b_QT[2 * h + half]], writes=[b_pq[j]])
                        k.op("act", lambda e: e.activation(out=PT[hi][mt_][:], in_=pq[j][:], func=AF.Exp),
                             reads=[b_pq[j]], writes=[b_PT[hi][mt_]])
                    for mt_ in range(2):
                        k.op("pe", lambda e: e.matmul(pd[:], lhsT=onesb[:], rhs=PT[hi][mt_][:], start=(mt_ == 0), stop=(mt_ == 1)),
                             reads=[b_identb, b_PT[hi][mt_]], writes=[b_pd])
                    k.op("dve", lambda e: e.reciprocal(out=rden[hi][:], in_=pd[:]), reads=[b_pd], writes=[b_rden[hi]])
                    for ec in range(2):
                        for mt_ in range(2):
                            k.op("pe", lambda e: e.matmul(pov[ec][:], lhsT=Vm[:, mt_, h * 256 + ec * 128:h * 256 + (ec + 1) * 128], rhs=PT[hi][mt_][:],
                                                          start=(mt_ == 0), stop=(mt_ == 1)),
                                 reads=[b_Vm, b_PT[hi][mt_]], writes=[b_pov[ec]])
                        k.op("dve", lambda e: e.tensor_tensor(out=OT[:, 2 * h + ec, :], in0=pov[ec][:], in1=rden[hi][:], op=ALU.mult),
                             reads=[b_pov[ec], b_rden[hi]], writes=[b_OT[2 * h + ec]])
                for t in range(4):
                    tok0 = g * 512 + t * 128
                    for nh in range(2):
                        for c in range(8):
                            k.op("pe", lambda e: e.matmul(po[:], lhsT=OT[:, c, t * 128:(t + 1) * 128], rhs=wo[:, c, nh * 512:(nh + 1) * 512],
                                                          start=(c == 0), stop=(c == 7)),
                                 reads=[b_OT[c], b_wo[c]], writes=[b_po])
                        k.op("dve", lambda e: e.tensor_tensor(out=hb[t][:, nh * 512:(nh + 1) * 512], in0=po[:],
                                                              in1=hb[t][:, nh * 512:(nh + 1) * 512], op=ALU.add),
                             reads=[b_po, b_hb[t]], writes=[b_hb[t]])
                    k.dma("sp", hbuf[tok0:tok0 + 128, :], hb[t][:], reads=[b_hb[t]])

    def phase_final():
        with k.scope():
            fb = k.sb("fb", [128, D], F32)
            b_fb = k.buf()
            k.dma("sp", fb[:], fin_in, writes=[b_fb])
            hb = [k.sb("hb", [128, D], F32) for _ in range(4)]
            b_hb = k.bufs(4)
            junk = k.sb("junk", [128, D], F32)
            b_junk = k.buf()
            st = [k.sb("st", [128, 4], F32) for _ in range(4)]
            b_st = k.bufs(4)
            for i in range(NT):
                j = i % 4
                k.dma("sp", hb[j][:], hbuf[i * 128:(i + 1) * 128, :], writes=[b_hb[j]])
                k.op("act", lambda e, j=j: e.activation(out=junk[:], in_=hb[j][:], func=AF.Square, accum_out=st[j][:, 0:1]),
                     reads=[b_hb[j]], writes=[b_junk, b_st[j]])
                k.op("act", lambda e, j=j: e.activation(out=st[j][:, 1:2], in_=st[j][:, 0:1], func=AF.Sqrt, bias=epsb[:, 0:1], scale=1.0 / D),
                     reads=[b_st[j], b_eps], writes=[b_st[j]])
                k.op("dve", lambda e, j=j: e.reciprocal(out=st[j][:, 2:3], in_=st[j][:, 1:2]), reads=[b_st[j]], writes=[b_st[j]])
                k.op("dve", lambda e, j=j: e.scalar_tensor_tensor(out=hb[j][:], in0=hb[j][:], scalar=st[j][:, 2:3], in1=fb[:],
                                                                  op0=ALU.mult, op1=ALU.mult),
                     reads=[b_hb[j], b_st[j], b_fb], writes=[b_hb[j]])
                k.dma("sp", out[i * 128:(i + 1) * 128, :], hb[j][:], reads=[b_hb[j]])

    phase_copy_x()
    done = False
    for l in range(n_layers):
        for ph in PHASES:
            if ph == "ffn1":
                phase_ffn(l, 0)
            elif ph == "ffn2":
                phase_ffn(l, 1)
            elif ph == "mixproj":
                phase_mixproj(l)
            elif ph == "dn":
                phase_dn(l)
            elif ph == "mix2":
                phase_mix2(l)
            elif ph == "swa":
                phase_swa(l)
            elif ph == "sg":
                phase_sg(l)
            elif ph == "wout":
                phase_wout(l)
            elif ph == "xa":
                phase_xa(l)
            if stop_after == (l, ph):
                done = True
                break
        if done:
            break
    phase_final()
    k.finish()
    return nc


DBG = {}
PHASES = ["ffn1", "mixproj", "dn", "mix2", "xa", "ffn2"]


SWA_PATTERNS = ((128, 1), (512, 4), (2048, 16))


def make_consts():
    cst = np.zeros((128, NCST), np.float32)
    cst[:, C_ID:C_ID + 128] = np.eye(128, dtype=np.float32)
    cst[:, C_ONE:C_ONE + 128] = 1.0
    s = np.arange(128)[:, None]
    i = np.arange(128)[None, :]
    cst[:, C_U:C_U + 128] = (s <= i)
    cst[:, C_MB:C_MB + 128] = np.where(i >= s, 0.0, NEG)
    cst[:, C_ST:C_ST + 128] = (i > s)
    slopes = [2.0 ** (-8.0 * (h + 1) / 4) for h in range(4)]
    kk = np.arange(128)[:, None]
    qq = np.arange(128)[None, :]
    for p, (win, dil) in enumerate(SWA_PATTERNS):
        for blk in range(2):
            rel = (128 if blk == 0 else 0) + qq - kk
            valid = (rel >= 0) & (rel <= win // dil)
            for h in range(4):
                o = C_SWA + ((p * 2 + blk) * 4 + h) * 128
                cst[:, o:o + 128] = np.where(valid, -slopes[h] * rel * dil, NEG)
    return cst


def pack_small(inp):
    sm = np.zeros((DEPTH, 128, NS), np.float32)

    def pk(v):
        return np.asarray(v, np.float32).reshape(8, 128).T

    for l in range(DEPTH):
        sm[l, :, O_G1:O_G1 + 8] = pk(inp["ffn1_norm"][l])
        sm[l, :, O_GM:O_GM + 8] = pk(inp["mix_norm"][l])
        sm[l, :, O_GX:O_GX + 8] = pk(inp["xa_norm"][l])
        sm[l, :, O_G2:O_G2 + 8] = pk(inp["ffn2_norm"][l])
        sm[l, :, O_GMEM:O_GMEM + 8] = pk(inp["xa_mem_norm"][l])
        cw = np.asarray(inp["dn_conv_w"][l], np.float32)
        sm[l, :, O_CONV:O_CONV + 48] = cw.reshape(4, 12, 128).transpose(2, 1, 0).reshape(128, 48)
        sm[l, :, O_ALOG:O_ALOG + 4] = np.asarray(inp["dn_a_log"][l], np.float32)[None, :]
        sm[l, :, O_DTB:O_DTB + 4] = np.asarray(inp["dn_dt_bias"][l], np.float32)[None, :]
        sm[l, :, O_DNG:O_DNG + 512] = np.tile(np.asarray(inp["dn_out_norm"][l], np.float32), 4)[None, :]
        sm[l, :, O_SGG:O_SGG + 256] = np.asarray(inp["sg_norm_gain"][l], np.float32)[None, :]
        sm[l, :, O_SGB:O_SGB + 256] = np.asarray(inp["sg_norm_bias"][l], np.float32)[None, :]
        sm[l, :, O_BSP:O_BSP + 4] = np.asarray(inp["sg_b_spatial"][l], np.float32).T
    return sm


_W_NAMES = ["ffn1_w_gate_up", "ffn2_w_gate_up", "ffn1_w_down", "ffn2_w_down", "mix_w_in", "mix_w_out",
            "xa_w_q", "xa_w_kv", "xa_w_o", "sg_w_spatial"]


def make_in_maps(inp, n_cores=N_CORES):
    shared = {n: np.ascontiguousarray(np.asarray(inp[n], np.float32)) for n in _W_NAMES}
    shared["small"] = pack_small(inp)
    shared["final_b"] = np.ascontiguousarray(np.broadcast_to(np.asarray(inp["final_norm"], np.float32)[None, :], (128, D)))
    shared["cst"] = make_consts()
    maps = []
    for c in range(n_cores):
        m = dict(shared)
        m["x"] = np.ascontiguousarray(np.asarray(inp["x"][c], np.float32))
        m["mem"] = np.ascontiguousarray(np.asarray(inp["mem"][c], np.float32))
        maps.append(m)
    return maps


_NC_CACHE = {}


def kernel(**inputs):
    if "nc" not in _NC_CACHE:
        _NC_CACHE["nc"] = build()
    nc = _NC_CACHE["nc"]
    maps = make_in_maps(inputs)
    res = run_bass_kernel_spmd(nc, maps, core_ids=list(range(N_CORES)))
    return np.stack([np.asarray(r["out"], np.float32) for r in res.results], axis=0)
```

```python
import numpy as np
import ml_dtypes
from contextlib import ExitStack, contextmanager
import concourse.bass as bass
import concourse.mybir as mybir
from concourse.bass_utils import run_bass_kernel_spmd

F32 = mybir.dt.float32
BF16 = mybir.dt.bfloat16
AF = mybir.ActivationFunctionType
ALU = mybir.AluOpType
AX = mybir.AxisListType

D = 1024
DFF = 2816
KC = 8
FC = 22
IN_COLS = 3336
EPS = 1e-6
NEG = -1.0e30
N_CORES = 4
SEQ = 8192
DEPTH = 2
MEMT = 256
VW = 72

O_G1, O_GM, O_GX, O_G2, O_GMEM = 0, 8, 16, 24, 32
O_CONV = 40
O_ALOG = 88
O_DTB = 92
O_BSP = 96
NSP = 100
O_DNG = 100
O_SGG = 612
O_SGB = 868
NS = 1124
C_ID, C_ONE, C_U, C_MB, C_ST, C_SWA = 0, 128, 256, 384, 512, 640
NCST = 640 + 24 * 128


class Buf:
    __slots__ = ("name", "w", "r", "psum")

    def __init__(self, name="", psum=False):
        self.name = name
        self.w = None
        self.r = {}
        self.psum = psum


class Q:
    def __init__(self, name):
        self.name = name
        self.prog = []
        self.sem = None
        self.count = 0
        self.waited = {}
        self.dsems = []
        self.dnext = 0


class _Rec:
    def __init__(self):
        self.call = None

    def __getattr__(self, name):
        def f(*a, **kw):
            self.call = (name, a, kw)
            return self
        return f


class K:
    def __init__(self, nc, n_dma_sems=(24, 12, 4)):
        self.nc = nc
        self.stacks = [ExitStack()]
        self.q = {n: Q(n) for n in ("pe", "act", "dve", "pool", "sp")}
        for n in ("pe", "act", "dve", "pool"):
            self.q[n].sem = self.stacks[0].enter_context(nc.semaphore("s_" + n))
        for qn, cnt in zip(("sp", "pool", "act"), n_dma_sems):
            for i in range(cnt):
                s = self.stacks[0].enter_context(nc.semaphore(f"d_{qn}_{i}"))
                self.q[qn].dsems.append([s, 0])
        self.nbuf = 0
        self.uid = 0

    def sb(self, name, shape, dt):
        self.uid += 1
        return self.stacks[-1].enter_context(self.nc.sbuf_tensor(f"{name}_{self.uid}", list(shape), dt))

    def ps(self, name, shape, dt=F32):
        self.uid += 1
        return self.stacks[-1].enter_context(self.nc.psum_tensor(f"{name}_{self.uid}", list(shape), dt))

    def buf(self, name=""):
        self.nbuf += 1
        return Buf(name or f"b{self.nbuf}")

    def bufs(self, n):
        return [self.buf() for _ in range(n)]

    def pbuf(self):
        self.nbuf += 1
        return Buf(f"p{self.nbuf}", psum=True)

    def pbufs(self, n):
        return [self.pbuf() for _ in range(n)]

    @contextmanager
    def scope(self):
        st = ExitStack()
        self.stacks.append(st)
        try:
            yield
        finally:
            self.barrier()
            self.stacks.pop()
            st.close()

    def barrier(self):
        deps = {}
        for n in ("pe", "act", "dve", "pool"):
            q = self.q[n]
            if q.count > 0:
                deps[id(q.sem)] = (q.sem, q.count)
        for n in ("sp", "pool", "act"):
            for s, v in self.q[n].dsems:
                if v > 0:
                    deps[id(s)] = (s, v)
        for n in ("pe", "act", "dve", "pool", "sp"):
            self._emit_waits(self.q[n], deps)

    def _deps(self, reads, writes, own=None):
        deps = {}

        def add(k, s, v):
            if k not in deps or deps[k][1] < v:
                deps[k] = (s, v)

        for b in reads:
            if b.w is not None:
                add(*b.w)
            if b.psum:
                for k, (s, v) in b.r.items():
                    if k != own:
                        add(k, s, v)
        for b in writes:
            if b.w is not None:
                add(*b.w)
            for k, (s, v) in b.r.items():
                add(k, s, v)
        return deps

    def _emit_waits(self, q, deps, skip_self=False):
        for k, (s, v) in deps.items():
            if skip_self and q.sem is not None and k == id(q.sem):
                continue
            if q.waited.get(k, 0) >= v:
                continue
            q.waited[k] = v
            q.prog.append(lambda e, s=s, v=v: e.wait_ge(s, v))

    def _mark(self, ev, reads, writes):
        k, s, v = ev
        for b in reads:
            b.r[k] = (s, v)
        for b in writes:
            b.w = ev
            b.r = {}

    def op(self, qn, fn, reads=(), writes=()):
        q = self.q[qn]
        deps = self._deps(reads, writes, own=id(q.sem))
        self._emit_waits(q, deps, skip_self=(qn == "pe"))
        q.count += 1
        c = q.count
        rec = _Rec()
        fn(rec)
        name, a, kw = rec.call
        q.prog.append(lambda e, name=name, a=a, kw=kw, s=q.sem: getattr(e, name)(*a, **kw).then_inc(s, 1))
        self._mark((id(q.sem), q.sem, c), reads, writes)

    def dma(self, qn, out, in_, reads=(), writes=(), **kw):
        q = self.q[qn]
        deps = self._deps(reads, writes)
        self._emit_waits(q, deps)
        slot = q.dsems[q.dnext]
        q.dnext = (q.dnext + 1) % len(q.dsems)
        s, v = slot
        if v > 0 and q.waited.get(id(s), 0) < v:
            q.waited[id(s)] = v
            q.prog.append(lambda e, s=s, v=v: e.wait_ge(s, v))
        slot[1] = v + 16
        q.prog.append(lambda e, s=s, out=out, in_=in_, kw=kw:
                      e.dma_start(out=out, in_=in_, **kw).then_inc(s, 16))
        self._mark((id(s), s, v + 16), reads, writes)

    def finish(self):
        self.barrier()
        nc = self.nc
        with nc.Block() as block:
            @block.tensor
            def _(e):
                for f in self.q["pe"].prog:
                    f(e)

            @block.scalar
            def _(e):
                for f in self.q["act"].prog:
                    f(e)

            @block.vector
            def _(e):
                for f in self.q["dve"].prog:
                    f(e)

            @block.gpsimd
            def _(e):
                for f in self.q["pool"].prog:
                    f(e)

            @block.sync
            def _(e):
                for f in self.q["sp"].prog:
                    f(e)
        while self.stacks:
            self.stacks.pop().close()


def run_pipelined(streams):
    its = [iter(src) for src, _ in streams]
    active = [[] for _ in streams]
    done = [False] * len(streams)
    while True:
        for si, (_, width) in enumerate(streams):
            while not done[si] and len(active[si]) < width:
                try:
                    active[si].append(next(its[si]))
                except StopIteration:
                    done[si] = True
            nxt = []
            for g in active[si]:
                try:
                    next(g)
                    nxt.append(g)
                except StopIteration:
                    pass
            active[si] = nxt
        if all(done) and not any(active):
            break

def build(S=SEQ, n_layers=DEPTH, stop_after=None, dbg=False):
    nc = bass.Bass("TRN2", target_bir_lowering=False)
    NT = S // 128
    NG = S // 512

    def din(name, shape, dt=F32):
        return nc.dram_tensor(name, list(shape), dt, kind="ExternalInput").ap()

    def dscr(name, shape, dt=F32):
        return nc.dram_tensor(name, list(shape), dt, kind=("ExternalOutput" if dbg else "Internal")).ap()

    x_in = din("x", [S, D])
    mem_in = din("mem", [MEMT, D])
    w_gu = [din("ffn1_w_gate_up", [DEPTH, D, 2 * DFF]), din("ffn2_w_gate_up", [DEPTH, D, 2 * DFF])]
    w_dn = [din("ffn1_w_down", [DEPTH, DFF, D]), din("ffn2_w_down", [DEPTH, DFF, D])]
    w_in = din("mix_w_in", [DEPTH, D, IN_COLS])
    w_out = din("mix_w_out", [DEPTH, D, D])
    w_q = din("xa_w_q", [DEPTH, D, D])
    w_kv = din("xa_w_kv", [DEPTH, D, 2 * D])
    w_o = din("xa_w_o", [DEPTH, D, D])
    w_sp = din("sg_w_spatial", [DEPTH, 4, 128, 128])
    small_in = din("small", [DEPTH, 128, NS])
    fin_in = din("final_b", [128, D])
    cst_in = din("cst", [128, NCST])
    out = nc.dram_tensor("out", [S, D], F32, kind="ExternalOutput").ap()

    hbuf = dscr("hbuf", [S, D])
    dnT = dscr("dnT", [1536, S])
    swaQT = dscr("swaQT", [256, S], BF16)
    swaKT = dscr("swaKT", [256, S], BF16)
    swaV = dscr("swaV", [S, 4 * VW], BF16)
    zbuf = dscr("zbuf", [S, 512])
    babuf = dscr("babuf", [S, 8])
    sgbuf = dscr("sgbuf", [S, 512])
    ndbuf = [dscr(f"nd{p}", [S, 4 * VW]) for p in range(3)]
    merged = dscr("merged", [S, D], BF16)

    k = K(nc)

    cst = k.sb("cst", [128, C_SWA], F32)
    small = [k.sb(f"small{l}", [128, NSP], F32) for l in range(DEPTH)]
    identb = k.sb("identb", [128, 128], BF16)
    onesb = k.sb("onesb", [128, 128], BF16)
    b_cst = k.buf("cst")
    b_small = k.bufs(DEPTH)
    b_identb = k.buf("identb")
    k.dma("sp", cst[:], cst_in[:, 0:C_SWA], writes=[b_cst])
    for l in range(DEPTH):
        k.dma("sp", small[l][:], small_in[l, :, 0:NSP], writes=[b_small[l]])
    k.op("dve", lambda e: e.tensor_copy(out=identb[:], in_=cst[:, C_ID:C_ID + 128]), reads=[b_cst], writes=[b_identb])
    k.op("dve", lambda e: e.tensor_copy(out=onesb[:], in_=cst[:, C_ONE:C_ONE + 128]), reads=[b_cst], writes=[b_identb])
    ident = cst[:, C_ID:C_ID + 128]
    ones = cst[:, C_ONE:C_ONE + 128]
    Umat = cst[:, C_U:C_U + 128]
    maskbT = cst[:, C_MB:C_MB + 128]
    strictT = cst[:, C_ST:C_ST + 128]

    rr = {"i": 0}

    def load_w(dst, dbufs, src2d, kcn, ncols, gain, gbuf, stage, sbufs):
        srcv = src2d.rearrange("(kc p) n -> p kc n", p=128)
        CH = stage[0].shape[1]
        for kc in range(kcn):
            for c0 in range(0, ncols, CH):
                w = min(CH, ncols - c0)
                i = rr["i"]
                rr["i"] += 1
                st, sbf = stage[i % len(stage)], sbufs[i % len(stage)]
                k.dma("sp", st[:, :w], srcv[:, kc, c0:c0 + w], writes=[sbf])
                o_ap = dst[:, kc, c0:c0 + w]
                i_ap = st[:, :w]
                rds = [sbf] + ([gbuf] if gain is not None else [])
                eng = ("act", "dve")[i % 2]
                if gain is not None:
                    g_ap = gain[:, kc:kc + 1]
                    if eng == "act":
                        k.op("act", lambda e, o_ap=o_ap, i_ap=i_ap, g_ap=g_ap:
                             e.activation(out=o_ap, in_=i_ap, func=AF.Copy, scale=g_ap),
                             reads=rds, writes=[dbufs[kc]])
                    else:
                        k.op(eng, lambda e, o_ap=o_ap, i_ap=i_ap, g_ap=g_ap:
                             e.tensor_scalar(out=o_ap, in0=i_ap, scalar1=g_ap, scalar2=None, op0=ALU.mult),
                             reads=rds, writes=[dbufs[kc]])
                else:
                    if eng == "act":
                        k.op("act", lambda e, o_ap=o_ap, i_ap=i_ap: e.activation(out=o_ap, in_=i_ap, func=AF.Copy),
                             reads=rds, writes=[dbufs[kc]])
                    else:
                        k.op(eng, lambda e, o_ap=o_ap, i_ap=i_ap: e.tensor_copy(out=o_ap, in_=i_ap),
                             reads=rds, writes=[dbufs[kc]])

    class NormT:
        def __init__(self):
            self.xs = [k.sb("xs", [128, D], BF16) for _ in range(2)]
            self.st = [k.sb("nst", [128, 4], F32) for _ in range(2)]
            self.tp = [k.ps("tp", [128, D], BF16) for _ in range(2)]
            self.b_xs = k.bufs(2)
            self.b_st = k.bufs(2)
            self.b_tp = k.pbufs(2)
            self.i = 0

        def run(self, hb, b_hb, dst3, b_dst, evac_eng=None):
            i = self.i % 2
            self.i += 1
            st, xs, tp = self.st[i], self.xs[i], self.tp[i]
            bst, bxs, btp = self.b_st[i], self.b_xs[i], self.b_tp[i]
            k.op("act", lambda e: e.activation(out=xs[:], in_=hb, func=AF.Square, accum_out=st[:, 0:1]),
                 reads=[b_hb], writes=[bxs, bst])
            k.op("act", lambda e: e.activation(out=st[:, 1:2], in_=st[:, 0:1], func=AF.Sqrt, bias=epsb[:, 0:1], scale=1.0 / D),
                 reads=[bst, b_eps], writes=[bst])
            k.op("dve", lambda e: e.reciprocal(out=st[:, 2:3], in_=st[:, 1:2]), reads=[bst], writes=[bst])
            k.op("dve", lambda e: e.tensor_scalar(out=xs[:], in0=hb, scalar1=st[:, 2:3], scalar2=None, op0=ALU.mult),
                 reads=[b_hb, bst], writes=[bxs])
            for kc in range(KC):
                k.op("pe", lambda e, kc=kc: e.transpose(out=tp[:, kc * 128:(kc + 1) * 128], in_=xs[:, kc * 128:(kc + 1) * 128],
                                                        identity=identb[:]),
                     reads=[bxs, b_identb], writes=[btp])
            eng = evac_eng or ("act" if self.i % 2 else "dve")
            src = tp[:].rearrange("p (a b) -> p a b", a=KC)
            if eng == "act":
                k.op("act", lambda e: e.activation(out=dst3, in_=src, func=AF.Copy), reads=[btp], writes=[b_dst])
            else:
                k.op("dve", lambda e: e.tensor_copy(out=dst3, in_=src), reads=[btp], writes=[b_dst])

    epsb = k.sb("epsb", [128, 1], F32)
    b_eps = k.buf()
    k.op("dve", lambda e: e.memset(epsb[:], EPS), writes=[b_eps])

    def phase_copy_x():
        with k.scope():
            t = [k.sb("cx", [128, D], F32) for _ in range(4)]
            bt = k.bufs(4)
            for i in range(NT):
                k.dma("sp", t[i % 4][:], x_in[i * 128:(i + 1) * 128, :], writes=[bt[i % 4]])
                k.dma("sp", hbuf[i * 128:(i + 1) * 128, :], t[i % 4][:], reads=[bt[i % 4]])

    def phase_ffn(l, which):
        gain_off = O_G1 if which == 0 else O_G2
        with k.scope():
            wgu = k.sb("wgu", [128, KC, 2 * DFF], BF16)
            wd = k.sb("wd", [128, FC, D], BF16)
            b_wgu = k.bufs(KC)
            b_wd = k.bufs(FC)
            stage = [k.sb("stg", [128, 704], F32) for _ in range(2)]
            b_stage = k.bufs(2)
            load_w(wgu, b_wgu, w_gu[which][l], KC, 2 * DFF, small[l][:, gain_off:gain_off + 8], b_small[l], stage, b_stage)
            load_w(wd, b_wd, w_dn[which][l], FC, D, None, None, stage, b_stage)
            nrm = NormT()
            hb = [k.sb("hb", [128, D], F32) for _ in range(4)]
            b_hb = k.bufs(4)
            nT = [k.sb("nT", [128, KC, 512], BF16) for _ in range(2)]
            b_nT = [k.bufs(4) for _ in range(2)]
            actT = k.sb("actT", [128, FC, 512], BF16)
            b_act = k.bufs(FC)
            sg = [k.sb("sg", [128, 512], F32) for _ in range(2)]
            b_sg = k.bufs(2)
            pg = [k.ps("pg", [128, 512]) for _ in range(2)]
            pu = [k.ps("pu", [128, 512]) for _ in range(2)]
            po = [k.ps("po", [128, 512]) for _ in range(2)]
            b_pg, b_pu, b_po = k.pbufs(2), k.pbufs(2), k.pbufs(2)
            for g in range(NG):
                gi = g % 2
                for t in range(4):
                    tok0 = g * 512 + t * 128
                    hbt, bhb = hb[t], b_hb[t]
                    k.dma("sp", hbt[:], hbuf[tok0:tok0 + 128, :], writes=[bhb])
                    nrm.run(hbt[:], bhb, nT[gi][:, :, t * 128:(t + 1) * 128], b_nT[gi][t])
                for fc in range(FC):
                    j = fc % 2
                    for kc in range(KC):
                        k.op("pe", lambda e, kc=kc, fc=fc, j=j: e.matmul(
                            pg[j][:], lhsT=wgu[:, kc, fc * 128:(fc + 1) * 128], rhs=nT[gi][:, kc, :],
                            start=(kc == 0), stop=(kc == KC - 1)),
                            reads=[b_wgu[kc]] + b_nT[gi], writes=[b_pg[j]])
                    for kc in range(KC):
                        k.op("pe", lambda e, kc=kc, fc=fc, j=j: e.matmul(
                            pu[j][:], lhsT=wgu[:, kc, DFF + fc * 128:DFF + (fc + 1) * 128], rhs=nT[gi][:, kc, :],
                            start=(kc == 0), stop=(kc == KC - 1)),
                            reads=[b_wgu[kc]] + b_nT[gi], writes=[b_pu[j]])
                    k.op("act", lambda e, j=j: e.activation(out=sg[j][:], in_=pg[j][:], func=AF.Silu),
                         reads=[b_pg[j]], writes=[b_sg[j]])
                    k.op("dve", lambda e, j=j, fc=fc: e.tensor_tensor(out=actT[:, fc, :], in0=pu[j][:], in1=sg[j][:], op=ALU.mult),
                         reads=[b_pu[j], b_sg[j]], writes=[b_act[fc]])
                for t in range(4):
                    tok0 = g * 512 + t * 128
                    hbt, bhb = hb[t], b_hb[t]
                    for nh in range(2):
                        j = (t * 2 + nh) % 2
                        for fc in range(FC):
                            k.op("pe", lambda e, fc=fc, t=t, nh=nh, j=j: e.matmul(
                                po[j][:], lhsT=actT[:, fc, t * 128:(t + 1) * 128], rhs=wd[:, fc, nh * 512:(nh + 1) * 512],
                                start=(fc == 0), stop=(fc == FC - 1)),
                                reads=[b_act[fc], b_wd[fc]], writes=[b_po[j]])
                        k.op("dve", lambda e, nh=nh, j=j, hbt=hbt: e.scalar_tensor_tensor(
                            out=hbt[:, nh * 512:(nh + 1) * 512], in0=po[j][:], scalar=0.5,
                            in1=hbt[:, nh * 512:(nh + 1) * 512], op0=ALU.mult, op1=ALU.add),
                            reads=[b_po[j], bhb], writes=[bhb])
                    k.dma("sp", hbuf[tok0:tok0 + 128, :], hbt[:], reads=[bhb])

    def phase_mixproj(l):
        with k.scope():
            win = k.sb("win", [128, KC, IN_COLS], BF16)
            b_win = k.bufs(KC)
            stage = [k.sb("stg", [128, 1112], F32) for _ in range(2)]
            b_stage = k.bufs(2)
            load_w(win, b_win, w_in[l], KC, IN_COLS, small[l][:, O_GM:O_GM + 8], b_small[l], stage, b_stage)
            nrm = NormT()
            hb = [k.sb("hb", [128, D], F32) for _ in range(4)]
            b_hb = k.bufs(4)
            nT = [k.sb("nT", [128, KC, 512], BF16) for _ in range(2)]
            b_nT = [k.bufs(4) for _ in range(2)]
            pf = [k.ps("pf", [128, 512]) for _ in range(2)]
            b_pf = k.pbufs(2)
            pt = [k.ps("pt", [128, 512]) for _ in range(4)]
            b_pt = k.pbufs(4)
            fo = [k.sb("fo", [128, 512], F32) for _ in range(2)]
            b_fo = k.bufs(2)
            fob = [k.sb("fob", [128, 512], BF16) for _ in range(2)]
            b_fob = k.bufs(2)
            zt = [k.sb("zt", [128, 512], F32) for _ in range(2)]
            b_zt = k.bufs(2)
            bat = [k.sb("bat", [128, 8], F32) for _ in range(2)]
            b_bat = k.bufs(2)
            vt = [k.sb("vt", [128, 4, VW], BF16) for _ in range(2)]
            b_vt = k.bufs(2)
            sgt = [k.sb("sgt", [128, 512], F32) for _ in range(2)]
            b_sgt = k.bufs(2)
            for j in range(2):
                k.op("dve", lambda e, j=j: e.memset(vt[j][:], 1.0), writes=[b_vt[j]])
            fch = [(c * 128, "dn", c) for c in range(12)] + [(2056 + j * 128, "q", j) for j in range(2)] + \
                  [(2312 + j * 128, "k", j) for j in range(2)]
            cnt = 0
            for g in range(NG):
                gi = g % 2
                for t in range(4):
                    tok0 = g * 512 + t * 128
                    k.dma("sp", hb[t][:], hbuf[tok0:tok0 + 128, :], writes=[b_hb[t]])
                    nrm.run(hb[t][:], b_hb[t], nT[gi][:, :, t * 128:(t + 1) * 128], b_nT[gi][t])
                for ci, (col0, kind, idx) in enumerate(fch):
                    if DBG.get("nofeat"):
                        break
                    j = ci % 2
                    for kc in range(KC):
                        k.op("pe", lambda e, kc=kc, col0=col0, j=j: e.matmul(
                            pf[j][:], lhsT=win[:, kc, col0:col0 + 128], rhs=nT[gi][:, kc, :],
                            start=(kc == 0), stop=(kc == KC - 1)), reads=[b_win[kc]] + b_nT[gi], writes=[b_pf[j]])
                    if kind == "dn":
                        if ci % 4 < 2:
                            k.op("act", lambda e, j=j: e.activation(out=fo[j][:], in_=pf[j][:], func=AF.Copy),
                                 reads=[b_pf[j]], writes=[b_fo[j]])
                        else:
                            k.op("dve", lambda e, j=j: e.tensor_copy(out=fo[j][:], in_=pf[j][:]),
                                 reads=[b_pf[j]], writes=[b_fo[j]])
                        k.dma("sp", dnT[idx * 128:(idx + 1) * 128, g * 512:(g + 1) * 512], fo[j][:], reads=[b_fo[j]])
                    else:
                        k.op("dve", lambda e, j=j: e.tensor_copy(out=fob[j][:], in_=pf[j][:]),
                             reads=[b_pf[j]], writes=[b_fob[j]])
                        dst = swaQT if kind == "q" else swaKT
                        k.dma("sp", dst[idx * 128:(idx + 1) * 128, g * 512:(g + 1) * 512], fob[j][:], reads=[b_fob[j]])
                for t in range(4):
                    if DBG.get("notok"):
                        break
                    tok0 = g * 512 + t * 128
                    j = cnt % 2
                    cnt += 1
                    groups = [(1536, 512), (2048, 8), (2568, 512), (3080, 256)]
                    for gi2, (c0, w) in enumerate(groups):
                        if DBG.get("noba") and gi2 == 1:
                            continue
                        for kc in range(KC):
                            k.op("pe", lambda e, kc=kc, c0=c0, w=w, gi2=gi2, t=t: e.matmul(
                                pt[gi2][:, 0:w], lhsT=nT[gi][:, kc, t * 128:(t + 1) * 128], rhs=win[:, kc, c0:c0 + w],
                                start=(kc == 0), stop=(kc == KC - 1)), reads=[b_win[kc], b_nT[gi][t]], writes=[b_pt[gi2]])
                    k.op("act", lambda e, j=j: e.activation(out=zt[j][:], in_=pt[0][:], func=AF.Copy),
                         reads=[b_pt[0]], writes=[b_zt[j]])
                    k.dma("sp", zbuf[tok0:tok0 + 128, :], zt[j][:], reads=[b_zt[j]])
                    if not DBG.get("noba"):
                        k.op("dve", lambda e, j=j: e.tensor_copy(out=bat[j][:], in_=pt[1][:, 0:8]),
                             reads=[b_pt[1]], writes=[b_bat[j]])
                        k.dma("sp", babuf[tok0:tok0 + 128, :], bat[j][:], reads=[b_bat[j]])
                    if not DBG.get("novt"):
                        k.op("dve", lambda e, j=j: e.tensor_copy(out=vt[j][:, :, 0:64],
                                                                 in_=pt[2][:, 0:256].rearrange("p (h d) -> p h d", h=4)),
                             reads=[b_pt[2]], writes=[b_vt[j]])
                        k.dma("sp", swaV[tok0:tok0 + 128, :], vt[j][:].rearrange("p h d -> p (h d)"), reads=[b_vt[j]])
                    k.op("act", lambda e, j=j: e.activation(out=sgt[j][:, 0:256], in_=pt[2][:, 256:512], func=AF.Copy),
                         reads=[b_pt[2]], writes=[b_sgt[j]])
                    k.op("act", lambda e, j=j: e.activation(out=sgt[j][:, 256:512], in_=pt[3][:, 0:256], func=AF.Copy),
                         reads=[b_pt[3]], writes=[b_sgt[j]])
                    k.dma("sp", sgbuf[tok0:tok0 + 128, :], sgt[j][:], reads=[b_sgt[j]])

    def phase_dn(l):
        sm = small[l]
        bsm = b_small[l]
        with k.scope():
            ba = k.sb("ba", [128, NT, 8], F32)
            b_ba = k.buf()
            bav = babuf.rearrange("(t p) c -> p t c", p=128)
            for t0 in range(0, NT, 8):
                t1 = min(NT, t0 + 8)
                k.dma("sp", ba[:, t0:t1, :], bav[:, t0:t1, :], writes=[b_ba])
            dng = k.sb("dng", [128, 512], F32)
            b_dng = k.buf()
            k.dma("sp", dng[:], small_in[l, :, O_DNG:O_DNG + 512], writes=[b_dng])
            G = k.sb("G", [128, NT, 4], F32)
            BETA = k.sb("BETA", [128, NT, 4], F32)
            NBETA = k.sb("NBETA", [128, NT, 4], F32)
            GC = k.sb("GC", [128, NT, 4], F32)
            EGC = k.sb("EGC", [128, NT, 4], F32)
            ETAIL = k.sb("ETAIL", [128, NT, 4], F32)
            EGL = k.sb("EGL", [128, NT, 4], F32)
            tA = k.sb("tA", [128, NT], F32)
            tB = k.sb("tB", [128, NT], F32)
            tC = k.sb("tC", [128, NT], F32)
            na = k.sb("na", [128, 4], F32)
            b_g = k.buf()
            b_tmp = k.buf()
            b_na = k.buf()
            k.op("act", lambda e: e.activation(out=na[:], in_=sm[:, O_ALOG:O_ALOG + 4], func=AF.Exp), reads=[bsm], writes=[b_na])
            k.op("dve", lambda e: e.tensor_scalar(out=na[:], in0=na[:], scalar1=-1.0, scalar2=None, op0=ALU.mult),
                 reads=[b_na], writes=[b_na])
            for h in range(4):
                k.op("dve", lambda e, h=h: e.tensor_scalar(out=tA[:], in0=ba[:, :, 4 + h], scalar1=sm[:, O_DTB + h:O_DTB + h + 1],
                                                           scalar2=None, op0=ALU.add), reads=[b_ba, bsm], writes=[b_tmp])
                k.op("dve", lambda e: e.scalar_tensor_tensor(out=tB[:], in0=tA[:], scalar=-1.0, in1=tA[:], op0=ALU.mult, op1=ALU.min),
                     reads=[b_tmp], writes=[b_tmp])
                k.op("act", lambda e: e.activation(out=tB[:], in_=tB[:], func=AF.Exp), reads=[b_tmp], writes=[b_tmp])
                k.op("act", lambda e: e.activation(out=tB[:], in_=tB[:], func=AF.Ln, bias=ones[:, 0:1], scale=1.0),
                     reads=[b_tmp, b_cst], writes=[b_tmp])
                k.op("dve", lambda e: e.scalar_tensor_tensor(out=tC[:], in0=tA[:], scalar=0.0, in1=tB[:], op0=ALU.max, op1=ALU.add),
                     reads=[b_tmp], writes=[b_tmp])
                k.op("dve", lambda e, h=h: e.tensor_scalar(out=G[:, :, h], in0=tC[:], scalar1=na[:, h:h + 1], scalar2=None, op0=ALU.mult),
                     reads=[b_tmp, b_na], writes=[b_g])
                k.op("act", lambda e, h=h: e.activation(out=tA[:], in_=ba[:, :, h], func=AF.Exp, scale=-1.0),
                     reads=[b_ba], writes=[b_tmp])
                k.op("dve", lambda e: e.tensor_scalar(out=tA[:], in0=tA[:], scalar1=1.0, scalar2=None, op0=ALU.add),
                     reads=[b_tmp], writes=[b_tmp])
                k.op("dve", lambda e, h=h: e.reciprocal(out=BETA[:, :, h], in_=tA[:]), reads=[b_tmp], writes=[b_g])
                k.op("dve", lambda e, h=h: e.tensor_scalar(out=NBETA[:, :, h], in0=BETA[:, :, h], scalar1=-1.0, scalar2=None, op0=ALU.mult),
                     reads=[b_g], writes=[b_g])
            PS = [k.ps("pss", [128, 512]) for _ in range(2)]
            b_PS = k.pbufs(2)
            pgc, pgl = PS
            b_pgc, b_pgl = b_PS
            Gf = G[:].rearrange("p t h -> p (t h)")
            k.op("pe", lambda e: e.matmul(pgc[:, 0:NT * 4], lhsT=Umat, rhs=Gf, start=True, stop=True), reads=[b_cst, b_g], writes=[b_pgc])
            k.op("pe", lambda e: e.matmul(pgl[:, 0:NT * 4], lhsT=ones, rhs=Gf, start=True, stop=True), reads=[b_cst, b_g], writes=[b_pgl])
            fl = lambda T: T[:].rearrange("p t h -> p (t h)")
            k.op("dve", lambda e: e.tensor_copy(out=fl(GC), in_=pgc[:, 0:NT * 4]), reads=[b_pgc], writes=[b_g])
            k.op("act", lambda e: e.activation(out=fl(EGC), in_=pgc[:, 0:NT * 4], func=AF.Exp), reads=[b_pgc], writes=[b_g])
            k.op("act", lambda e: e.activation(out=fl(EGL), in_=pgl[:, 0:NT * 4], func=AF.Exp), reads=[b_pgl], writes=[b_g])
            k.op("dve", lambda e: e.tensor_tensor(out=fl(ETAIL), in0=pgl[:, 0:NT * 4], in1=fl(GC), op=ALU.subtract),
                 reads=[b_pgl, b_g], writes=[b_g])
            k.op("act", lambda e: e.activation(out=fl(ETAIL), in_=fl(ETAIL), func=AF.Exp), reads=[b_g], writes=[b_g])

            x12 = k.sb("x12", [128, 12, 515], F32)
            b_x12 = k.buf()
            y12 = [k.sb("y12", [128, 12, 512], F32) for _ in range(2)]
            b_y12 = [k.bufs(12) for _ in range(2)]
            cacc = [k.sb("cacc", [128, 512], F32) for _ in range(2)]
            b_cacc = k.bufs(2)
            sqt = [k.sb("sqt", [128, 512], F32) for _ in range(2)]
            b_sqt = k.bufs(2)
            ident4 = k.sb("ident4", [128, 4, 128], F32)
            strict4 = k.sb("strict4", [128, 4, 128], F32)
            b_c4 = k.buf()
            for h in range(4):
                k.op("dve", lambda e: e.tensor_copy(out=ident4[:, h, :], in_=ident), reads=[b_cst], writes=[b_c4])
                k.op("dve", lambda e: e.tensor_copy(out=strict4[:, h, :], in_=strictT), reads=[b_cst], writes=[b_c4])

            def mk(name, n):
                return [k.sb(name, [128, 4, 128], F32) for _ in range(n)], k.bufs(n)

            vtok, b_vtok = mk("vtok", 4)
            ktail, b_ktail = mk("ktail", 4)
            KTc, b_KTc = mk("KTc", 4)
            Pm, b_Pm = mk("Pm", 4)
            aqk, b_aqk = mk("aqk", 4)
            QdT, b_QdT = mk("QdT", 4)
            GU, b_GU = mk("GU", 2)
            tmpd, b_tmpd = mk("tmpd", 2)
            decT, b_decT = mk("decT", 2)
            EGB, b_EGB = mk("EGB", 2)
            N1, b_N1 = mk("N1", 2)
            Ya, b_Ya = mk("Ya", 2)
            YTa, b_YTa = mk("YTa", 2)
            Yb, b_Yb = mk("Yb", 2)
            YTb, b_YTb = mk("YTb", 2)
            rneg, b_rneg = mk("rneg", 2)
            vnew, b_vnew = mk("vnew", 2)
            Sst = k.sb("Sst", [128, 4, 128], F32)
            b_S = k.buf()
            k.op("pool", lambda e: e.memset(Sst[:], 0.0), writes=[b_S])
            PBr = [[k.ps("pbr", [128, 512]) for _ in range(3)] for _ in range(2)]
            b_PBr = [k.pbufs(3) for _ in range(2)]
            zt = [k.sb("zt", [128, 512], F32) for _ in range(2)]
            b_zt = k.bufs(2)
            ost = [k.sb("ost", [128, 8], F32) for _ in range(2)]
            b_ost = k.bufs(2)
            ojunk = k.sb("ojunk", [128, 128], F32)
            b_ojunk = k.buf()
            osb = [k.sb("osb", [128, 512], F32) for _ in range(2)]
            b_osb = k.bufs(2)
            mo = [k.sb("mo", [128, 512], BF16) for _ in range(2)]
            b_mo = k.bufs(2)
            H4 = range(4)

            def hs(h):
                return slice(h * 128, (h + 1) * 128)

            def v4(ps_tile):
                return ps_tile[:].rearrange("p (h c) -> p h c", h=4)

            def group_prep(g):
                gi = g % 2
                xg, bxg = x12, b_x12
                src = dnT.rearrange("(c p) s -> p c s", p=128)
                if g == 0:
                    k.op("pool", lambda e: e.memset(xg[:, :, 0:3], 0.0), writes=[bxg])
                    k.dma("sp", xg[:, :, 3:515], src[:, :, 0:512], writes=[bxg])
                else:
                    k.dma("sp", xg[:, :, :], src[:, :, g * 512 - 3:g * 512 + 512], writes=[bxg])
                for c in range(12):
                    j = c % 2
                    wv = lambda tap: sm[:, O_CONV + c * 4 + tap:O_CONV + c * 4 + tap + 1]
                    k.op("act", lambda e: e.activation(out=cacc[j][:], in_=xg[:, c, 0:512], func=AF.Copy, scale=wv(0)),
                         reads=[bxg, bsm], writes=[b_cacc[j]])
                    for tap in range(1, 4):
                        k.op("dve", lambda e: e.scalar_tensor_tensor(
                            out=cacc[j][:], in0=xg[:, c, tap:tap + 512], scalar=wv(tap), in1=cacc[j][:], op0=ALU.mult, op1=ALU.add),
                            reads=[bxg, bsm, b_cacc[j]], writes=[b_cacc[j]])
                    k.op("act", lambda e: e.activation(out=y12[gi][:, c, :], in_=cacc[j][:], func=AF.Silu),
                         reads=[b_cacc[j]], writes=[b_y12[gi][c]])
                    yield
                    if c < 8:
                        k.op("act", lambda e: e.activation(out=sqt[j][:], in_=y12[gi][:, c, :], func=AF.Square),
                             reads=[b_y12[gi][c]], writes=[b_sqt[j]])
                        k.op("pe", lambda e: e.matmul(PS[1][:], lhsT=ones, rhs=sqt[j][:], start=True, stop=True),
                             reads=[b_sqt[j], b_cst], writes=[b_PS[1]])
                        k.op("act", lambda e: e.activation(out=sqt[j][:], in_=PS[1][:], func=AF.Sqrt, bias=epsb[:, 0:1], scale=1.0),
                             reads=[b_PS[1], b_eps], writes=[b_sqt[j]])
                        k.op("dve", lambda e: e.reciprocal(out=sqt[j][:], in_=sqt[j][:]), reads=[b_sqt[j]], writes=[b_sqt[j]])
                        sc = (128.0 ** -0.5) if c < 4 else 1.0
                        k.op("dve", lambda e: e.scalar_tensor_tensor(
                            out=y12[gi][:, c, :], in0=y12[gi][:, c, :], scalar=sc, in1=sqt[j][:], op0=ALU.mult, op1=ALU.mult),
                            reads=[b_y12[gi][c], b_sqt[j]], writes=[b_y12[gi][c]])
                        yield

            def tile_prep(T):
                g, t = divmod(T, 4)
                gi = g % 2
                s_ = T % 4
                r = T % 2
                X, Y, Z = PBr[r]
                bX, bY, bZ = b_PBr[r]
                ts = slice(t * 128, (t + 1) * 128)
                QT = lambda h: y12[gi][:, h, ts]
                KT = lambda h: y12[gi][:, 4 + h, ts]
                VT = lambda h: y12[gi][:, 8 + h, ts]
                bQ = [b_y12[gi][h] for h in H4]
                bK = [b_y12[gi][4 + h] for h in H4]
                bV = [b_y12[gi][8 + h] for h in H4]
                col = lambda A, h: A[:, T, h:h + 1]
                for h in H4:
                    k.op("pe", lambda e: e.transpose(out=X[:, hs(h)], in_=VT(h), identity=ident), reads=[bV[h], b_cst], writes=[bX])
                for h in H4:
                    k.op("pe", lambda e: e.transpose(out=Y[:, hs(h)], in_=KT(h), identity=ident), reads=[bK[h], b_cst], writes=[bY])
                k.op("act", lambda e: e.activation(out=vtok[s_][:], in_=v4(X), func=AF.Copy), reads=[bX], writes=[b_vtok[s_]])
                for h in H4:
                    k.op("dve", lambda e: e.tensor_scalar(out=ktail[s_][:, h, :], in0=Y[:, hs(h)], scalar1=col(ETAIL, h), scalar2=None, op0=ALU.mult),
                         reads=[bY, b_g], writes=[b_ktail[s_]])
                k.op("pool", lambda e: e.tensor_copy(out=KTc[s_][:], in_=y12[gi][:, 4:8, ts]), reads=bK, writes=[b_KTc[s_]])
                for h in H4:
                    k.op("dve", lambda e: e.tensor_scalar(out=GU[r][:, h, :], in0=Umat, scalar1=col(G, h), scalar2=None, op0=ALU.mult),
                         reads=[b_cst, b_g], writes=[b_GU[r]])
                yield
                for h in H4:
                    k.op("pe", lambda e: e.matmul(X[:, hs(h)], lhsT=ones, rhs=GU[r][:, h, :], start=True, stop=True),
                         reads=[b_cst, b_GU[r]], writes=[bX])
                for h in H4:
                    k.op("pe", lambda e: e.matmul(Y[:, hs(h)], lhsT=KT(h), rhs=KT(h), start=True, stop=True), reads=[bK[h]], writes=[bY])
                for h in H4:
                    k.op("dve", lambda e: e.scalar_tensor_tensor(out=tmpd[r][:, h, :], in0=X[:, hs(h)], scalar=col(GC, h), in1=maskbT,
                                                                 op0=ALU.subtract, op1=ALU.add),
                         reads=[bX, b_g, b_cst], writes=[b_tmpd[r]])
                k.op("act", lambda e: e.activation(out=EGB[r][:], in_=v4(X), func=AF.Exp), reads=[bX], writes=[b_EGB[r]])
                k.op("act", lambda e: e.activation(out=decT[r][:], in_=tmpd[r][:], func=AF.Exp), reads=[b_tmpd[r]], writes=[b_decT[r]])
                yield
                for h in H4:
                    k.op("dve", lambda e: e.scalar_tensor_tensor(out=N1[r][:, h, :], in0=Y[:, hs(h)], scalar=col(BETA, h), in1=decT[r][:, h, :],
                                                                 op0=ALU.mult, op1=ALU.mult),
                         reads=[bY, b_g, b_decT[r]], writes=[b_N1[r]])
                k.op("pool", lambda e: e.tensor_tensor(out=Ya[r][:], in0=N1[r][:], in1=strict4[:], op=ALU.mult),
                     reads=[b_N1[r], b_c4], writes=[b_Ya[r]])
                k.op("pool", lambda e: e.tensor_tensor(out=Pm[s_][:], in0=ident4[:], in1=Ya[r][:], op=ALU.subtract),
                     reads=[b_Ya[r], b_c4], writes=[b_Pm[s_]])
                k.op("pool", lambda e: e.tensor_tensor(out=QdT[s_][:], in0=y12[gi][:, 0:4, ts], in1=EGB[r][:], op=ALU.mult),
                     reads=bQ + [b_EGB[r]], writes=[b_QdT[s_]])
                yield
                for h in H4:
                    k.op("pe", lambda e: e.transpose(out=X[:, hs(h)], in_=Ya[r][:, h, :], identity=ident), reads=[b_Ya[r], b_cst], writes=[bX])
                for h in H4:
                    k.op("pe", lambda e: e.matmul(Y[:, hs(h)], lhsT=KT(h), rhs=QT(h), start=True, stop=True), reads=[bK[h], bQ[h]], writes=[bY])
                k.op("act", lambda e: e.activation(out=YTa[r][:], in_=v4(X), func=AF.Copy), reads=[bX], writes=[b_YTa[r]])
                k.op("dve", lambda e: e.tensor_tensor(out=aqk[s_][:], in0=v4(Y), in1=decT[r][:], op=ALU.mult),
                     reads=[bY, b_decT[r]], writes=[b_aqk[s_]])
                yield
                Yc, YTc, bYc, bYTc = Ya[r], YTa[r], b_Ya[r], b_YTa[r]
                Yn, YTn, bYn, bYTn = Yb[r], YTb[r], b_Yb[r], b_YTb[r]
                for lev in range(1, 7):
                    if lev < 6:
                        for h in H4:
                            k.op("pe", lambda e: e.matmul(X[:, hs(h)], lhsT=YTc[:, h, :], rhs=Yc[:, h, :], start=True, stop=True),
                                 reads=[bYc, bYTc], writes=[bX])
                    for h in H4:
                        k.op("pe", lambda e: e.matmul(Y[:, hs(h)], lhsT=Yc[:, h, :], rhs=YTc[:, h, :], start=True, stop=True),
                             reads=[bYc, bYTc], writes=[bY])
                    if lev > 1:
                        for h in H4:
                            k.op("pe", lambda e: e.matmul(Z[:, hs(h)], lhsT=YTc[:, h, :], rhs=Pm[s_][:, h, :], start=True, stop=True),
                                 reads=[bYTc, b_Pm[s_]], writes=[bZ])
                    if lev < 6:
                        k.op("act", lambda e: e.activation(out=Yn[:], in_=v4(X), func=AF.Copy), reads=[bX], writes=[bYn])
                    k.op("dve", lambda e: e.tensor_copy(out=YTn[:], in_=v4(Y)), reads=[bY], writes=[bYTn])
                    if lev > 1:
                        k.op("dve", lambda e: e.tensor_tensor(out=Pm[s_][:], in0=v4(Z), in1=Pm[s_][:], op=ALU.add),
                             reads=[bZ, b_Pm[s_]], writes=[b_Pm[s_]])
                    yield
                    Yc, YTc, bYc, bYTc, Yn, YTn, bYn, bYTn = Yn, YTn, bYn, bYTn, Yc, YTc, bYc, bYTc
                for h in H4:
                    k.op("pe", lambda e: e.matmul(Z[:, hs(h)], lhsT=YTc[:, h, :], rhs=Pm[s_][:, h, :], start=True, stop=True),
                         reads=[bYTc, b_Pm[s_]], writes=[bZ])
                k.op("dve", lambda e: e.tensor_tensor(out=Pm[s_][:], in0=v4(Z), in1=Pm[s_][:], op=ALU.add),
                     reads=[bZ, b_Pm[s_]], writes=[b_Pm[s_]])
                yield

            def tile_scan(T):
                s_ = T % 4
                p = T % 2
                SA, SB = PS
                bSA, bSB = b_PS
                col = lambda A, h: A[:, T, h:h + 1]
                tok0 = T * 128
                k.dma("sp", zt[p][:], zbuf[tok0:tok0 + 128, :], writes=[b_zt[p]])
                for h in H4:
                    k.op("pe", lambda e: e.matmul(SA[:, hs(h)], lhsT=KTc[s_][:, h, :], rhs=Sst[:, h, :], start=True, stop=True),
                         reads=[b_KTc[s_], b_S], writes=[bSA])
                for h in H4:
                    k.op("dve", lambda e: e.scalar_tensor_tensor(out=rneg[p][:, h, :], in0=SA[:, hs(h)], scalar=col(EGC, h), in1=vtok[s_][:, h, :],
                                                                 op0=ALU.mult, op1=ALU.subtract),
                         reads=[bSA, b_g, b_vtok[s_]], writes=[b_rneg[p]])
                yield
                for h in H4:
                    k.op("pe", lambda e: e.matmul(SA[:, hs(h)], lhsT=Pm[s_][:, h, :], rhs=rneg[p][:, h, :], start=True, stop=True),
                         reads=[b_Pm[s_], b_rneg[p]], writes=[bSA])
                for h in H4:
                    k.op("act", lambda e: e.activation(out=vnew[p][:, h, :], in_=SA[:, hs(h)], func=AF.Copy, scale=col(NBETA, h)),
                         reads=[bSA, b_g], writes=[b_vnew[p]])
                yield
                for h in H4:
                    k.op("pe", lambda e: e.matmul(SB[:, hs(h)], lhsT=QdT[s_][:, h, :], rhs=Sst[:, h, :], start=True, stop=False),
                         reads=[b_QdT[s_], b_S], writes=[bSB])
                    k.op("pe", lambda e: e.matmul(SB[:, hs(h)], lhsT=aqk[s_][:, h, :], rhs=vnew[p][:, h, :], start=False, stop=True),
                         reads=[b_aqk[s_], b_vnew[p]], writes=[bSB])
                k.op("act", lambda e: e.activation(out=osb[p][:], in_=SB[:], func=AF.Copy), reads=[bSB], writes=[b_osb[p]])
                for h in H4:
                    k.op("pe", lambda e: e.matmul(SA[:, hs(h)], lhsT=ktail[s_][:, h, :], rhs=vnew[p][:, h, :], start=True, stop=True),
                         reads=[b_ktail[s_], b_vnew[p]], writes=[bSA])
                for h in H4:
                    k.op("dve", lambda e: e.scalar_tensor_tensor(out=Sst[:, h, :], in0=Sst[:, h, :], scalar=col(EGL, h), in1=SA[:, hs(h)],
                                                                 op0=ALU.mult, op1=ALU.add),
                         reads=[bSA, b_g, b_S], writes=[b_S])
                yield
                k.op("act", lambda e: e.activation(out=zt[p][:], in_=zt[p][:], func=AF.Silu), reads=[b_zt[p]], writes=[b_zt[p]])
                k.op("pool", lambda e: e.tensor_tensor(out=zt[p][:], in0=zt[p][:], in1=dng[:], op=ALU.mult),
                     reads=[b_zt[p], b_dng], writes=[b_zt[p]])
                for h in H4:
                    k.op("act", lambda e: e.activation(out=ojunk[:], in_=osb[p][:, hs(h)], func=AF.Square, accum_out=ost[p][:, h:h + 1]),
                         reads=[b_osb[p]], writes=[b_ojunk, b_ost[p]])
                k.op("act", lambda e: e.activation(out=ost[p][:, 4:8], in_=ost[p][:, 0:4], func=AF.Sqrt, bias=epsb[:, 0:1], scale=1.0 / 128),
                     reads=[b_ost[p], b_eps], writes=[b_ost[p]])
                k.op("dve", lambda e: e.reciprocal(out=ost[p][:, 4:8], in_=ost[p][:, 4:8]), reads=[b_ost[p]], writes=[b_ost[p]])
                for h in H4:
                    k.op("dve", lambda e: e.scalar_tensor_tensor(out=mo[p][:, hs(h)], in0=osb[p][:, hs(h)], scalar=ost[p][:, 4 + h:5 + h],
                                                                 in1=zt[p][:, hs(h)], op0=ALU.mult, op1=ALU.mult),
                         reads=[b_osb[p], b_ost[p], b_zt[p]], writes=[b_mo[p]])
                k.dma("sp", merged[tok0:tok0 + 128, 0:512], mo[p][:], reads=[b_mo[p]])
                yield

            def chain(*gens):
                for gnr in gens:
                    yield from gnr

            def drive(gens):
                alive = list(gens)
                while alive:
                    nxt = []
                    for gnr in alive:
                        try:
                            next(gnr)
                            nxt.append(gnr)
                        except StopIteration:
                            pass
                    alive = nxt

            drive([group_prep(0)])
            st0 = [tile_prep(0), tile_prep(1)]
            if NG > 1:
                st0.append(group_prep(1))
            drive(st0)
            NP = NT // 2
            for pstep in range(NP):
                gens = [chain(tile_scan(2 * pstep), tile_scan(2 * pstep + 1))]
                if 2 * pstep + 2 < NT:
                    gens.append(tile_prep(2 * pstep + 2))
                    gens.append(tile_prep(2 * pstep + 3))
                if pstep % 2 == 0 and pstep > 0:
                    gq = pstep // 2 + 1
                    if gq < NG:
                        gens.append(group_prep(gq))
                drive(gens)

    def phase_swa(l, extra_streams=None, do_merge=True):
        with k.scope():
            QTs = k.sb("QTs", [128, 2, S], BF16)
            KTz = k.sb("KTz", [128, 4, S], BF16)
            b_qk = k.buf()
            qv = swaQT.rearrange("(j p) s -> p j s", p=128)
            CH = min(S, 2048)
            for h in range(4):
                k.op("dve" if h % 2 else "pool", lambda e: e.memset(KTz[:, h, :], 0.0), writes=[b_qk])
            for c0 in range(0, S, CH):
                k.dma("sp", QTs[:, :, c0:c0 + CH], qv[:, :, c0:c0 + CH], writes=[b_qk])
                for h in range(4):
                    r0 = 64 * (h % 2)
                    k.dma("sp", KTz[r0:r0 + 64, h, c0:c0 + CH], swaKT[h * 64:(h + 1) * 64, c0:c0 + CH], writes=[b_qk])
            swab = k.sb("swab", [128, 24 * 128], F32)
            b_swab = k.buf()
            k.dma("sp", swab[:], cst_in[:, C_SWA:C_SWA + 24 * 128], writes=[b_swab])
            Vb = [k.sb("Vb", [128, 4 * VW], BF16) for _ in range(4)]
            b_Vb = k.bufs(4)
            pS = [[k.ps("pS", [128, 512]) for _ in range(2)] for _ in range(2)]
            b_pS = [k.pbufs(2) for _ in range(2)]
            pO = [k.ps("pO", [128, 512]) for _ in range(2)]
            b_pO = k.pbufs(2)
            tmp = [[k.sb("stmp", [128, 512], F32) for _ in range(2)] for _ in range(2)]
            b_tmp = [k.bufs(2) for _ in range(2)]
            PT = [[k.sb("PT", [128, 512], BF16) for _ in range(2)] for _ in range(2)]
            b_PT = [k.bufs(2) for _ in range(2)]
            ot = [k.sb("ot", [128, 4 * VW], F32) for _ in range(2)]
            b_ot = k.bufs(2)
            def swa_iter(p, dil, r, n, it, vi, prev_v):
                sl = lambda n_: slice(r + dil * 128 * n_, r + dil * 128 * n_ + dil * 127 + 1, dil)
                ii = it % 2
                k.dma("sp", Vb[vi][:], swaV[sl(n), :], writes=[b_Vb[vi]])
                blks = ([0] if n > 0 else []) + [1]
                for blk in blks:
                    ks = sl(n - 1) if blk == 0 else sl(n)
                    for h in range(4):
                        k.op("pe", lambda e: e.matmul(pS[ii][blk][:, h * 128:(h + 1) * 128],
                                                      lhsT=KTz[:, h, ks], rhs=QTs[:, h // 2, sl(n)],
                                                      start=True, stop=True),
                             reads=[b_qk], writes=[b_pS[ii][blk]])
                yield
                for blk in blks:
                    bo = ((p * 2 + blk) * 4) * 128
                    k.op("dve", lambda e: e.scalar_tensor_tensor(out=tmp[ii][blk][:], in0=pS[ii][blk][:], scalar=0.125,
                                                                 in1=swab[:, bo:bo + 512], op0=ALU.mult, op1=ALU.add),
                         reads=[b_pS[ii][blk], b_swab], writes=[b_tmp[ii][blk]])
                yield
                for blk in blks:
                    k.op("act", lambda e: e.activation(out=PT[ii][blk][:], in_=tmp[ii][blk][:], func=AF.Exp),
                         reads=[b_tmp[ii][blk]], writes=[b_PT[ii][blk]])
                yield
                for h in range(4):
                    if n > 0:
                        k.op("pe", lambda e: e.matmul(pO[ii][:, h * VW:(h + 1) * VW], lhsT=PT[ii][0][:, h * 128:(h + 1) * 128],
                                                      rhs=Vb[prev_v][:, h * VW:(h + 1) * VW], start=True, stop=False),
                             reads=[b_PT[ii][0], b_Vb[prev_v]], writes=[b_pO[ii]])
                    k.op("pe", lambda e: e.matmul(pO[ii][:, h * VW:(h + 1) * VW], lhsT=PT[ii][1][:, h * 128:(h + 1) * 128],
                                                  rhs=Vb[vi][:, h * VW:(h + 1) * VW], start=(n == 0), stop=True),
                         reads=[b_PT[ii][1], b_Vb[vi]], writes=[b_pO[ii]])
                yield
                k.op("act", lambda e: e.activation(out=ot[ii][:], in_=pO[ii][:, 0:4 * VW], func=AF.Copy),
                     reads=[b_pO[ii]], writes=[b_ot[ii]])
                yield
                k.dma("sp", ndbuf[p][sl(n), :], ot[ii][:], reads=[b_ot[ii]])
                yield

            def swa_all():
                it = 0
                for p, (win, dil) in enumerate(SWA_PATTERNS):
                    nblk = (S // dil) // 128
                    for r in range(dil):
                        prev_v = None
                        for n in range(nblk):
                            vi = it % 4
                            yield swa_iter(p, dil, r, n, it, vi, prev_v)
                            prev_v = vi
                            it += 1

            streams = [(swa_all(), 2)]
            if extra_streams:
                streams += extra_streams
            run_pipelined(streams)
        if not do_merge:
            return
        with k.scope():
            nd = [[k.sb("nd", [128, 4 * VW], F32) for _ in range(3)] for _ in range(2)]
            b_nd = [k.bufs(3) for _ in range(2)]
            rd = [k.sb("rd", [128, 4], F32) for _ in range(2)]
            b_rd = k.bufs(2)
            mo = [k.sb("mo", [128, 256], BF16) for _ in range(2)]
            b_mo = k.bufs(2)
            for T in range(NT):
                i = T % 2
                tok0 = T * 128
                for p in range(3):
                    k.dma("sp", nd[i][p][:], ndbuf[p][tok0:tok0 + 128, :], writes=[b_nd[i][p]])
                k.op("dve", lambda e: e.tensor_tensor(out=nd[i][0][:], in0=nd[i][0][:], in1=nd[i][1][:], op=ALU.add),
                     reads=[b_nd[i][0], b_nd[i][1]], writes=[b_nd[i][0]])
                k.op("dve", lambda e: e.tensor_tensor(out=nd[i][0][:], in0=nd[i][0][:], in1=nd[i][2][:], op=ALU.add),
                     reads=[b_nd[i][0], b_nd[i][2]], writes=[b_nd[i][0]])
                k.op("dve", lambda e: e.reciprocal(out=rd[i][:], in_=nd[i][0][:, 64:4 * VW:VW]), reads=[b_nd[i][0]], writes=[b_rd[i]])
                for h in range(4):
                    k.op("dve", lambda e: e.tensor_scalar(out=mo[i][:, h * 64:(h + 1) * 64], in0=nd[i][0][:, h * VW:h * VW + 64],
                                                          scalar1=rd[i][:, h:h + 1], scalar2=None, op0=ALU.mult),
                         reads=[b_nd[i][0], b_rd[i]], writes=[b_mo[i]])
                k.dma("sp", merged[tok0:tok0 + 128, 512:768], mo[i][:], reads=[b_mo[i]])

    def sg_setup(l):
        sm = small[l]
        bsm = b_small[l]
        if True:
            sgg = k.sb("sgg", [128, 256], F32)
            sgb = k.sb("sgb", [128, 256], F32)
            b_sgp = k.buf()
            k.dma("sp", sgg[:], small_in[l, :, O_SGG:O_SGG + 256], writes=[b_sgp])
            k.dma("sp", sgb[:], small_in[l, :, O_SGB:O_SGB + 256], writes=[b_sgp])
            wraw = k.sb("wraw", [128, 4, 128], F32)
            b_wraw = k.buf()
            k.dma("sp", wraw[:], w_sp[l].rearrange("g t s -> t g s"), writes=[b_wraw])
            WT = k.sb("WT", [128, 4, 128], BF16)
            b_WT = k.buf()
            pm = [k.ps("pm", [128, 512]) for _ in range(2)]
            b_pm = k.pbufs(2)
            pw, b_pw = pm[0], b_pm[0]
            for g in range(4):
                k.op("pe", lambda e: e.transpose(out=pw[:, g * 128:(g + 1) * 128], in_=wraw[:, g, :], identity=ident),
                     reads=[b_wraw, b_cst], writes=[b_pw])
            for g in range(4):
                k.op("dve", lambda e: e.tensor_tensor(out=WT[:, g, :], in0=pw[:, g * 128:(g + 1) * 128], in1=Umat, op=ALU.mult),
                     reads=[b_pw, b_cst], writes=[b_WT])
            xt = [k.sb("xt", [128, 512], F32) for _ in range(2)]
            b_xt = k.bufs(2)
            t1 = [k.sb("t1", [128, 512], F32) for _ in range(2)]
            b_t1 = k.bufs(2)
            ge = [k.sb("ge", [128, 512], F32) for _ in range(2)]
            b_ge = k.bufs(2)
            stt = [k.sb("sst", [128, 16], F32) for _ in range(2)]
            b_stt = k.bufs(2)
            vn = [k.sb("vn", [128, 256], F32) for _ in range(2)]
            b_vn = k.bufs(2)
            vnb = [k.sb("vnb", [128, 256], BF16) for _ in range(2)]
            b_vnb = k.bufs(2)
            mo = [k.sb("mo", [128, 256], BF16) for _ in range(2)]
            b_mo = k.bufs(2)
            def sg_tile(T):
                i = T % 2
                tok0 = T * 128
                k.dma("sp", xt[i][:], sgbuf[tok0:tok0 + 128, :], writes=[b_xt[i]])
                yield
                k.op("pool", lambda e: e.tensor_tensor(out=t1[i][:], in0=xt[i][:], in1=xt[i][:], op=ALU.mult),
                     reads=[b_xt[i]], writes=[b_t1[i]])
                yield
                k.op("dve", lambda e: e.tensor_scalar(out=t1[i][:], in0=t1[i][:], scalar1=0.044715, scalar2=1.0, op0=ALU.mult, op1=ALU.add),
                     reads=[b_t1[i]], writes=[b_t1[i]])
                yield
                k.op("pool", lambda e: e.tensor_tensor(out=t1[i][:], in0=t1[i][:], in1=xt[i][:], op=ALU.mult),
                     reads=[b_t1[i], b_xt[i]], writes=[b_t1[i]])
                yield
                k.op("act", lambda e: e.activation(out=t1[i][:], in_=t1[i][:], func=AF.Sigmoid, scale=1.5957691216057308),
                     reads=[b_t1[i]], writes=[b_t1[i]])
                yield
                k.op("dve", lambda e: e.tensor_tensor(out=ge[i][:], in0=t1[i][:], in1=xt[i][:], op=ALU.mult),
                     reads=[b_t1[i], b_xt[i]], writes=[b_ge[i]])
                yield
                k.op("dve", lambda e: e.bn_stats(out=stt[i][:, 0:6], in_=ge[i][:, 256:512]), reads=[b_ge[i]], writes=[b_stt[i]])
                yield
                k.op("dve", lambda e: e.bn_aggr(out=stt[i][:, 8:10], in_=stt[i][:, 0:6]), reads=[b_stt[i]], writes=[b_stt[i]])
                yield
                k.op("act", lambda e: e.activation(out=stt[i][:, 10:11], in_=stt[i][:, 9:10], func=AF.Sqrt, bias=epsb[:, 0:1], scale=1.0),
                     reads=[b_stt[i], b_eps], writes=[b_stt[i]])
                yield
                k.op("dve", lambda e: e.reciprocal(out=stt[i][:, 11:12], in_=stt[i][:, 10:11]), reads=[b_stt[i]], writes=[b_stt[i]])
                yield
                k.op("dve", lambda e: e.tensor_scalar(out=vn[i][:], in0=ge[i][:, 256:512], scalar1=stt[i][:, 8:9], scalar2=stt[i][:, 11:12],
                                                      op0=ALU.subtract, op1=ALU.mult),
                     reads=[b_ge[i], b_stt[i]], writes=[b_vn[i]])
                yield
                k.op("pool", lambda e: e.tensor_tensor(out=vn[i][:], in0=vn[i][:], in1=sgg[:], op=ALU.mult),
                     reads=[b_vn[i], b_sgp], writes=[b_vn[i]])
                yield
                k.op("pool", lambda e: e.tensor_tensor(out=vnb[i][:], in0=vn[i][:], in1=sgb[:], op=ALU.add),
                     reads=[b_vn[i], b_sgp], writes=[b_vnb[i]])
                for g in range(4):
                    yield
                    k.op("pe", lambda e: e.matmul(pm[i][:, g * 64:(g + 1) * 64], lhsT=WT[:, g, :], rhs=vnb[i][:, g * 64:(g + 1) * 64],
                                                  start=True, stop=True),
                         reads=[b_WT, b_vnb[i]], writes=[b_pm[i]])
                for g in range(4):
                    yield
                    k.op("dve", lambda e: e.scalar_tensor_tensor(out=mo[i][:, g * 64:(g + 1) * 64], in0=pm[i][:, g * 64:(g + 1) * 64],
                                                                 scalar=sm[:, O_BSP + g:O_BSP + g + 1], in1=ge[i][:, g * 64:(g + 1) * 64],
                                                                 op0=ALU.add, op1=ALU.mult),
                         reads=[b_pm[i], bsm, b_ge[i]], writes=[b_mo[i]])
                yield
                k.dma("sp", merged[tok0:tok0 + 128, 768:1024], mo[i][:], reads=[b_mo[i]])
                yield
            return sg_tile

    def phase_sg(l):
        with k.scope():
            f = sg_setup(l)
            run_pipelined([((f(T) for T in range(NT)), 2)])

    def wout_load(l):
        wo = k.sb("wo", [128, KC, D], BF16)
        b_wo = k.bufs(KC)
        stage = [k.sb("stg", [128, 1024], F32) for _ in range(2)]
        b_stage = k.bufs(2)
        load_w(wo, b_wo, w_out[l], KC, D, None, None, stage, b_stage)
        return wo, b_wo

    def phase_wout(l, pre=None, fused_merge=False):
        with k.scope():
            if pre is None:
                pre = wout_load(l)
            wo, b_wo = pre
            if fused_merge:
                nd = [[k.sb("nd", [128, 4 * VW], F32) for _ in range(3)] for _ in range(2)]
                b_nd = [k.bufs(3) for _ in range(2)]
                rd = [k.sb("rd", [128, 4], F32) for _ in range(2)]
                b_rd = k.bufs(2)
            mt = [k.sb("mt", [128, D], BF16) for _ in range(2)]
            b_mt = k.bufs(2)
            tp = [k.ps("tp", [128, D], BF16) for _ in range(2)]
            b_tp = k.pbufs(2)
            mT = [k.sb("mT", [128, KC, 128], BF16) for _ in range(2)]
            b_mT = k.bufs(2)
            hb = [k.sb("hb", [128, D], F32) for _ in range(2)]
            b_hb = k.bufs(2)
            po4 = [k.ps("po", [128, 512]) for _ in range(4)]
            b_po4 = k.pbufs(4)
            def wout_tile(T):
                i = T % 2
                ip = T % 2
                tok0 = T * 128
                if fused_merge:
                    k.dma("sp", mt[i][:, 0:512], merged[tok0:tok0 + 128, 0:512], writes=[b_mt[i]])
                    yield
                    k.dma("sp", mt[i][:, 768:1024], merged[tok0:tok0 + 128, 768:1024], writes=[b_mt[i]])
                    for p in range(3):
                        yield
                        k.dma("sp", nd[i][p][:], ndbuf[p][tok0:tok0 + 128, :], writes=[b_nd[i][p]])
                    yield
                    k.op("pool", lambda e: e.tensor_tensor(out=nd[i][0][:], in0=nd[i][0][:], in1=nd[i][1][:], op=ALU.add),
                         reads=[b_nd[i][0], b_nd[i][1]], writes=[b_nd[i][0]])
                    yield
                    k.op("pool", lambda e: e.tensor_tensor(out=nd[i][0][:], in0=nd[i][0][:], in1=nd[i][2][:], op=ALU.add),
                         reads=[b_nd[i][0], b_nd[i][2]], writes=[b_nd[i][0]])
                    yield
                    k.op("dve", lambda e: e.reciprocal(out=rd[i][:], in_=nd[i][0][:, 64:4 * VW:VW]), reads=[b_nd[i][0]], writes=[b_rd[i]])
                    for h in range(4):
                        yield
                        k.op("dve", lambda e: e.tensor_scalar(out=mt[i][:, 512 + h * 64:512 + (h + 1) * 64], in0=nd[i][0][:, h * VW:h * VW + 64],
                                                              scalar1=rd[i][:, h:h + 1], scalar2=None, op0=ALU.mult),
                             reads=[b_nd[i][0], b_rd[i]], writes=[b_mt[i]])
                else:
                    yield
                    k.dma("sp", mt[i][:], merged[tok0:tok0 + 128, :], writes=[b_mt[i]])
                yield
                k.dma("sp", hb[i][:], hbuf[tok0:tok0 + 128, :], writes=[b_hb[i]])
                yield
                for kc in range(KC):
                    k.op("pe", lambda e: e.transpose(out=tp[ip][:, kc * 128:(kc + 1) * 128], in_=mt[i][:, kc * 128:(kc + 1) * 128], identity=identb[:]),
                         reads=[b_mt[i], b_identb], writes=[b_tp[ip]])
                yield
                k.op("act", lambda e: e.activation(out=mT[i][:], in_=tp[ip][:].rearrange("p (a b) -> p a b", a=KC), func=AF.Copy),
                     reads=[b_tp[ip]], writes=[b_mT[i]])
                for nh in range(2):
                    yield
                    for kc in range(KC):
                        k.op("pe", lambda e: e.matmul(po4[2 * ip + nh][:], lhsT=mT[i][:, kc, :], rhs=wo[:, kc, nh * 512:(nh + 1) * 512],
                                                      start=(kc == 0), stop=(kc == KC - 1)),
                             reads=[b_mT[i], b_wo[kc]], writes=[b_po4[2 * ip + nh]])
                    yield
                    k.op("dve", lambda e: e.tensor_tensor(out=hb[i][:, nh * 512:(nh + 1) * 512], in0=po4[2 * ip + nh][:],
                                                          in1=hb[i][:, nh * 512:(nh + 1) * 512], op=ALU.add),
                         reads=[b_po4[2 * ip + nh], b_hb[i]], writes=[b_hb[i]])
                yield
                k.dma("sp", hbuf[tok0:tok0 + 128, :], hb[i][:], reads=[b_hb[i]])
                yield

            run_pipelined([((wout_tile(T) for T in range(NT)), 2)])

    def phase_mix2(l):
        with k.scope():
            pre = wout_load(l)
            sg_tile = sg_setup(l)
            phase_swa(l, extra_streams=[((sg_tile(T) for T in range(NT)), 2)], do_merge=False)
            k.barrier()
            phase_wout(l, pre=pre, fused_merge=True)

    def phase_xa(l):
        sm = small[l]
        bsm = b_small[l]
        with k.scope():
            KmT = k.sb("KmT", [128, 8, 256], BF16)
            Vm = k.sb("Vm", [128, 2, D], BF16)
            b_KmT = k.buf()
            b_Vm = k.buf()
            wq = k.sb("wq", [128, KC, D], BF16)
            wo = k.sb("wo", [128, KC, D], BF16)
            b_wq = k.bufs(KC)
            b_wo = k.bufs(KC)
            stage = [k.sb("stg", [128, 1024], F32) for _ in range(2)]
            b_stage = k.bufs(2)
            nrm = NormT()
            with k.scope():
                wkv = k.sb("wkv", [128, KC, 2 * D], BF16)
                b_wkv = k.bufs(KC)
                load_w(wkv, b_wkv, w_kv[l], KC, 2 * D, sm[:, O_GMEM:O_GMEM + 8], bsm, stage, b_stage)
                memT = k.sb("memT", [128, KC, 256], BF16)
                b_memT = k.bufs(2)
                mb = [k.sb("mb", [128, D], F32) for _ in range(2)]
                b_mb = k.bufs(2)
                pk = [k.ps("pk", [128, 512]) for _ in range(2)]
                b_pk = k.pbufs(2)
                for mt_ in range(2):
                    k.dma("sp", mb[mt_][:], mem_in[mt_ * 128:(mt_ + 1) * 128, :], writes=[b_mb[mt_]])
                    nrm.run(mb[mt_][:], b_mb[mt_], memT[:, :, mt_ * 128:(mt_ + 1) * 128], b_memT[mt_])
                for c in range(8):
                    j = c % 2
                    for kc in range(KC):
                        k.op("pe", lambda e: e.matmul(pk[j][:, 0:256], lhsT=wkv[:, kc, c * 128:(c + 1) * 128], rhs=memT[:, kc, :],
                                                      start=(kc == 0), stop=(kc == KC - 1)),
                             reads=[b_wkv[kc]] + b_memT, writes=[b_pk[j]])
                    k.op("act", lambda e: e.activation(out=KmT[:, c, :], in_=pk[j][:, 0:256], func=AF.Copy), reads=[b_pk[j]], writes=[b_KmT])
                for mt_ in range(2):
                    for nh in range(2):
                        j = (mt_ * 2 + nh) % 2
                        for kc in range(KC):
                            k.op("pe", lambda e: e.matmul(pk[j][:], lhsT=memT[:, kc, mt_ * 128:(mt_ + 1) * 128],
                                                          rhs=wkv[:, kc, D + nh * 512:D + (nh + 1) * 512], start=(kc == 0), stop=(kc == KC - 1)),
                                 reads=[b_wkv[kc], b_memT[mt_]], writes=[b_pk[j]])
                        k.op("dve", lambda e: e.tensor_copy(out=Vm[:, mt_, nh * 512:(nh + 1) * 512], in_=pk[j][:]), reads=[b_pk[j]], writes=[b_Vm])
            load_w(wq, b_wq, w_q[l], KC, D, sm[:, O_GX:O_GX + 8], bsm, stage, b_stage)
            load_w(wo, b_wo, w_o[l], KC, D, None, None, stage, b_stage)
            hb = [k.sb("hb", [128, D], F32) for _ in range(4)]
            b_hb = k.bufs(4)
            nT = [k.sb("nT", [128, KC, 512], BF16) for _ in range(2)]
            b_nT = [k.bufs(4) for _ in range(2)]
            QT = k.sb("QT", [128, 8, 512], BF16)
            b_QT = k.bufs(8)
            PT = [[k.sb("PT", [128, 512], BF16) for _ in range(2)] for _ in range(2)]
            b_PT = [k.bufs(2) for _ in range(2)]
            rden = [k.sb("rden", [128, 512], F32) for _ in range(2)]
            b_rden = k.bufs(2)
            OT = k.sb("OT", [128, 8, 512], BF16)
            b_OT = k.bufs(8)
            pq = [k.ps("pq", [128, 512]) for _ in range(2)]
            b_pq = k.pbufs(2)
            pd = k.ps("pd", [128, 512])
            b_pd = k.pbuf()
            pov = [k.ps("pov", [128, 512]) for _ in range(2)]
            b_pov = k.pbufs(2)
            po = k.ps("po", [128, 512])
            b_po = k.pbuf()
            cq = 0
            for g in range(NG):
                gi = g % 2
                for t in range(4):
                    tok0 = g * 512 + t * 128
                    k.dma("sp", hb[t][:], hbuf[tok0:tok0 + 128, :], writes=[b_hb[t]])
                    nrm.run(hb[t][:], b_hb[t], nT[gi][:, :, t * 128:(t + 1) * 128], b_nT[gi][t])
                for c in range(8):
                    j = c % 2
                    for kc in range(KC):
                        k.op("pe", lambda e: e.matmul(pq[j][:], lhsT=wq[:, kc, c * 128:(c + 1) * 128], rhs=nT[gi][:, kc, :],
                                                      start=(kc == 0), stop=(kc == KC - 1)),
                             reads=[b_wq[kc]] + b_nT[gi], writes=[b_pq[j]])
                    k.op("act", lambda e: e.activation(out=QT[:, c, :], in_=pq[j][:], func=AF.Copy, scale=0.0625),
                         reads=[b_pq[j]], writes=[b_QT[c]])
                for h in range(4):
                    hi = h % 2
                    for mt_ in range(2):
                        j = cq % 2
                        cq += 1
                        for half in range(2):
                            k.op("pe", lambda e: e.matmul(pq[j][:], lhsT=KmT[:, 2 * h + half, mt_ * 128:(mt_ + 1) * 128], rhs=QT[:, 2 * h + half, :],
                                                          start=(half == 0), stop=(half == 1)),
                                 reads=[b_KmT, b_QT[2 * h + half]], writes=[b_pq[j]])
                        k.op("act", lambda e: e.activation(out=PT[hi][mt_][:], in_=pq[j][:], func=AF.Exp),
                             reads=[b_pq[j]], writes=[b_PT[hi][mt_]])
                    for mt_ in range(2):
                        k.op("pe", lambda e: e.matmul(pd[:], lhsT=onesb[:], rhs=PT[hi][mt_][:], start=(mt_ == 0), stop=(mt_ == 1)),
                             reads=[b_identb, b_PT[hi][mt_]], writes=[b_pd])
                    k.op("dve", lambda e: e.reciprocal(out=rden[hi][:], in_=pd[:]), reads=[b_pd], writes=[b_rden[hi]])
                    for ec in range(2):
                        for mt_ in range(2):
                            k.op("pe", lambda e: e.matmul(pov[ec][:], lhsT=Vm[:, mt_, h * 256 + ec * 128:h * 256 + (ec + 1) * 128], rhs=PT[hi][mt_][:],
                                                          start=(mt_ == 0), stop=(mt_ == 1)),
                                 reads=[b_Vm, b_PT[hi][mt_]], writes=[b_pov[ec]])
                        k.op("dve", lambda e: e.tensor_tensor(out=OT[:, 2 * h + ec, :], in0=pov[ec][:], in1=rden[hi][:], op=ALU.mult),
                             reads=[b_pov[ec], b_rden[hi]], writes=[b_OT[2 * h + ec]])
                for t in range(4):
                    tok0 = g * 512 + t * 128
                    for nh in range(2):
                        for c in range(8):
                            k.op("pe", lambda e: e.matmul(po[:], lhsT=OT[:, c, t * 128:(t + 1) * 128], rhs=wo[:, c, nh * 512:(nh + 1) * 512],
                                                          start=(c == 0), stop=(c == 7)),
                                 reads=[b_OT[c], b_wo[c]], writes=[b_po])
                        k.op("dve", lambda e: e.tensor_tensor(out=hb[t][:, nh * 512:(nh + 1) * 512], in0=po[:],
                                                              in1=hb[t][:, nh * 512:(nh + 1) * 512], op=ALU.add),
                             reads=[b_po, b_hb[t]], writes=[b_hb[t]])
                    k.dma("sp", hbuf[tok0:tok0 + 128, :], hb[t][:], reads=[b_hb[t]])

    def phase_final():
        with k.scope():
            fb = k.sb("fb", [128, D], F32)
            b_fb = k.buf()
            k.dma("sp", fb[:], fin_in, writes=[b_fb])
            hb = [k.sb("hb", [128, D], F32) for _ in range(4)]
            b_hb = k.bufs(4)
            junk = k.sb("junk", [128, D], F32)
            b_junk = k.buf()
            st = [k.sb("st", [128, 4], F32) for _ in range(4)]
            b_st = k.bufs(4)
            for i in range(NT):
                j = i % 4
                k.dma("sp", hb[j][:], hbuf[i * 128:(i + 1) * 128, :], writes=[b_hb[j]])
                k.op("act", lambda e, j=j: e.activation(out=junk[:], in_=hb[j][:], func=AF.Square, accum_out=st[j][:, 0:1]),
                     reads=[b_hb[j]], writes=[b_junk, b_st[j]])
                k.op("act", lambda e, j=j: e.activation(out=st[j][:, 1:2], in_=st[j][:, 0:1], func=AF.Sqrt, bias=epsb[:, 0:1], scale=1.0 / D),
                     reads=[b_st[j], b_eps], writes=[b_st[j]])
                k.op("dve", lambda e, j=j: e.reciprocal(out=st[j][:, 2:3], in_=st[j][:, 1:2]), reads=[b_st[j]], writes=[b_st[j]])
                k.op("dve", lambda e, j=j: e.scalar_tensor_tensor(out=hb[j][:], in0=hb[j][:], scalar=st[j][:, 2:3], in1=fb[:],
                                                                  op0=ALU.mult, op1=ALU.mult),
                     reads=[b_hb[j], b_st[j], b_fb], writes=[b_hb[j]])
                k.dma("sp", out[i * 128:(i + 1) * 128, :], hb[j][:], reads=[b_hb[j]])

    phase_copy_x()
    done = False
    for l in range(n_layers):
        for ph in PHASES:
            if ph == "ffn1":
                phase_ffn(l, 0)
            elif ph == "ffn2":
                phase_ffn(l, 1)
            elif ph == "mixproj":
                phase_mixproj(l)
            elif ph == "dn":
                phase_dn(l)
            elif ph == "mix2":
                phase_mix2(l)
            elif ph == "swa":
                phase_swa(l)
            elif ph == "sg":
                phase_sg(l)
            elif ph == "wout":
                phase_wout(l)
            elif ph == "xa":
                phase_xa(l)
            if stop_after == (l, ph):
                done = True
                break
        if done:
            break
    phase_final()
    k.finish()
    return nc


DBG = {}
PHASES = ["ffn1", "mixproj", "dn", "mix2", "xa", "ffn2"]


SWA_PATTERNS = ((128, 1), (512, 4), (2048, 16))


def make_consts():
    cst = np.zeros((128, NCST), np.float32)
    cst[:, C_ID:C_ID + 128] = np.eye(128, dtype=np.float32)
    cst[:, C_ONE:C_ONE + 128] = 1.0
    s = np.arange(128)[:, None]
    i = np.arange(128)[None, :]
    cst[:, C_U:C_U + 128] = (s <= i)
    cst[:, C_MB:C_MB + 128] = np.where(i >= s, 0.0, NEG)
    cst[:, C_ST:C_ST + 128] = (i > s)
    slopes = [2.0 ** (-8.0 * (h + 1) / 4) for h in range(4)]
    kk = np.arange(128)[:, None]
    qq = np.arange(128)[None, :]
    for p, (win, dil) in enumerate(SWA_PATTERNS):
        for blk in range(2):
            rel = (128 if blk == 0 else 0) + qq - kk
            valid = (rel >= 0) & (rel <= win // dil)
            for h in range(4):
                o = C_SWA + ((p * 2 + blk) * 4 + h) * 128
                cst[:, o:o + 128] = np.where(valid, -slopes[h] * rel * dil, NEG)
    return cst


def pack_small(inp):
    sm = np.zeros((DEPTH, 128, NS), np.float32)

    def pk(v):
        return np.asarray(v, np.float32).reshape(8, 128).T

    for l in range(DEPTH):
        sm[l, :, O_G1:O_G1 + 8] = pk(inp["ffn1_norm"][l])
        sm[l, :, O_GM:O_GM + 8] = pk(inp["mix_norm"][l])
        sm[l, :, O_GX:O_GX + 8] = pk(inp["xa_norm"][l])
        sm[l, :, O_G2:O_G2 + 8] = pk(inp["ffn2_norm"][l])
        sm[l, :, O_GMEM:O_GMEM + 8] = pk(inp["xa_mem_norm"][l])
        cw = np.asarray(inp["dn_conv_w"][l], np.float32)
        sm[l, :, O_CONV:O_CONV + 48] = cw.reshape(4, 12, 128).transpose(2, 1, 0).reshape(128, 48)
        sm[l, :, O_ALOG:O_ALOG + 4] = np.asarray(inp["dn_a_log"][l], np.float32)[None, :]
        sm[l, :, O_DTB:O_DTB + 4] = np.asarray(inp["dn_dt_bias"][l], np.float32)[None, :]
        sm[l, :, O_DNG:O_DNG + 512] = np.tile(np.asarray(inp["dn_out_norm"][l], np.float32), 4)[None, :]
        sm[l, :, O_SGG:O_SGG + 256] = np.asarray(inp["sg_norm_gain"][l], np.float32)[None, :]
        sm[l, :, O_SGB:O_SGB + 256] = np.asarray(inp["sg_norm_bias"][l], np.float32)[None, :]
        sm[l, :, O_BSP:O_BSP + 4] = np.asarray(inp["sg_b_spatial"][l], np.float32).T
    return sm


_W_NAMES = ["ffn1_w_gate_up", "ffn2_w_gate_up", "ffn1_w_down", "ffn2_w_down", "mix_w_in", "mix_w_out",
            "xa_w_q", "xa_w_kv", "xa_w_o", "sg_w_spatial"]


def make_in_maps(inp, n_cores=N_CORES):
    shared = {n: np.ascontiguousarray(np.asarray(inp[n], np.float32)) for n in _W_NAMES}
    shared["small"] = pack_small(inp)
    shared["final_b"] = np.ascontiguousarray(np.broadcast_to(np.asarray(inp["final_norm"], np.float32)[None, :], (128, D)))
    shared["cst"] = make_consts()
    maps = []
    for c in range(n_cores):
        m = dict(shared)
        m["x"] = np.ascontiguousarray(np.asarray(inp["x"][c], np.float32))
        m["mem"] = np.ascontiguousarray(np.asarray(inp["mem"][c], np.float32))
        maps.append(m)
    return maps


_NC_CACHE = {}


def kernel(**inputs):
    if "nc" not in _NC_CACHE:
        _NC_CACHE["nc"] = build()
    nc = _NC_CACHE["nc"]
    maps = make_in_maps(inputs)
    res = run_bass_kernel_spmd(nc, maps, core_ids=list(range(N_CORES)))
    return np.stack([np.asarray(r["out"], np.float32) for r in res.results], axis=0)
```

```python
import numpy as np
import ml_dtypes
from contextlib import ExitStack, contextmanager
import concourse.bass as bass
import concourse.mybir as mybir
from concourse.bass_utils import run_bass_kernel_spmd

F32 = mybir.dt.float32
BF16 = mybir.dt.bfloat16
AF = mybir.ActivationFunctionType
ALU = mybir.AluOpType
AX = mybir.AxisListType

D = 1024
DFF = 2816
KC = 8
FC = 22
IN_COLS = 3336
EPS = 1e-6
NEG = -1.0e30
N_CORES = 4
SEQ = 8192
DEPTH = 2
MEMT = 256
VW = 72

O_G1, O_GM, O_GX, O_G2, O_GMEM = 0, 8, 16, 24, 32
O_CONV = 40
O_ALOG = 88
O_DTB = 92
O_BSP = 96
NSP = 100
O_DNG = 100
O_SGG = 612
O_SGB = 868
NS = 1124
C_ID, C_ONE, C_U, C_MB, C_ST, C_SWA = 0, 128, 256, 384, 512, 640
NCST = 640 + 24 * 128


class Buf:
    __slots__ = ("name", "w", "r", "psum")

    def __init__(self, name="", psum=False):
        self.name = name
        self.w = None
        self.r = {}
        self.psum = psum


class Q:
    def __init__(self, name):
        self.name = name
        self.prog = []
        self.sem = None
        self.count = 0
        self.waited = {}
        self.dsems = []
        self.dnext = 0


class _Rec:
    def __init__(self):
        self.call = None

    def __getattr__(self, name):
        def f(*a, **kw):
            self.call = (name, a, kw)
            return self
        return f


class K:
    def __init__(self, nc, n_dma_sems=(24, 12, 4)):
        self.nc = nc
        self.stacks = [ExitStack()]
        self.q = {n: Q(n) for n in ("pe", "act", "dve", "pool", "sp")}
        for n in ("pe", "act", "dve", "pool"):
            self.q[n].sem = self.stacks[0].enter_context(nc.semaphore("s_" + n))
        for qn, cnt in zip(("sp", "pool", "act"), n_dma_sems):
            for i in range(cnt):
                s = self.stacks[0].enter_context(nc.semaphore(f"d_{qn}_{i}"))
                self.q[qn].dsems.append([s, 0])
        self.nbuf = 0
        self.uid = 0

    def sb(self, name, shape, dt):
        self.uid += 1
        return self.stacks[-1].enter_context(self.nc.sbuf_tensor(f"{name}_{self.uid}", list(shape), dt))

    def ps(self, name, shape, dt=F32):
        self.uid += 1
        return self.stacks[-1].enter_context(self.nc.psum_tensor(f"{name}_{self.uid}", list(shape), dt))

    def buf(self, name=""):
        self.nbuf += 1
        return Buf(name or f"b{self.nbuf}")

    def bufs(self, n):
        return [self.buf() for _ in range(n)]

    def pbuf(self):
        self.nbuf += 1
        return Buf(f"p{self.nbuf}", psum=True)

    def pbufs(self, n):
        return [self.pbuf() for _ in range(n)]

    @contextmanager
    def scope(self):
        st = ExitStack()
        self.stacks.append(st)
        try:
            yield
        finally:
            self.barrier()
            self.stacks.pop()
            st.close()

    def barrier(self):
        deps = {}
        for n in ("pe", "act", "dve", "pool"):
            q = self.q[n]
            if q.count > 0:
                deps[id(q.sem)] = (q.sem, q.count)
        for n in ("sp", "pool", "act"):
            for s, v in self.q[n].dsems:
                if v > 0:
                    deps[id(s)] = (s, v)
        for n in ("pe", "act", "dve", "pool", "sp"):
            self._emit_waits(self.q[n], deps)

    def _deps(self, reads, writes, own=None):
        deps = {}

        def add(k, s, v):
            if k not in deps or deps[k][1] < v:
                deps[k] = (s, v)

        for b in reads:
            if b.w is not None:
                add(*b.w)
            if b.psum:
                for k, (s, v) in b.r.items():
                    if k != own:
                        add(k, s, v)
        for b in writes:
            if b.w is not None:
                add(*b.w)
            for k, (s, v) in b.r.items():
                add(k, s, v)
        return deps

    def _emit_waits(self, q, deps, skip_self=False):
        for k, (s, v) in deps.items():
            if skip_self and q.sem is not None and k == id(q.sem):
                continue
            if q.waited.get(k, 0) >= v:
                continue
            q.waited[k] = v
            q.prog.append(lambda e, s=s, v=v: e.wait_ge(s, v))

    def _mark(self, ev, reads, writes):
        k, s, v = ev
        for b in reads:
            b.r[k] = (s, v)
        for b in writes:
            b.w = ev
            b.r = {}

    def op(self, qn, fn, reads=(), writes=()):
        q = self.q[qn]
        deps = self._deps(reads, writes, own=id(q.sem))
        self._emit_waits(q, deps, skip_self=(qn == "pe"))
        q.count += 1
        c = q.count
        rec = _Rec()
        fn(rec)
        name, a, kw = rec.call
        q.prog.append(lambda e, name=name, a=a, kw=kw, s=q.sem: getattr(e, name)(*a, **kw).then_inc(s, 1))
        self._mark((id(q.sem), q.sem, c), reads, writes)

    def dma(self, qn, out, in_, reads=(), writes=(), **kw):
        q = self.q[qn]
        deps = self._deps(reads, writes)
        self._emit_waits(q, deps)
        slot = q.dsems[q.dnext]
        q.dnext = (q.dnext + 1) % len(q.dsems)
        s, v = slot
        if v > 0 and q.waited.get(id(s), 0) < v:
            q.waited[id(s)] = v
            q.prog.append(lambda e, s=s, v=v: e.wait_ge(s, v))
        slot[1] = v + 16
        q.prog.append(lambda e, s=s, out=out, in_=in_, kw=kw:
                      e.dma_start(out=out, in_=in_, **kw).then_inc(s, 16))
        self._mark((id(s), s, v + 16), reads, writes)

    def finish(self):
        self.barrier()
        nc = self.nc
        with nc.Block() as block:
            @block.tensor
            def _(e):
                for f in self.q["pe"].prog:
                    f(e)

            @block.scalar
            def _(e):
                for f in self.q["act"].prog:
                    f(e)

            @block.vector
            def _(e):
                for f in self.q["dve"].prog:
                    f(e)

            @block.gpsimd
            def _(e):
                for f in self.q["pool"].prog:
                    f(e)

            @block.sync
            def _(e):
                for f in self.q["sp"].prog:
                    f(e)
        while self.stacks:
            self.stacks.pop().close()


def run_pipelined(streams):
    its = [iter(src) for src, _ in streams]
    active = [[] for _ in streams]
    done = [False] * len(streams)
    while True:
        for si, (_, width) in enumerate(streams):
            while not done[si] and len(active[si]) < width:
                try:
                    active[si].append(next(its[si]))
                except StopIteration:
                    done[si] = True
            nxt = []
            for g in active[si]:
                try:
                    next(g)
                    nxt.append(g)
                except StopIteration:
                    pass
            active[si] = nxt
        if all(done) and not any(active):
            break

def build(S=SEQ, n_layers=DEPTH, stop_after=None, dbg=False):
    nc = bass.Bass("TRN2", target_bir_lowering=False)
    NT = S // 128
    NG = S // 512

    def din(name, shape, dt=F32):
        return nc.dram_tensor(name, list(shape), dt, kind="ExternalInput").ap()

    def dscr(name, shape, dt=F32):
        return nc.dram_tensor(name, list(shape), dt, kind=("ExternalOutput" if dbg else "Internal")).ap()

    x_in = din("x", [S, D])
    mem_in = din("mem", [MEMT, D])
    w_gu = [din("ffn1_w_gate_up", [DEPTH, D, 2 * DFF]), din("ffn2_w_gate_up", [DEPTH, D, 2 * DFF])]
    w_dn = [din("ffn1_w_down", [DEPTH, DFF, D]), din("ffn2_w_down", [DEPTH, DFF, D])]
    w_in = din("mix_w_in", [DEPTH, D, IN_COLS])
    w_out = din("mix_w_out", [DEPTH, D, D])
    w_q = din("xa_w_q", [DEPTH, D, D])
    w_kv = din("xa_w_kv", [DEPTH, D, 2 * D])
    w_o = din("xa_w_o", [DEPTH, D, D])
    w_sp = din("sg_w_spatial", [DEPTH, 4, 128, 128])
    small_in = din("small", [DEPTH, 128, NS])
    fin_in = din("final_b", [128, D])
    cst_in = din("cst", [128, NCST])
    out = nc.dram_tensor("out", [S, D], F32, kind="ExternalOutput").ap()

    hbuf = dscr("hbuf", [S, D])
    dnT = dscr("dnT", [1536, S])
    swaQT = dscr("swaQT", [256, S], BF16)
    swaKT = dscr("swaKT", [256, S], BF16)
    swaV = dscr("swaV", [S, 4 * VW], BF16)
    zbuf = dscr("zbuf", [S, 512])
    babuf = dscr("babuf", [S, 8])
    sgbuf = dscr("sgbuf", [S, 512])
    ndbuf = [dscr(f"nd{p}", [S, 4 * VW]) for p in range(3)]
    merged = dscr("merged", [S, D], BF16)

    k = K(nc)

    cst = k.sb("cst", [128, C_SWA], F32)
    small = [k.sb(f"small{l}", [128, NSP], F32) for l in range(DEPTH)]
    identb = k.sb("identb", [128, 128], BF16)
    onesb = k.sb("onesb", [128, 128], BF16)
    b_cst = k.buf("cst")
    b_small = k.bufs(DEPTH)
    b_identb = k.buf("identb")
    k.dma("sp", cst[:], cst_in[:, 0:C_SWA], writes=[b_cst])
    for l in range(DEPTH):
        k.dma("sp", small[l][:], small_in[l, :, 0:NSP], writes=[b_small[l]])
    k.op("dve", lambda e: e.tensor_copy(out=identb[:], in_=cst[:, C_ID:C_ID + 128]), reads=[b_cst], writes=[b_identb])
    k.op("dve", lambda e: e.tensor_copy(out=onesb[:], in_=cst[:, C_ONE:C_ONE + 128]), reads=[b_cst], writes=[b_identb])
    ident = cst[:, C_ID:C_ID + 128]
    ones = cst[:, C_ONE:C_ONE + 128]
    Umat = cst[:, C_U:C_U + 128]
    maskbT = cst[:, C_MB:C_MB + 128]
    strictT = cst[:, C_ST:C_ST + 128]

    rr = {"i": 0}

    def load_w(dst, dbufs, src2d, kcn, ncols, gain, gbuf, stage, sbufs):
        srcv = src2d.rearrange("(kc p) n -> p kc n", p=128)
        CH = stage[0].shape[1]
        for kc in range(kcn):
            for c0 in range(0, ncols, CH):
                w = min(CH, ncols - c0)
                i = rr["i"]
                rr["i"] += 1
                st, sbf = stage[i % len(stage)], sbufs[i % len(stage)]
                k.dma("sp", st[:, :w], srcv[:, kc, c0:c0 + w], writes=[sbf])
                o_ap = dst[:, kc, c0:c0 + w]
                i_ap = st[:, :w]
                rds = [sbf] + ([gbuf] if gain is not None else [])
                eng = ("act", "dve")[i % 2]
                if gain is not None:
                    g_ap = gain[:, kc:kc + 1]
                    if eng == "act":
                        k.op("act", lambda e, o_ap=o_ap, i_ap=i_ap, g_ap=g_ap:
                             e.activation(out=o_ap, in_=i_ap, func=AF.Copy, scale=g_ap),
                             reads=rds, writes=[dbufs[kc]])
                    else:
                        k.op(eng, lambda e, o_ap=o_ap, i_ap=i_ap, g_ap=g_ap:
                             e.tensor_scalar(out=o_ap, in0=i_ap, scalar1=g_ap, scalar2=None, op0=ALU.mult),
                             reads=rds, writes=[dbufs[kc]])
                else:
                    if eng == "act":
                        k.op("act", lambda e, o_ap=o_ap, i_ap=i_ap: e.activation(out=o_ap, in_=i_ap, func=AF.Copy),
                             reads=rds, writes=[dbufs[kc]])
                    else:
                        k.op(eng, lambda e, o_ap=o_ap, i_ap=i_ap: e.tensor_copy(out=o_ap, in_=i_ap),
                             reads=rds, writes=[dbufs[kc]])

    class NormT:
        def __init__(self):
            self.xs = [k.sb("xs", [128, D], BF16) for _ in range(2)]
            self.st = [k.sb("nst", [128, 4], F32) for _ in range(2)]
            self.tp = [k.ps("tp", [128, D], BF16) for _ in range(2)]
            self.b_xs = k.bufs(2)
            self.b_st = k.bufs(2)
            self.b_tp = k.pbufs(2)
            self.i = 0

        def run(self, hb, b_hb, dst3, b_dst, evac_eng=None):
            i = self.i % 2
            self.i += 1
            st, xs, tp = self.st[i], self.xs[i], self.tp[i]
            bst, bxs, btp = self.b_st[i], self.b_xs[i], self.b_tp[i]
            k.op("act", lambda e: e.activation(out=xs[:], in_=hb, func=AF.Square, accum_out=st[:, 0:1]),
                 reads=[b_hb], writes=[bxs, bst])
            k.op("act", lambda e: e.activation(out=st[:, 1:2], in_=st[:, 0:1], func=AF.Sqrt, bias=epsb[:, 0:1], scale=1.0 / D),
                 reads=[bst, b_eps], writes=[bst])
            k.op("dve", lambda e: e.reciprocal(out=st[:, 2:3], in_=st[:, 1:2]), reads=[bst], writes=[bst])
            k.op("dve", lambda e: e.tensor_scalar(out=xs[:], in0=hb, scalar1=st[:, 2:3], scalar2=None, op0=ALU.mult),
                 reads=[b_hb, bst], writes=[bxs])
            for kc in range(KC):
                k.op("pe", lambda e, kc=kc: e.transpose(out=tp[:, kc * 128:(kc + 1) * 128], in_=xs[:, kc * 128:(kc + 1) * 128],
                                                        identity=identb[:]),
                     reads=[bxs, b_identb], writes=[btp])
            eng = evac_eng or ("act" if self.i % 2 else "dve")
            src = tp[:].rearrange("p (a b) -> p a b", a=KC)
            if eng == "act":
                k.op("act", lambda e: e.activation(out=dst3, in_=src, func=AF.Copy), reads=[btp], writes=[b_dst])
            else:
                k.op("dve", lambda e: e.tensor_copy(out=dst3, in_=src), reads=[btp], writes=[b_dst])

    epsb = k.sb("epsb", [128, 1], F32)
    b_eps = k.buf()
    k.op("dve", lambda e: e.memset(epsb[:], EPS), writes=[b_eps])

    def phase_copy_x():
        with k.scope():
            t = [k.sb("cx", [128, D], F32) for _ in range(4)]
            bt = k.bufs(4)
            for i in range(NT):
                k.dma("sp", t[i % 4][:], x_in[i * 128:(i + 1) * 128, :], writes=[bt[i % 4]])
                k.dma("sp", hbuf[i * 128:(i + 1) * 128, :], t[i % 4][:], reads=[bt[i % 4]])

    def phase_ffn(l, which):
        gain_off = O_G1 if which == 0 else O_G2
        with k.scope():
            wgu = k.sb("wgu", [128, KC, 2 * DFF], BF16)
            wd = k.sb("wd", [128, FC, D], BF16)
            b_wgu = k.bufs(KC)
            b_wd = k.bufs(FC)
            stage = [k.sb("stg", [128, 704], F32) for _ in range(2)]
            b_stage = k.bufs(2)
            load_w(wgu, b_wgu, w_gu[which][l], KC, 2 * DFF, small[l][:, gain_off:gain_off + 8], b_small[l], stage, b_stage)
            load_w(wd, b_wd, w_dn[which][l], FC, D, None, None, stage, b_stage)
            nrm = NormT()
            hb = [k.sb("hb", [128, D], F32) for _ in range(4)]
            b_hb = k.bufs(4)
            nT = [k.sb("nT", [128, KC, 512], BF16) for _ in range(2)]
            b_nT = [k.bufs(4) for _ in range(2)]
            actT = k.sb("actT", [128, FC, 512], BF16)
            b_act = k.bufs(FC)
            sg = [k.sb("sg", [128, 512], F32) for _ in range(2)]
            b_sg = k.bufs(2)
            pg = [k.ps("pg", [128, 512]) for _ in range(2)]
            pu = [k.ps("pu", [128, 512]) for _ in range(2)]
            po = [k.ps("po", [128, 512]) for _ in range(2)]
            b_pg, b_pu, b_po = k.pbufs(2), k.pbufs(2), k.pbufs(2)
            for g in range(NG):
                gi = g % 2
                for t in range(4):
                    tok0 = g * 512 + t * 128
                    hbt, bhb = hb[t], b_hb[t]
                    k.dma("sp", hbt[:], hbuf[tok0:tok0 + 128, :], writes=[bhb])
                    nrm.run(hbt[:], bhb, nT[gi][:, :, t * 128:(t + 1) * 128], b_nT[gi][t])
                for fc in range(FC):
                    j = fc % 2
                    for kc in range(KC):
                        k.op("pe", lambda e, kc=kc, fc=fc, j=j: e.matmul(
                            pg[j][:], lhsT=wgu[:, kc, fc * 128:(fc + 1) * 128], rhs=nT[gi][:, kc, :],
                            start=(kc == 0), stop=(kc == KC - 1)),
                            reads=[b_wgu[kc]] + b_nT[gi], writes=[b_pg[j]])
                    for kc in range(KC):
                        k.op("pe", lambda e, kc=kc, fc=fc, j=j: e.matmul(
                            pu[j][:], lhsT=wgu[:, kc, DFF + fc * 128:DFF + (fc + 1) * 128], rhs=nT[gi][:, kc, :],
                            start=(kc == 0), stop=(kc == KC - 1)),
                            reads=[b_wgu[kc]] + b_nT[gi], writes=[b_pu[j]])
                    k.op("act", lambda e, j=j: e.activation(out=sg[j][:], in_=pg[j][:], func=AF.Silu),
                         reads=[b_pg[j]], writes=[b_sg[j]])
                    k.op("dve", lambda e, j=j, fc=fc: e.tensor_tensor(out=actT[:, fc, :], in0=pu[j][:], in1=sg[j][:], op=ALU.mult),
                         reads=[b_pu[j], b_sg[j]], writes=[b_act[fc]])
                for t in range(4):
                    tok0 = g * 512 + t * 128
                    hbt, bhb = hb[t], b_hb[t]
                    for nh in range(2):
                        j = (t * 2 + nh) % 2
                        for fc in range(FC):
                            k.op("pe", lambda e, fc=fc, t=t, nh=nh, j=j: e.matmul(
                                po[j][:], lhsT=actT[:, fc, t * 128:(t + 1) * 128], rhs=wd[:, fc, nh * 512:(nh + 1) * 512],
                                start=(fc == 0), stop=(fc == FC - 1)),
                                reads=[b_act[fc], b_wd[fc]], writes=[b_po[j]])
                        k.op("dve", lambda e, nh=nh, j=j, hbt=hbt: e.scalar_tensor_tensor(
                            out=hbt[:, nh * 512:(nh + 1) * 512], in0=po[j][:], scalar=0.5,
                            in1=hbt[:, nh * 512:(nh + 1) * 512], op0=ALU.mult, op1=ALU.add),
                            reads=[b_po[j], bhb], writes=[bhb])
                    k.dma("sp", hbuf[tok0:tok0 + 128, :], hbt[:], reads=[bhb])

    def phase_mixproj(l):
        with k.scope():
            win = k.sb("win", [128, KC, IN_COLS], BF16)
            b_win = k.bufs(KC)
            stage = [k.sb("stg", [128, 1112], F32) for _ in range(2)]
            b_stage = k.bufs(2)
            load_w(win, b_win, w_in[l], KC, IN_COLS, small[l][:, O_GM:O_GM + 8], b_small[l], stage, b_stage)
            nrm = NormT()
            hb = [k.sb("hb", [128, D], F32) for _ in range(4)]
            b_hb = k.bufs(4)
            nT = [k.sb("nT", [128, KC, 512], BF16) for _ in range(2)]
            b_nT = [k.bufs(4) for _ in range(2)]
            pf = [k.ps("pf", [128, 512]) for _ in range(2)]
            b_pf = k.pbufs(2)
            pt = [k.ps("pt", [128, 512]) for _ in range(4)]
            b_pt = k.pbufs(4)
            fo = [k.sb("fo", [128, 512], F32) for _ in range(2)]
            b_fo = k.bufs(2)
            fob = [k.sb("fob", [128, 512], BF16) for _ in range(2)]
            b_fob = k.bufs(2)
            zt = [k.sb("zt", [128, 512], F32) for _ in range(2)]
            b_zt = k.bufs(2)
            bat = [k.sb("bat", [128, 8], F32) for _ in range(2)]
            b_bat = k.bufs(2)
            vt = [k.sb("vt", [128, 4, VW], BF16) for _ in range(2)]
            b_vt = k.bufs(2)
            sgt = [k.sb("sgt", [128, 512], F32) for _ in range(2)]
            b_sgt = k.bufs(2)
            for j in range(2):
                k.op("dve", lambda e, j=j: e.memset(vt[j][:], 1.0), writes=[b_vt[j]])
            fch = [(c * 128, "dn", c) for c in range(12)] + [(2056 + j * 128, "q", j) for j in range(2)] + \
                  [(2312 + j * 128, "k", j) for j in range(2)]
            cnt = 0
            for g in range(NG):
                gi = g % 2
                for t in range(4):
                    tok0 = g * 512 + t * 128
                    k.dma("sp", hb[t][:], hbuf[tok0:tok0 + 128, :], writes=[b_hb[t]])
                    nrm.run(hb[t][:], b_hb[t], nT[gi][:, :, t * 128:(t + 1) * 128], b_nT[gi][t])
                for ci, (col0, kind, idx) in enumerate(fch):
                    if DBG.get("nofeat"):
                        break
                    j = ci % 2
                    for kc in range(KC):
                        k.op("pe", lambda e, kc=kc, col0=col0, j=j: e.matmul(
                            pf[j][:], lhsT=win[:, kc, col0:col0 + 128], rhs=nT[gi][:, kc, :],
                            start=(kc == 0), stop=(kc == KC - 1)), reads=[b_win[kc]] + b_nT[gi], writes=[b_pf[j]])
                    if kind == "dn":
                        if ci % 4 < 2:
                            k.op("act", lambda e, j=j: e.activation(out=fo[j][:], in_=pf[j][:], func=AF.Copy),
                                 reads=[b_pf[j]], writes=[b_fo[j]])
                        else:
                            k.op("dve", lambda e, j=j: e.tensor_copy(out=fo[j][:], in_=pf[j][:]),
                                 reads=[b_pf[j]], writes=[b_fo[j]])
                        k.dma("sp", dnT[idx * 128:(idx + 1) * 128, g * 512:(g + 1) * 512], fo[j][:], reads=[b_fo[j]])
                    else:
                        k.op("dve", lambda e, j=j: e.tensor_copy(out=fob[j][:], in_=pf[j][:]),
                             reads=[b_pf[j]], writes=[b_fob[j]])
                        dst = swaQT if kind == "q" else swaKT
                        k.dma("sp", dst[idx * 128:(idx + 1) * 128, g * 512:(g + 1) * 512], fob[j][:], reads=[b_fob[j]])
                for t in range(4):
                    if DBG.get("notok"):
                        break
                    tok0 = g * 512 + t * 128
                    j = cnt % 2
                    cnt += 1
                    groups = [(1536, 512), (2048, 8), (2568, 512), (3080, 256)]
                    for gi2, (c0, w) in enumerate(groups):
                        if DBG.get("noba") and gi2 == 1:
                            continue
                        for kc in range(KC):
                            k.op("pe", lambda e, kc=kc, c0=c0, w=w, gi2=gi2, t=t: e.matmul(
                                pt[gi2][:, 0:w], lhsT=nT[gi][:, kc, t * 128:(t + 1) * 128], rhs=win[:, kc, c0:c0 + w],
                                start=(kc == 0), stop=(kc == KC - 1)), reads=[b_win[kc], b_nT[gi][t]], writes=[b_pt[gi2]])
                    k.op("act", lambda e, j=j: e.activation(out=zt[j][:], in_=pt[0][:], func=AF.Copy),
                         reads=[b_pt[0]], writes=[b_zt[j]])
                    k.dma("sp", zbuf[tok0:tok0 + 128, :], zt[j][:], reads=[b_zt[j]])
                    if not DBG.get("noba"):
                        k.op("dve", lambda e, j=j: e.tensor_copy(out=bat[j][:], in_=pt[1][:, 0:8]),
                             reads=[b_pt[1]], writes=[b_bat[j]])
                        k.dma("sp", babuf[tok0:tok0 + 128, :], bat[j][:], reads=[b_bat[j]])
                    if not DBG.get("novt"):
                        k.op("dve", lambda e, j=j: e.tensor_copy(out=vt[j][:, :, 0:64],
                                                                 in_=pt[2][:, 0:256].rearrange("p (h d) -> p h d", h=4)),
                             reads=[b_pt[2]], writes=[b_vt[j]])
                        k.dma("sp", swaV[tok0:tok0 + 128, :], vt[j][:].rearrange("p h d -> p (h d)"), reads=[b_vt[j]])
                    k.op("act", lambda e, j=j: e.activation(out=sgt[j][:, 0:256], in_=pt[2][:, 256:512], func=AF.Copy),
                         reads=[b_pt[2]], writes=[b_sgt[j]])
                    k.op("act", lambda e, j=j: e.activation(out=sgt[j][:, 256:512], in_=pt[3][:, 0:256], func=AF.Copy),
                         reads=[b_pt[3]], writes=[b_sgt[j]])
                    k.dma("sp", sgbuf[tok0:tok0 + 128, :], sgt[j][:], reads=[b_sgt[j]])

    def phase_dn(l):
        sm = small[l]
        bsm = b_small[l]
        with k.scope():
            ba = k.sb("ba", [128, NT, 8], F32)
            b_ba = k.buf()
            bav = babuf.rearrange("(t p) c -> p t c", p=128)
            for t0 in range(0, NT, 8):
                t1 = min(NT, t0 + 8)
                k.dma("sp", ba[:, t0:t1, :], bav[:, t0:t1, :], writes=[b_ba])
            dng = k.sb("dng", [128, 512], F32)
            b_dng = k.buf()
            k.dma("sp", dng[:], small_in[l, :, O_DNG:O_DNG + 512], writes=[b_dng])
            G = k.sb("G", [128, NT, 4], F32)
            BETA = k.sb("BETA", [128, NT, 4], F32)
            NBETA = k.sb("NBETA", [128, NT, 4], F32)
            GC = k.sb("GC", [128, NT, 4], F32)
            EGC = k.sb("EGC", [128, NT, 4], F32)
            ETAIL = k.sb("ETAIL", [128, NT, 4], F32)
            EGL = k.sb("EGL", [128, NT, 4], F32)
            tA = k.sb("tA", [128, NT], F32)
            tB = k.sb("tB", [128, NT], F32)
            tC = k.sb("tC", [128, NT], F32)
            na = k.sb("na", [128, 4], F32)
            b_g = k.buf()
            b_tmp = k.buf()
            b_na = k.buf()
            k.op("act", lambda e: e.activation(out=na[:], in_=sm[:, O_ALOG:O_ALOG + 4], func=AF.Exp), reads=[bsm], writes=[b_na])
            k.op("dve", lambda e: e.tensor_scalar(out=na[:], in0=na[:], scalar1=-1.0, scalar2=None, op0=ALU.mult),
                 reads=[b_na], writes=[b_na])
            for h in range(4):
                k.op("dve", lambda e, h=h: e.tensor_scalar(out=tA[:], in0=ba[:, :, 4 + h], scalar1=sm[:, O_DTB + h:O_DTB + h + 1],
                                                           scalar2=None, op0=ALU.add), reads=[b_ba, bsm], writes=[b_tmp])
                k.op("dve", lambda e: e.scalar_tensor_tensor(out=tB[:], in0=tA[:], scalar=-1.0, in1=tA[:], op0=ALU.mult, op1=ALU.min),
                     reads=[b_tmp], writes=[b_tmp])
                k.op("act", lambda e: e.activation(out=tB[:], in_=tB[:], func=AF.Exp), reads=[b_tmp], writes=[b_tmp])
                k.op("act", lambda e: e.activation(out=tB[:], in_=tB[:], func=AF.Ln, bias=ones[:, 0:1], scale=1.0),
                     reads=[b_tmp, b_cst], writes=[b_tmp])
                k.op("dve", lambda e: e.scalar_tensor_tensor(out=tC[:], in0=tA[:], scalar=0.0, in1=tB[:], op0=ALU.max, op1=ALU.add),
                     reads=[b_tmp], writes=[b_tmp])
                k.op("dve", lambda e, h=h: e.tensor_scalar(out=G[:, :, h], in0=tC[:], scalar1=na[:, h:h + 1], scalar2=None, op0=ALU.mult),
                     reads=[b_tmp, b_na], writes=[b_g])
                k.op("act", lambda e, h=h: e.activation(out=tA[:], in_=ba[:, :, h], func=AF.Exp, scale=-1.0),
                     reads=[b_ba], writes=[b_tmp])
                k.op("dve", lambda e: e.tensor_scalar(out=tA[:], in0=tA[:], scalar1=1.0, scalar2=None, op0=ALU.add),
                     reads=[b_tmp], writes=[b_tmp])
                k.op("dve", lambda e, h=h: e.reciprocal(out=BETA[:, :, h], in_=tA[:]), reads=[b_tmp], writes=[b_g])
                k.op("dve", lambda e, h=h: e.tensor_scalar(out=NBETA[:, :, h], in0=BETA[:, :, h], scalar1=-1.0, scalar2=None, op0=ALU.mult),
                     reads=[b_g], writes=[b_g])
            PS = [k.ps("pss", [128, 512]) for _ in range(2)]
            b_PS = k.pbufs(2)
            pgc, pgl = PS
            b_pgc, b_pgl = b_PS
            Gf = G[:].rearrange("p t h -> p (t h)")
            k.op("pe", lambda e: e.matmul(pgc[:, 0:NT * 4], lhsT=Umat, rhs=Gf, start=True, stop=True), reads=[b_cst, b_g], writes=[b_pgc])
            k.op("pe", lambda e: e.matmul(pgl[:, 0:NT * 4], lhsT=ones, rhs=Gf, start=True, stop=True), reads=[b_cst, b_g], writes=[b_pgl])
            fl = lambda T: T[:].rearrange("p t h -> p (t h)")
            k.op("dve", lambda e: e.tensor_copy(out=fl(GC), in_=pgc[:, 0:NT * 4]), reads=[b_pgc], writes=[b_g])
            k.op("act", lambda e: e.activation(out=fl(EGC), in_=pgc[:, 0:NT * 4], func=AF.Exp), reads=[b_pgc], writes=[b_g])
            k.op("act", lambda e: e.activation(out=fl(EGL), in_=pgl[:, 0:NT * 4], func=AF.Exp), reads=[b_pgl], writes=[b_g])
            k.op("dve", lambda e: e.tensor_tensor(out=fl(ETAIL), in0=pgl[:, 0:NT * 4], in1=fl(GC), op=ALU.subtract),
                 reads=[b_pgl, b_g], writes=[b_g])
            k.op("act", lambda e: e.activation(out=fl(ETAIL), in_=fl(ETAIL), func=AF.Exp), reads=[b_g], writes=[b_g])

            x12 = k.sb("x12", [128, 12, 515], F32)
            b_x12 = k.buf()
            y12 = [k.sb("y12", [128, 12, 512], F32) for _ in range(2)]
            b_y12 = [k.bufs(12) for _ in range(2)]
            cacc = [k.sb("cacc", [128, 512], F32) for _ in range(2)]
            b_cacc = k.bufs(2)
            sqt = [k.sb("sqt", [128, 512], F32) for _ in range(2)]
            b_sqt = k.bufs(2)
            ident4 = k.sb("ident4", [128, 4, 128], F32)
            strict4 = k.sb("strict4", [128, 4, 128], F32)
            b_c4 = k.buf()
            for h in range(4):
                k.op("dve", lambda e: e.tensor_copy(out=ident4[:, h, :], in_=ident), reads=[b_cst], writes=[b_c4])
                k.op("dve", lambda e: e.tensor_copy(out=strict4[:, h, :], in_=strictT), reads=[b_cst], writes=[b_c4])

            def mk(name, n):
                return [k.sb(name, [128, 4, 128], F32) for _ in range(n)], k.bufs(n)

            vtok, b_vtok = mk("vtok", 4)
            ktail, b_ktail = mk("ktail", 4)
            KTc, b_KTc = mk("KTc", 4)
            Pm, b_Pm = mk("Pm", 4)
            aqk, b_aqk = mk("aqk", 4)
            QdT, b_QdT = mk("QdT", 4)
            GU, b_GU = mk("GU", 2)
            tmpd, b_tmpd = mk("tmpd", 2)
            decT, b_decT = mk("decT", 2)
            EGB, b_EGB = mk("EGB", 2)
            N1, b_N1 = mk("N1", 2)
            Ya, b_Ya = mk("Ya", 2)
            YTa, b_YTa = mk("YTa", 2)
            Yb, b_Yb = mk("Yb", 2)
            YTb, b_YTb = mk("YTb", 2)
            rneg, b_rneg = mk("rneg", 2)
            vnew, b_vnew = mk("vnew", 2)
            Sst = k.sb("Sst", [128, 4, 128], F32)
            b_S = k.buf()
            k.op("pool", lambda e: e.memset(Sst[:], 0.0), writes=[b_S])
            PBr = [[k.ps("pbr", [128, 512]) for _ in range(3)] for _ in range(2)]
            b_PBr = [k.pbufs(3) for _ in range(2)]
            zt = [k.sb("zt", [128, 512], F32) for _ in range(2)]
            b_zt = k.bufs(2)
            ost = [k.sb("ost", [128, 8], F32) for _ in range(2)]
            b_ost = k.bufs(2)
            ojunk = k.sb("ojunk", [128, 128], F32)
            b_ojunk = k.buf()
            osb = [k.sb("osb", [128, 512], F32) for _ in range(2)]
            b_osb = k.bufs(2)
            mo = [k.sb("mo", [128, 512], BF16) for _ in range(2)]
            b_mo = k.bufs(2)
            H4 = range(4)

            def hs(h):
                return slice(h * 128, (h + 1) * 128)

            def v4(ps_tile):
                return ps_tile[:].rearrange("p (h c) -> p h c", h=4)

            def group_prep(g):
                gi = g % 2
                xg, bxg = x12, b_x12
                src = dnT.rearrange("(c p) s -> p c s", p=128)
                if g == 0:
                    k.op("pool", lambda e: e.memset(xg[:, :, 0:3], 0.0), writes=[bxg])
                    k.dma("sp", xg[:, :, 3:515], src[:, :, 0:512], writes=[bxg])
                else:
                    k.dma("sp", xg[:, :, :], src[:, :, g * 512 - 3:g * 512 + 512], writes=[bxg])
                for c in range(12):
                    j = c % 2
                    wv = lambda tap: sm[:, O_CONV + c * 4 + tap:O_CONV + c * 4 + tap + 1]
                    k.op("act", lambda e: e.activation(out=cacc[j][:], in_=xg[:, c, 0:512], func=AF.Copy, scale=wv(0)),
                         reads=[bxg, bsm], writes=[b_cacc[j]])
                    for tap in range(1, 4):
                        k.op("dve", lambda e: e.scalar_tensor_tensor(
                            out=cacc[j][:], in0=xg[:, c, tap:tap + 512], scalar=wv(tap), in1=cacc[j][:], op0=ALU.mult, op1=ALU.add),
                            reads=[bxg, bsm, b_cacc[j]], writes=[b_cacc[j]])
                    k.op("act", lambda e: e.activation(out=y12[gi][:, c, :], in_=cacc[j][:], func=AF.Silu),
                         reads=[b_cacc[j]], writes=[b_y12[gi][c]])
                    yield
                    if c < 8:
                        k.op("act", lambda e: e.activation(out=sqt[j][:], in_=y12[gi][:, c, :], func=AF.Square),
                             reads=[b_y12[gi][c]], writes=[b_sqt[j]])
                        k.op("pe", lambda e: e.matmul(PS[1][:], lhsT=ones, rhs=sqt[j][:], start=True, stop=True),
                             reads=[b_sqt[j], b_cst], writes=[b_PS[1]])
                        k.op("act", lambda e: e.activation(out=sqt[j][:], in_=PS[1][:], func=AF.Sqrt, bias=epsb[:, 0:1], scale=1.0),
                             reads=[b_PS[1], b_eps], writes=[b_sqt[j]])
                        k.op("dve", lambda e: e.reciprocal(out=sqt[j][:], in_=sqt[j][:]), reads=[b_sqt[j]], writes=[b_sqt[j]])
                        sc = (128.0 ** -0.5) if c < 4 else 1.0
                        k.op("dve", lambda e: e.scalar_tensor_tensor(
                            out=y12[gi][:, c, :], in0=y12[gi][:, c, :], scalar=sc, in1=sqt[j][:], op0=ALU.mult, op1=ALU.mult),
                            reads=[b_y12[gi][c], b_sqt[j]], writes=[b_y12[gi][c]])
                        yield

            def tile_prep(T):
                g, t = divmod(T, 4)
                gi = g % 2
                s_ = T % 4
                r = T % 2
                X, Y, Z = PBr[r]
                bX, bY, bZ = b_PBr[r]
                ts = slice(t * 128, (t + 1) * 128)
                QT = lambda h: y12[gi][:, h, ts]
                KT = lambda h: y12[gi][:, 4 + h, ts]
                VT = lambda h: y12[gi][:, 8 + h, ts]
                bQ = [b_y12[gi][h] for h in H4]
                bK = [b_y12[gi][4 + h] for h in H4]
                bV = [b_y12[gi][8 + h] for h in H4]
                col = lambda A, h: A[:, T, h:h + 1]
                for h in H4:
                    k.op("pe", lambda e: e.transpose(out=X[:, hs(h)], in_=VT(h), identity=ident), reads=[bV[h], b_cst], writes=[bX])
                for h in H4:
                    k.op("pe", lambda e: e.transpose(out=Y[:, hs(h)], in_=KT(h), identity=ident), reads=[bK[h], b_cst], writes=[bY])
                k.op("act", lambda e: e.activation(out=vtok[s_][:], in_=v4(X), func=AF.Copy), reads=[bX], writes=[b_vtok[s_]])
                for h in H4:
                    k.op("dve", lambda e: e.tensor_scalar(out=ktail[s_][:, h, :], in0=Y[:, hs(h)], scalar1=col(ETAIL, h), scalar2=None, op0=ALU.mult),
                         reads=[bY, b_g], writes=[b_ktail[s_]])
                k.op("pool", lambda e: e.tensor_copy(out=KTc[s_][:], in_=y12[gi][:, 4:8, ts]), reads=bK, writes=[b_KTc[s_]])
                for h in H4:
                    k.op("dve", lambda e: e.tensor_scalar(out=GU[r][:, h, :], in0=Umat, scalar1=col(G, h), scalar2=None, op0=ALU.mult),
                         reads=[b_cst, b_g], writes=[b_GU[r]])
                yield
                for h in H4:
                    k.op("pe", lambda e: e.matmul(X[:, hs(h)], lhsT=ones, rhs=GU[r][:, h, :], start=True, stop=True),
                         reads=[b_cst, b_GU[r]], writes=[bX])
                for h in H4:
                    k.op("pe", lambda e: e.matmul(Y[:, hs(h)], lhsT=KT(h), rhs=KT(h), start=True, stop=True), reads=[bK[h]], writes=[bY])
                for h in H4:
                    k.op("dve", lambda e: e.scalar_tensor_tensor(out=tmpd[r][:, h, :], in0=X[:, hs(h)], scalar=col(GC, h), in1=maskbT,
                                                                 op0=ALU.subtract, op1=ALU.add),
                         reads=[bX, b_g, b_cst], writes=[b_tmpd[r]])
                k.op("act", lambda e: e.activation(out=EGB[r][:], in_=v4(X), func=AF.Exp), reads=[bX], writes=[b_EGB[r]])
                k.op("act", lambda e: e.activation(out=decT[r][:], in_=tmpd[r][:], func=AF.Exp), reads=[b_tmpd[r]], writes=[b_decT[r]])
                yield
                for h in H4:
                    k.op("dve", lambda e: e.scalar_tensor_tensor(out=N1[r][:, h, :], in0=Y[:, hs(h)], scalar=col(BETA, h), in1=decT[r][:, h, :],
                                                                 op0=ALU.mult, op1=ALU.mult),
                         reads=[bY, b_g, b_decT[r]], writes=[b_N1[r]])
                k.op("pool", lambda e: e.tensor_tensor(out=Ya[r][:], in0=N1[r][:], in1=strict4[:], op=ALU.mult),
                     reads=[b_N1[r], b_c4], writes=[b_Ya[r]])
                k.op("pool", lambda e: e.tensor_tensor(out=Pm[s_][:], in0=ident4[:], in1=Ya[r][:], op=ALU.subtract),
                     reads=[b_Ya[r], b_c4], writes=[b_Pm[s_]])
                k.op("pool", lambda e: e.tensor_tensor(out=QdT[s_][:], in0=y12[gi][:, 0:4, ts], in1=EGB[r][:], op=ALU.mult),
                     reads=bQ + [b_EGB[r]], writes=[b_QdT[s_]])
                yield
                for h in H4:
                    k.op("pe", lambda e: e.transpose(out=X[:, hs(h)], in_=Ya[r][:, h, :], identity=ident), reads=[b_Ya[r], b_cst], writes=[bX])
                for h in H4:
                    k.op("pe", lambda e: e.matmul(Y[:, hs(h)], lhsT=KT(h), rhs=QT(h), start=True, stop=True), reads=[bK[h], bQ[h]], writes=[bY])
                k.op("act", lambda e: e.activation(out=YTa[r][:], in_=v4(X), func=AF.Copy), reads=[bX], writes=[b_YTa[r]])
                k.op("dve", lambda e: e.tensor_tensor(out=aqk[s_][:], in0=v4(Y), in1=decT[r][:], op=ALU.mult),
                     reads=[bY, b_decT[r]], writes=[b_aqk[s_]])
                yield
                Yc, YTc, bYc, bYTc = Ya[r], YTa[r], b_Ya[r], b_YTa[r]
                Yn, YTn, bYn, bYTn = Yb[r], YTb[r], b_Yb[r], b_YTb[r]
                for lev in range(1, 7):
                    if lev < 6:
                        for h in H4:
                            k.op("pe", lambda e: e.matmul(X[:, hs(h)], lhsT=YTc[:, h, :], rhs=Yc[:, h, :], start=True, stop=True),
                                 reads=[bYc, bYTc], writes=[bX])
                    for h in H4:
                        k.op("pe", lambda e: e.matmul(Y[:, hs(h)], lhsT=Yc[:, h, :], rhs=YTc[:, h, :], start=True, stop=True),
                             reads=[bYc, bYTc], writes=[bY])
                    if lev > 1:
                        for h in H4:
                            k.op("pe", lambda e: e.matmul(Z[:, hs(h)], lhsT=YTc[:, h, :], rhs=Pm[s_][:, h, :], start=True, stop=True),
                                 reads=[bYTc, b_Pm[s_]], writes=[bZ])
                    if lev < 6:
                        k.op("act", lambda e: e.activation(out=Yn[:], in_=v4(X), func=AF.Copy), reads=[bX], writes=[bYn])
                    k.op("dve", lambda e: e.tensor_copy(out=YTn[:], in_=v4(Y)), reads=[bY], writes=[bYTn])
                    if lev > 1:
                        k.op("dve", lambda e: e.tensor_tensor(out=Pm[s_][:], in0=v4(Z), in1=Pm[s_][:], op=ALU.add),
                             reads=[bZ, b_Pm[s_]], writes=[b_Pm[s_]])
                    yield
                    Yc, YTc, bYc, bYTc, Yn, YTn, bYn, bYTn = Yn, YTn, bYn, bYTn, Yc, YTc, bYc, bYTc
                for h in H4:
                    k.op("pe", lambda e: e.matmul(Z[:, hs(h)], lhsT=YTc[:, h, :], rhs=Pm[s_][:, h, :], start=True, stop=True),
                         reads=[bYTc, b_Pm[s_]], writes=[bZ])
                k.op("dve", lambda e: e.tensor_tensor(out=Pm[s_][:], in0=v4(Z), in1=Pm[s_][:], op=ALU.add),
                     reads=[bZ, b_Pm[s_]], writes=[b_Pm[s_]])
                yield

            def tile_scan(T):
                s_ = T % 4
                p = T % 2
                SA, SB = PS
                bSA, bSB = b_PS
                col = lambda A, h: A[:, T, h:h + 1]
                tok0 = T * 128
                k.dma("sp", zt[p][:], zbuf[tok0:tok0 + 128, :], writes=[b_zt[p]])
                for h in H4:
                    k.op("pe", lambda e: e.matmul(SA[:, hs(h)], lhsT=KTc[s_][:, h, :], rhs=Sst[:, h, :], start=True, stop=True),
                         reads=[b_KTc[s_], b_S], writes=[bSA])
                for h in H4:
                    k.op("dve", lambda e: e.scalar_tensor_tensor(out=rneg[p][:, h, :], in0=SA[:, hs(h)], scalar=col(EGC, h), in1=vtok[s_][:, h, :],
                                                                 op0=ALU.mult, op1=ALU.subtract),
                         reads=[bSA, b_g, b_vtok[s_]], writes=[b_rneg[p]])
                yield
                for h in H4:
                    k.op("pe", lambda e: e.matmul(SA[:, hs(h)], lhsT=Pm[s_][:, h, :], rhs=rneg[p][:, h, :], start=True, stop=True),
                         reads=[b_Pm[s_], b_rneg[p]], writes=[bSA])
                for h in H4:
                    k.op("act", lambda e: e.activation(out=vnew[p][:, h, :], in_=SA[:, hs(h)], func=AF.Copy, scale=col(NBETA, h)),
                         reads=[bSA, b_g], writes=[b_vnew[p]])
                yield
                for h in H4:
                    k.op("pe", lambda e: e.matmul(SB[:, hs(h)], lhsT=QdT[s_][:, h, :], rhs=Sst[:, h, :], start=True, stop=False),
                         reads=[b_QdT[s_], b_S], writes=[bSB])
                    k.op("pe", lambda e: e.matmul(SB[:, hs(h)], lhsT=aqk[s_][:, h, :], rhs=vnew[p][:, h, :], start=False, stop=True),
                         reads=[b_aqk[s_], b_vnew[p]], writes=[bSB])
                k.op("act", lambda e: e.activation(out=osb[p][:], in_=SB[:], func=AF.Copy), reads=[bSB], writes=[b_osb[p]])
                for h in H4:
                    k.op("pe", lambda e: e.matmul(SA[:, hs(h)], lhsT=ktail[s_][:, h, :], rhs=vnew[p][:, h, :], start=True, stop=True),
                         reads=[b_ktail[s_], b_vnew[p]], writes=[bSA])
                for h in H4:
                    k.op("dve", lambda e: e.scalar_tensor_tensor(out=Sst[:, h, :], in0=Sst[:, h, :], scalar=col(EGL, h), in1=SA[:, hs(h)],
                                                                 op0=ALU.mult, op1=ALU.add),
                         reads=[bSA, b_g, b_S], writes=[b_S])
                yield
                k.op("act", lambda e: e.activation(out=zt[p][:], in_=zt[p][:], func=AF.Silu), reads=[b_zt[p]], writes=[b_zt[p]])
                k.op("pool", lambda e: e.tensor_tensor(out=zt[p][:], in0=zt[p][:], in1=dng[:], op=ALU.mult),
                     reads=[b_zt[p], b_dng], writes=[b_zt[p]])
                for h in H4:
                    k.op("act", lambda e: e.activation(out=ojunk[:], in_=osb[p][:, hs(h)], func=AF.Square, accum_out=ost[p][:, h:h + 1]),
                         reads=[b_osb[p]], writes=[b_ojunk, b_ost[p]])
                k.op("act", lambda e: e.activation(out=ost[p][:, 4:8], in_=ost[p][:, 0:4], func=AF.Sqrt, bias=epsb[:, 0:1], scale=1.0 / 128),
                     reads=[b_ost[p], b_eps], writes=[b_ost[p]])
                k.op("dve", lambda e: e.reciprocal(out=ost[p][:, 4:8], in_=ost[p][:, 4:8]), reads=[b_ost[p]], writes=[b_ost[p]])
                for h in H4:
                    k.op("dve", lambda e: e.scalar_tensor_tensor(out=mo[p][:, hs(h)], in0=osb[p][:, hs(h)], scalar=ost[p][:, 4 + h:5 + h],
                                                                 in1=zt[p][:, hs(h)], op0=ALU.mult, op1=ALU.mult),
                         reads=[b_osb[p], b_ost[p], b_zt[p]], writes=[b_mo[p]])
                k.dma("sp", merged[tok0:tok0 + 128, 0:512], mo[p][:], reads=[b_mo[p]])
                yield

            def chain(*gens):
                for gnr in gens:
                    yield from gnr

            def drive(gens):
                alive = list(gens)
                while alive:
                    nxt = []
                    for gnr in alive:
                        try:
                            next(gnr)
                            nxt.append(gnr)
                        except StopIteration:
                            pass
                    alive = nxt

            drive([group_prep(0)])
            st0 = [tile_prep(0), tile_prep(1)]
            if NG > 1:
                st0.append(group_prep(1))
            drive(st0)
            NP = NT // 2
            for pstep in range(NP):
                gens = [chain(tile_scan(2 * pstep), tile_scan(2 * pstep + 1))]
                if 2 * pstep + 2 < NT:
                    gens.append(tile_prep(2 * pstep + 2))
                    gens.append(tile_prep(2 * pstep + 3))
                if pstep % 2 == 0 and pstep > 0:
                    gq = pstep // 2 + 1
                    if gq < NG:
                        gens.append(group_prep(gq))
                drive(gens)

    def phase_swa(l, extra_streams=None, do_merge=True):
        with k.scope():
            QTs = k.sb("QTs", [128, 2, S], BF16)
            KTz = k.sb("KTz", [128, 4, S], BF16)
            b_qk = k.buf()
            qv = swaQT.rearrange("(j p) s -> p j s", p=128)
            CH = min(S, 2048)
            for h in range(4):
                k.op("dve" if h % 2 else "pool", lambda e: e.memset(KTz[:, h, :], 0.0), writes=[b_qk])
            for c0 in range(0, S, CH):
                k.dma("sp", QTs[:, :, c0:c0 + CH], qv[:, :, c0:c0 + CH], writes=[b_qk])
                for h in range(4):
                    r0 = 64 * (h % 2)
                    k.dma("sp", KTz[r0:r0 + 64, h, c0:c0 + CH], swaKT[h * 64:(h + 1) * 64, c0:c0 + CH], writes=[b_qk])
            swab = k.sb("swab", [128, 24 * 128], F32)
            b_swab = k.buf()
            k.dma("sp", swab[:], cst_in[:, C_SWA:C_SWA + 24 * 128], writes=[b_swab])
            Vb = [k.sb("Vb", [128, 4 * VW], BF16) for _ in range(4)]
            b_Vb = k.bufs(4)
            pS = [[k.ps("pS", [128, 512]) for _ in range(2)] for _ in range(2)]
            b_pS = [k.pbufs(2) for _ in range(2)]
            pO = [k.ps("pO", [128, 512]) for _ in range(2)]
            b_pO = k.pbufs(2)
            tmp = [[k.sb("stmp", [128, 512], F32) for _ in range(2)] for _ in range(2)]
            b_tmp = [k.bufs(2) for _ in range(2)]
            PT = [[k.sb("PT", [128, 512], BF16) for _ in range(2)] for _ in range(2)]
            b_PT = [k.bufs(2) for _ in range(2)]
            ot = [k.sb("ot", [128, 4 * VW], F32) for _ in range(2)]
            b_ot = k.bufs(2)
            def swa_iter(p, dil, r, n, it, vi, prev_v):
                sl = lambda n_: slice(r + dil * 128 * n_, r + dil * 128 * n_ + dil * 127 + 1, dil)
                ii = it % 2
                k.dma("sp", Vb[vi][:], swaV[sl(n), :], writes=[b_Vb[vi]])
                blks = ([0] if n > 0 else []) + [1]
                for blk in blks:
                    ks = sl(n - 1) if blk == 0 else sl(n)
                    for h in range(4):
                        k.op("pe", lambda e: e.matmul(pS[ii][blk][:, h * 128:(h + 1) * 128],
                                                      lhsT=KTz[:, h, ks], rhs=QTs[:, h // 2, sl(n)],
                                                      start=True, stop=True),
                             reads=[b_qk], writes=[b_pS[ii][blk]])
                yield
                for blk in blks:
                    bo = ((p * 2 + blk) * 4) * 128
                    k.op("dve", lambda e: e.scalar_tensor_tensor(out=tmp[ii][blk][:], in0=pS[ii][blk][:], scalar=0.125,
                                                                 in1=swab[:, bo:bo + 512], op0=ALU.mult, op1=ALU.add),
                         reads=[b_pS[ii][blk], b_swab], writes=[b_tmp[ii][blk]])
                yield
                for blk in blks:
                    k.op("act", lambda e: e.activation(out=PT[ii][blk][:], in_=tmp[ii][blk][:], func=AF.Exp),
                         reads=[b_tmp[ii][blk]], writes=[b_PT[ii][blk]])
                yield
                for h in range(4):
                    if n > 0:
                        k.op("pe", lambda e: e.matmul(pO[ii][:, h * VW:(h + 1) * VW], lhsT=PT[ii][0][:, h * 128:(h + 1) * 128],
                                                      rhs=Vb[prev_v][:, h * VW:(h + 1) * VW], start=True, stop=False),
                             reads=[b_PT[ii][0], b_Vb[prev_v]], writes=[b_pO[ii]])
                    k.op("pe", lambda e: e.matmul(pO[ii][:, h * VW:(h + 1) * VW], lhsT=PT[ii][1][:, h * 128:(h + 1) * 128],
                                                  rhs=Vb[vi][:, h * VW:(h + 1) * VW], start=(n == 0), stop=True),
                         reads=[b_PT[ii][1], b_Vb[vi]], writes=[b_pO[ii]])
                yield
                k.op("act", lambda e: e.activation(out=ot[ii][:], in_=pO[ii][:, 0:4 * VW], func=AF.Copy),
                     reads=[b_pO[ii]], writes=[b_ot[ii]])
                yield
                k.dma("sp", ndbuf[p][sl(n), :], ot[ii][:], reads=[b_ot[ii]])
                yield

            def swa_all():
                it = 0
                for p, (win, dil) in enumerate(SWA_PATTERNS):
                    nblk = (S // dil) // 128
                    for r in range(dil):
                        prev_v = None
                        for n in range(nblk):
                            vi = it % 4
                            yield swa_iter(p, dil, r, n, it, vi, prev_v)
                            prev_v = vi
                            it += 1

            streams = [(swa_all(), 2)]
            if extra_streams:
                streams += extra_streams
            run_pipelined(streams)
        if not do_merge:
            return
        with k.scope():
            nd = [[k.sb("nd", [128, 4 * VW], F32) for _ in range(3)] for _ in range(2)]
            b_nd = [k.bufs(3) for _ in range(2)]
            rd = [k.sb("rd", [128, 4], F32) for _ in range(2)]
            b_rd = k.bufs(2)
            mo = [k.sb("mo", [128, 256], BF16) for _ in range(2)]
            b_mo = k.bufs(2)
            for T in range(NT):
                i = T % 2
                tok0 = T * 128
                for p in range(3):
                    k.dma("sp", nd[i][p][:], ndbuf[p][tok0:tok0 + 128, :], writes=[b_nd[i][p]])
                k.op("dve", lambda e: e.tensor_tensor(out=nd[i][0][:], in0=nd[i][0][:], in1=nd[i][1][:], op=ALU.add),
                     reads=[b_nd[i][0], b_nd[i][1]], writes=[b_nd[i][0]])
                k.op("dve", lambda e: e.tensor_tensor(out=nd[i][0][:], in0=nd[i][0][:], in1=nd[i][2][:], op=ALU.add),
                     reads=[b_nd[i][0], b_nd[i][2]], writes=[b_nd[i][0]])
                k.op("dve", lambda e: e.reciprocal(out=rd[i][:], in_=nd[i][0][:, 64:4 * VW:VW]), reads=[b_nd[i][0]], writes=[b_rd[i]])
                for h in range(4):
                    k.op("dve", lambda e: e.tensor_scalar(out=mo[i][:, h * 64:(h + 1) * 64], in0=nd[i][0][:, h * VW:h * VW + 64],
                                                          scalar1=rd[i][:, h:h + 1], scalar2=None, op0=ALU.mult),
                         reads=[b_nd[i][0], b_rd[i]], writes=[b_mo[i]])
                k.dma("sp", merged[tok0:tok0 + 128, 512:768], mo[i][:], reads=[b_mo[i]])

    def sg_setup(l):
        sm = small[l]
        bsm = b_small[l]
        if True:
            sgg = k.sb("sgg", [128, 256], F32)
            sgb = k.sb("sgb", [128, 256], F32)
            b_sgp = k.buf()
            k.dma("sp", sgg[:], small_in[l, :, O_SGG:O_SGG + 256], writes=[b_sgp])
            k.dma("sp", sgb[:], small_in[l, :, O_SGB:O_SGB + 256], writes=[b_sgp])
            wraw = k.sb("wraw", [128, 4, 128], F32)
            b_wraw = k.buf()
            k.dma("sp", wraw[:], w_sp[l].rearrange("g t s -> t g s"), writes=[b_wraw])
            WT = k.sb("WT", [128, 4, 128], BF16)
            b_WT = k.buf()
            pm = [k.ps("pm", [128, 512]) for _ in range(2)]
            b_pm = k.pbufs(2)
            pw, b_pw = pm[0], b_pm[0]
            for g in range(4):
                k.op("pe", lambda e: e.transpose(out=pw[:, g * 128:(g + 1) * 128], in_=wraw[:, g, :], identity=ident),
                     reads=[b_wraw, b_cst], writes=[b_pw])
            for g in range(4):
                k.op("dve", lambda e: e.tensor_tensor(out=WT[:, g, :], in0=pw[:, g * 128:(g + 1) * 128], in1=Umat, op=ALU.mult),
                     reads=[b_pw, b_cst], writes=[b_WT])
            xt = [k.sb("xt", [128, 512], F32) for _ in range(2)]
            b_xt = k.bufs(2)
            t1 = [k.sb("t1", [128, 512], F32) for _ in range(2)]
            b_t1 = k.bufs(2)
            ge = [k.sb("ge", [128, 512], F32) for _ in range(2)]
            b_ge = k.bufs(2)
            stt = [k.sb("sst", [128, 16], F32) for _ in range(2)]
            b_stt = k.bufs(2)
            vn = [k.sb("vn", [128, 256], F32) for _ in range(2)]
            b_vn = k.bufs(2)
            vnb = [k.sb("vnb", [128, 256], BF16) for _ in range(2)]
            b_vnb = k.bufs(2)
            mo = [k.sb("mo", [128, 256], BF16) for _ in range(2)]
            b_mo = k.bufs(2)
            def sg_tile(T):
                i = T % 2
                tok0 = T * 128
                k.dma("sp", xt[i][:], sgbuf[tok0:tok0 + 128, :], writes=[b_xt[i]])
                yield
                k.op("pool", lambda e: e.tensor_tensor(out=t1[i][:], in0=xt[i][:], in1=xt[i][:], op=ALU.mult),
                     reads=[b_xt[i]], writes=[b_t1[i]])
                yield
                k.op("dve", lambda e: e.tensor_scalar(out=t1[i][:], in0=t1[i][:], scalar1=0.044715, scalar2=1.0, op0=ALU.mult, op1=ALU.add),
                     reads=[b_t1[i]], writes=[b_t1[i]])
                yield
                k.op("pool", lambda e: e.tensor_tensor(out=t1[i][:], in0=t1[i][:], in1=xt[i][:], op=ALU.mult),
                     reads=[b_t1[i], b_xt[i]], writes=[b_t1[i]])
                yield
                k.op("act", lambda e: e.activation(out=t1[i][:], in_=t1[i][:], func=AF.Sigmoid, scale=1.5957691216057308),
                     reads=[b_t1[i]], writes=[b_t1[i]])
                yield
                k.op("dve", lambda e: e.tensor_tensor(out=ge[i][:], in0=t1[i][:], in1=xt[i][:], op=ALU.mult),
                     reads=[b_t1[i], b_xt[i]], writes=[b_ge[i]])
                yield
                k.op("dve", lambda e: e.bn_stats(out=stt[i][:, 0:6], in_=ge[i][:, 256:512]), reads=[b_ge[i]], writes=[b_stt[i]])
                yield
                k.op("dve", lambda e: e.bn_aggr(out=stt[i][:, 8:10], in_=stt[i][:, 0:6]), reads=[b_stt[i]], writes=[b_stt[i]])
                yield
                k.op("act", lambda e: e.activation(out=stt[i][:, 10:11], in_=stt[i][:, 9:10], func=AF.Sqrt, bias=epsb[:, 0:1], scale=1.0),
                     reads=[b_stt[i], b_eps], writes=[b_stt[i]])
                yield
                k.op("dve", lambda e: e.reciprocal(out=stt[i][:, 11:12], in_=stt[i][:, 10:11]), reads=[b_stt[i]], writes=[b_stt[i]])
                yield
                k.op("dve", lambda e: e.tensor_scalar(out=vn[i][:], in0=ge[i][:, 256:512], scalar1=stt[i][:, 8:9], scalar2=stt[i][:, 11:12],
                                                      op0=ALU.subtract, op1=ALU.mult),
                     reads=[b_ge[i], b_stt[i]], writes=[b_vn[i]])
                yield
                k.op("pool", lambda e: e.tensor_tensor(out=vn[i][:], in0=vn[i][:], in1=sgg[:], op=ALU.mult),
                     reads=[b_vn[i], b_sgp], writes=[b_vn[i]])
                yield
                k.op("pool", lambda e: e.tensor_tensor(out=vnb[i][:], in0=vn[i][:], in1=sgb[:], op=ALU.add),
                     reads=[b_vn[i], b_sgp], writes=[b_vnb[i]])
                for g in range(4):
                    yield
                    k.op("pe", lambda e: e.matmul(pm[i][:, g * 64:(g + 1) * 64], lhsT=WT[:, g, :], rhs=vnb[i][:, g * 64:(g + 1) * 64],
                                                  start=True, stop=True),
                         reads=[b_WT, b_vnb[i]], writes=[b_pm[i]])
                for g in range(4):
                    yield
                    k.op("dve", lambda e: e.scalar_tensor_tensor(out=mo[i][:, g * 64:(g + 1) * 64], in0=pm[i][:, g * 64:(g + 1) * 64],
                                                                 scalar=sm[:, O_BSP + g:O_BSP + g + 1], in1=ge[i][:, g * 64:(g + 1) * 64],
                                                                 op0=ALU.add, op1=ALU.mult),
                         reads=[b_pm[i], bsm, b_ge[i]], writes=[b_mo[i]])
                yield
                k.dma("sp", merged[tok0:tok0 + 128, 768:1024], mo[i][:], reads=[b_mo[i]])
                yield
            return sg_tile

    def phase_sg(l):
        with k.scope():
            f = sg_setup(l)
            run_pipelined([((f(T) for T in range(NT)), 2)])

    def wout_load(l):
        wo = k.sb("wo", [128, KC, D], BF16)
        b_wo = k.bufs(KC)
        stage = [k.sb("stg", [128, 1024], F32) for _ in range(2)]
        b_stage = k.bufs(2)
        load_w(wo, b_wo, w_out[l], KC, D, None, None, stage, b_stage)
        return wo, b_wo

    def phase_wout(l, pre=None, fused_merge=False):
        with k.scope():
            if pre is None:
                pre = wout_load(l)
            wo, b_wo = pre
            if fused_merge:
                nd = [[k.sb("nd", [128, 4 * VW], F32) for _ in range(3)] for _ in range(2)]
                b_nd = [k.bufs(3) for _ in range(2)]
                rd = [k.sb("rd", [128, 4], F32) for _ in range(2)]
                b_rd = k.bufs(2)
            mt = [k.sb("mt", [128, D], BF16) for _ in range(2)]
            b_mt = k.bufs(2)
            tp = [k.ps("tp", [128, D], BF16) for _ in range(2)]
            b_tp = k.pbufs(2)
            mT = [k.sb("mT", [128, KC, 128], BF16) for _ in range(2)]
            b_mT = k.bufs(2)
            hb = [k.sb("hb", [128, D], F32) for _ in range(2)]
            b_hb = k.bufs(2)
            po4 = [k.ps("po", [128, 512]) for _ in range(4)]
            b_po4 = k.pbufs(4)
            def wout_tile(T):
                i = T % 2
                ip = T % 2
                tok0 = T * 128
                if fused_merge:
                    k.dma("sp", mt[i][:, 0:512], merged[tok0:tok0 + 128, 0:512], writes=[b_mt[i]])
                    yield
                    k.dma("sp", mt[i][:, 768:1024], merged[tok0:tok0 + 128, 768:1024], writes=[b_mt[i]])
                    for p in range(3):
                        yield
                        k.dma("sp", nd[i][p][:], ndbuf[p][tok0:tok0 + 128, :], writes=[b_nd[i][p]])
                    yield
                    k.op("pool", lambda e: e.tensor_tensor(out=nd[i][0][:], in0=nd[i][0][:], in1=nd[i][1][:], op=ALU.add),
                         reads=[b_nd[i][0], b_nd[i][1]], writes=[b_nd[i][0]])
                    yield
                    k.op("pool", lambda e: e.tensor_tensor(out=nd[i][0][:], in0=nd[i][0][:], in1=nd[i][2][:], op=ALU.add),
                         reads=[b_nd[i][0], b_nd[i][2]], writes=[b_nd[i][0]])
                    yield
                    k.op("dve", lambda e: e.reciprocal(out=rd[i][:], in_=nd[i][0][:, 64:4 * VW:VW]), reads=[b_nd[i][0]], writes=[b_rd[i]])
                    for h in range(4):
                        yield
                        k.op("dve", lambda e: e.tensor_scalar(out=mt[i][:, 512 + h * 64:512 + (h + 1) * 64], in0=nd[i][0][:, h * VW:h * VW + 64],
                                                              scalar1=rd[i][:, h:h + 1], scalar2=None, op0=ALU.mult),
                             reads=[b_nd[i][0], b_rd[i]], writes=[b_mt[i]])
                else:
                    yield
                    k.dma("sp", mt[i][:], merged[tok0:tok0 + 128, :], writes=[b_mt[i]])
                yield
                k.dma("sp", hb[i][:], hbuf[tok0:tok0 + 128, :], writes=[b_hb[i]])
                yield
                for kc in range(KC):
                    k.op("pe", lambda e: e.transpose(out=tp[ip][:, kc * 128:(kc + 1) * 128], in_=mt[i][:, kc * 128:(kc + 1) * 128], identity=identb[:]),
                         reads=[b_mt[i], b_identb], writes=[b_tp[ip]])
                yield
                k.op("act", lambda e: e.activation(out=mT[i][:], in_=tp[ip][:].rearrange("p (a b) -> p a b", a=KC), func=AF.Copy),
                     reads=[b_tp[ip]], writes=[b_mT[i]])
                for nh in range(2):
                    yield
                    for kc in range(KC):
                        k.op("pe", lambda e: e.matmul(po4[2 * ip + nh][:], lhsT=mT[i][:, kc, :], rhs=wo[:, kc, nh * 512:(nh + 1) * 512],
                                                      start=(kc == 0), stop=(kc == KC - 1)),
                             reads=[b_mT[i], b_wo[kc]], writes=[b_po4[2 * ip + nh]])
                    yield
                    k.op("dve", lambda e: e.tensor_tensor(out=hb[i][:, nh * 512:(nh + 1) * 512], in0=po4[2 * ip + nh][:],
                                                          in1=hb[i][:, nh * 512:(nh + 1) * 512], op=ALU.add),
                         reads=[b_po4[2 * ip + nh], b_hb[i]], writes=[b_hb[i]])
                yield
                k.dma("sp", hbuf[tok0:tok0 + 128, :], hb[i][:], reads=[b_hb[i]])
                yield

            run_pipelined([((wout_tile(T) for T in range(NT)), 2)])

    def phase_mix2(l):
        with k.scope():
            pre = wout_load(l)
            sg_tile = sg_setup(l)
            phase_swa(l, extra_streams=[((sg_tile(T) for T in range(NT)), 2)], do_merge=False)
            k.barrier()
            phase_wout(l, pre=pre, fused_merge=True)

    def phase_xa(l):
        sm = small[l]
        bsm = b_small[l]
        with k.scope():
            KmT = k.sb("KmT", [128, 8, 256], BF16)
            Vm = k.sb("Vm", [128, 2, D], BF16)
            b_KmT = k.buf()
            b_Vm = k.buf()
            wq = k.sb("wq", [128, KC, D], BF16)
            wo = k.sb("wo", [128, KC, D], BF16)
            b_wq = k.bufs(KC)
            b_wo = k.bufs(KC)
            stage = [k.sb("stg", [128, 1024], F32) for _ in range(2)]
            b_stage = k.bufs(2)
            nrm = NormT()
            with k.scope():
                wkv = k.sb("wkv", [128, KC, 2 * D], BF16)
                b_wkv = k.bufs(KC)
                load_w(wkv, b_wkv, w_kv[l], KC, 2 * D, sm[:, O_GMEM:O_GMEM + 8], bsm, stage, b_stage)
                memT = k.sb("memT", [128, KC, 256], BF16)
                b_memT = k.bufs(2)
                mb = [k.sb("mb", [128, D], F32) for _ in range(2)]
                b_mb = k.bufs(2)
                pk = [k.ps("pk", [128, 512]) for _ in range(2)]
                b_pk = k.pbufs(2)
                for mt_ in range(2):
                    k.dma("sp", mb[mt_][:], mem_in[mt_ * 128:(mt_ + 1) * 128, :], writes=[b_mb[mt_]])
                    nrm.run(mb[mt_][:], b_mb[mt_], memT[:, :, mt_ * 128:(mt_ + 1) * 128], b_memT[mt_])
                for c in range(8):
                    j = c % 2
                    for kc in range(KC):
                        k.op("pe", lambda e: e.matmul(pk[j][:, 0:256], lhsT=wkv[:, kc, c * 128:(c + 1) * 128], rhs=memT[:, kc, :],
                                                      start=(kc == 0), stop=(kc == KC - 1)),
                             reads=[b_wkv[kc]] + b_memT, writes=[b_pk[j]])
                    k.op("act", lambda e: e.activation(out=KmT[:, c, :], in_=pk[j][:, 0:256], func=AF.Copy), reads=[b_pk[j]], writes=[b_KmT])
                for mt_ in range(2):
                    for nh in range(2):
                        j = (mt_ * 2 + nh) % 2
                        for kc in range(KC):
                            k.op("pe", lambda e: e.matmul(pk[j][:], lhsT=memT[:, kc, mt_ * 128:(mt_ + 1) * 128],
                                                          rhs=wkv[:, kc, D + nh * 512:D + (nh + 1) * 512], start=(kc == 0), stop=(kc == KC - 1)),
                                 reads=[b_wkv[kc], b_memT[mt_]], writes=[b_pk[j]])
                        k.op("dve", lambda e: e.tensor_copy(out=Vm[:, mt_, nh * 512:(nh + 1) * 512], in_=pk[j][:]), reads=[b_pk[j]], writes=[b_Vm])
            load_w(wq, b_wq, w_q[l], KC, D, sm[:, O_GX:O_GX + 8], bsm, stage, b_stage)
            load_w(wo, b_wo, w_o[l], KC, D, None, None, stage, b_stage)
            hb = [k.sb("hb", [128, D], F32) for _ in range(8)]
            b_hb = k.bufs(8)
            nT = [k.sb("nT", [128, KC, 512], BF16) for _ in range(2)]
            b_nT = [k.bufs(4) for _ in range(2)]
            QT = [k.sb("QT", [128, 8, 512], BF16) for _ in range(2)]
            b_QT = [k.bufs(8) for _ in range(2)]
            PT = [[[k.sb("PT", [128, 512], BF16) for _ in range(2)] for _ in range(2)] for _ in range(2)]
            b_PT = [[k.bufs(2) for _ in range(2)] for _ in range(2)]
            rden = [[k.sb("rden", [128, 512], F32) for _ in range(2)] for _ in range(2)]
            b_rden = [k.bufs(2) for _ in range(2)]
            OT = [k.sb("OT", [128, 8, 512], BF16) for _ in range(2)]
            b_OT = [k.bufs(8) for _ in range(2)]
            pq = [k.ps("pq", [128, 512]) for _ in range(2)]
            b_pq = k.pbufs(2)
            pd = k.ps("pd", [128, 512])
            b_pd = k.pbuf()
            pov = [k.ps("pov", [128, 512]) for _ in range(2)]
            b_pov = k.pbufs(2)
            po = k.ps("po", [128, 512])
            b_po = k.pbuf()
            cq = {"n": 0}

            def xa_group(g):
                gi = g % 2
                for t in range(4):
                    tok0 = g * 512 + t * 128
                    hbt, bhb = hb[gi * 4 + t], b_hb[gi * 4 + t]
                    k.dma("sp", hbt[:], hbuf[tok0:tok0 + 128, :], writes=[bhb])
                    nrm.run(hbt[:], bhb, nT[gi][:, :, t * 128:(t + 1) * 128], b_nT[gi][t])
                    yield
                for c in range(8):
                    j = cq["n"] % 2
                    cq["n"] += 1
                    for kc in range(KC):
                        k.op("pe", lambda e: e.matmul(pq[j][:], lhsT=wq[:, kc, c * 128:(c + 1) * 128], rhs=nT[gi][:, kc, :],
                                                      start=(kc == 0), stop=(kc == KC - 1)),
                             reads=[b_wq[kc]] + b_nT[gi], writes=[b_pq[j]])
                    k.op("act", lambda e: e.activation(out=QT[gi][:, c, :], in_=pq[j][:], func=AF.Copy, scale=0.0625),
                         reads=[b_pq[j]], writes=[b_QT[gi][c]])
                    yield
                for h in range(4):
                    hi = h % 2
                    for mt_ in range(2):
                        j = cq["n"] % 2
                        cq["n"] += 1
                        for half in range(2):
                            k.op("pe", lambda e: e.matmul(pq[j][:], lhsT=KmT[:, 2 * h + half, mt_ * 128:(mt_ + 1) * 128], rhs=QT[gi][:, 2 * h + half, :],
                                                          start=(half == 0), stop=(half == 1)),
                                 reads=[b_KmT, b_QT[gi][2 * h + half]], writes=[b_pq[j]])
                        k.op("act", lambda e: e.activation(out=PT[gi][hi][mt_][:], in_=pq[j][:], func=AF.Exp),
                             reads=[b_pq[j]], writes=[b_PT[gi][hi][mt_]])
                        yield
                    for mt_ in range(2):
                        k.op("pe", lambda e: e.matmul(pd[:], lhsT=onesb[:], rhs=PT[gi][hi][mt_][:], start=(mt_ == 0), stop=(mt_ == 1)),
                             reads=[b_identb, b_PT[gi][hi][mt_]], writes=[b_pd])
                    k.op("dve", lambda e: e.reciprocal(out=rden[gi][hi][:], in_=pd[:]), reads=[b_pd], writes=[b_rden[gi][hi]])
                    yield
                    for ec in range(2):
                        for mt_ in range(2):
                            k.op("pe", lambda e: e.matmul(pov[ec][:], lhsT=Vm[:, mt_, h * 256 + ec * 128:h * 256 + (ec + 1) * 128], rhs=PT[gi][hi][mt_][:],
                                                          start=(mt_ == 0), stop=(mt_ == 1)),
                                 reads=[b_Vm, b_PT[gi][hi][mt_]], writes=[b_pov[ec]])
                        k.op("dve", lambda e: e.tensor_tensor(out=OT[gi][:, 2 * h + ec, :], in0=pov[ec][:], in1=rden[gi][hi][:], op=ALU.mult),
                             reads=[b_pov[ec], b_rden[gi][hi]], writes=[b_OT[gi][2 * h + ec]])
                        yield
                for t in range(4):
                    tok0 = g * 512 + t * 128
                    hbt, bhb = hb[gi * 4 + t], b_hb[gi * 4 + t]
                    for nh in range(2):
                        for c in range(8):
                            k.op("pe", lambda e: e.matmul(po[:], lhsT=OT[gi][:, c, t * 128:(t + 1) * 128], rhs=wo[:, c, nh * 512:(nh + 1) * 512],
                                                          start=(c == 0), stop=(c == 7)),
                                 reads=[b_OT[gi][c], b_wo[c]], writes=[b_po])
                        k.op("dve", lambda e: e.tensor_tensor(out=hbt[:, nh * 512:(nh + 1) * 512], in0=po[:],
                                                              in1=hbt[:, nh * 512:(nh + 1) * 512], op=ALU.add),
                             reads=[b_po, bhb], writes=[bhb])
                        yield
                    k.dma("sp", hbuf[tok0:tok0 + 128, :], hbt[:], reads=[bhb])
                    yield

            run_pipelined([((xa_group(g) for g in range(NG)), 2)])

    def phase_final():
        with k.scope():
            fb = k.sb("fb", [128, D], F32)
            b_fb = k.buf()
            k.dma("sp", fb[:], fin_in, writes=[b_fb])
            hb = [k.sb("hb", [128, D], F32) for _ in range(4)]
            b_hb = k.bufs(4)
            junk = k.sb("junk", [128, D], F32)
            b_junk = k.buf()
            st = [k.sb("st", [128, 4], F32) for _ in range(4)]
            b_st = k.bufs(4)
            for i in range(NT):
                j = i % 4
                k.dma("sp", hb[j][:], hbuf[i * 128:(i + 1) * 128, :], writes=[b_hb[j]])
                k.op("act", lambda e, j=j: e.activation(out=junk[:], in_=hb[j][:], func=AF.Square, accum_out=st[j][:, 0:1]),
                     reads=[b_hb[j]], writes=[b_junk, b_st[j]])
                k.op("act", lambda e, j=j: e.activation(out=st[j][:, 1:2], in_=st[j][:, 0:1], func=AF.Sqrt, bias=epsb[:, 0:1], scale=1.0 / D),
                     reads=[b_st[j], b_eps], writes=[b_st[j]])
                k.op("dve", lambda e, j=j: e.reciprocal(out=st[j][:, 2:3], in_=st[j][:, 1:2]), reads=[b_st[j]], writes=[b_st[j]])
                k.op("dve", lambda e, j=j: e.scalar_tensor_tensor(out=hb[j][:], in0=hb[j][:], scalar=st[j][:, 2:3], in1=fb[:],
                                                                  op0=ALU.mult, op1=ALU.mult),
                     reads=[b_hb[j], b_st[j], b_fb], writes=[b_hb[j]])
                k.dma("sp", out[i * 128:(i + 1) * 128, :], hb[j][:], reads=[b_hb[j]])

    phase_copy_x()
    done = False
    for l in range(n_layers):
        for ph in PHASES:
            if ph == "ffn1":
                phase_ffn(l, 0)
            elif ph == "ffn2":
                phase_ffn(l, 1)
            elif ph == "mixproj":
                phase_mixproj(l)
            elif ph == "dn":
                phase_dn(l)
            elif ph == "mix2":
                phase_mix2(l)
            elif ph == "swa":
                phase_swa(l)
            elif ph == "sg":
                phase_sg(l)
            elif ph == "wout":
                phase_wout(l)
            elif ph == "xa":
                phase_xa(l)
            if stop_after == (l, ph):
                done = True
                break
        if done:
            break
    phase_final()
    k.finish()
    return nc


DBG = {}
PHASES = ["ffn1", "mixproj", "dn", "mix2", "xa", "ffn2"]


SWA_PATTERNS = ((128, 1), (512, 4), (2048, 16))


def make_consts():
    cst = np.zeros((128, NCST), np.float32)
    cst[:, C_ID:C_ID + 128] = np.eye(128, dtype=np.float32)
    cst[:, C_ONE:C_ONE + 128] = 1.0
    s = np.arange(128)[:, None]
    i = np.arange(128)[None, :]
    cst[:, C_U:C_U + 128] = (s <= i)
    cst[:, C_MB:C_MB + 128] = np.where(i >= s, 0.0, NEG)
    cst[:, C_ST:C_ST + 128] = (i > s)
    slopes = [2.0 ** (-8.0 * (h + 1) / 4) for h in range(4)]
    kk = np.arange(128)[:, None]
    qq = np.arange(128)[None, :]
    for p, (win, dil) in enumerate(SWA_PATTERNS):
        for blk in range(2):
            rel = (128 if blk == 0 else 0) + qq - kk
            valid = (rel >= 0) & (rel <= win // dil)
            for h in range(4):
                o = C_SWA + ((p * 2 + blk) * 4 + h) * 128
                cst[:, o:o + 128] = np.where(valid, -slopes[h] * rel * dil, NEG)
    return cst


def pack_small(inp):
    sm = np.zeros((DEPTH, 128, NS), np.float32)

    def pk(v):
        return np.asarray(v, np.float32).reshape(8, 128).T

    for l in range(DEPTH):
        sm[l, :, O_G1:O_G1 + 8] = pk(inp["ffn1_norm"][l])
        sm[l, :, O_GM:O_GM + 8] = pk(inp["mix_norm"][l])
        sm[l, :, O_GX:O_GX + 8] = pk(inp["xa_norm"][l])
        sm[l, :, O_G2:O_G2 + 8] = pk(inp["ffn2_norm"][l])
        sm[l, :, O_GMEM:O_GMEM + 8] = pk(inp["xa_mem_norm"][l])
        cw = np.asarray(inp["dn_conv_w"][l], np.float32)
        sm[l, :, O_CONV:O_CONV + 48] = cw.reshape(4, 12, 128).transpose(2, 1, 0).reshape(128, 48)
        sm[l, :, O_ALOG:O_ALOG + 4] = np.asarray(inp["dn_a_log"][l], np.float32)[None, :]
        sm[l, :, O_DTB:O_DTB + 4] = np.asarray(inp["dn_dt_bias"][l], np.float32)[None, :]
        sm[l, :, O_DNG:O_DNG + 512] = np.tile(np.asarray(inp["dn_out_norm"][l], np.float32), 4)[None, :]
        sm[l, :, O_SGG:O_SGG + 256] = np.asarray(inp["sg_norm_gain"][l], np.float32)[None, :]
        sm[l, :, O_SGB:O_SGB + 256] = np.asarray(inp["sg_norm_bias"][l], np.float32)[None, :]
        sm[l, :, O_BSP:O_BSP + 4] = np.asarray(inp["sg_b_spatial"][l], np.float32).T
    return sm


_W_NAMES = ["ffn1_w_gate_up", "ffn2_w_gate_up", "ffn1_w_down", "ffn2_w_down", "mix_w_in", "mix_w_out",
            "xa_w_q", "xa_w_kv", "xa_w_o", "sg_w_spatial"]


def make_in_maps(inp, n_cores=N_CORES):
    shared = {n: np.ascontiguousarray(np.asarray(inp[n], np.float32)) for n in _W_NAMES}
    shared["small"] = pack_small(inp)
    shared["final_b"] = np.ascontiguousarray(np.broadcast_to(np.asarray(inp["final_norm"], np.float32)[None, :], (128, D)))
    shared["cst"] = make_consts()
    maps = []
    for c in range(n_cores):
        m = dict(shared)
        m["x"] = np.ascontiguousarray(np.asarray(inp["x"][c], np.float32))
        m["mem"] = np.ascontiguousarray(np.asarray(inp["mem"][c], np.float32))
        maps.append(m)
    return maps


_NC_CACHE = {}


def kernel(**inputs):
    if "nc" not in _NC_CACHE:
        _NC_CACHE["nc"] = build()
    nc = _NC_CACHE["nc"]
    maps = make_in_maps(inputs)
    res = run_bass_kernel_spmd(nc, maps, core_ids=list(range(N_CORES)))
    return np.stack([np.asarray(r["out"], np.float32) for r in res.results], axis=0)
```
